# Optimizing a Trainium2 kernel written in Bass

```python
import math
import jax, jax.numpy as jnp
from jax import lax
import numpy as np

D_MODEL = 2048
BATCH = 4
SEQ = 4096
DEPTH = 4

N_MIXERS = 4
MIX_WIDTH = D_MODEL
GROUP_WIDTH = MIX_WIDTH // N_MIXERS
HEAD_DIM = 128
N_GROUP_HEADS = GROUP_WIDTH // HEAD_DIM
Q_BLOCK = 128
NORM_EPS = 1e-6
GDN_DK = HEAD_DIM
GDN_DV = HEAD_DIM
GDN_CONV = 4
GDN_CHUNK = 64
DT_MIN = 1e-3
DT_MAX = 1e-1
DIFF_DQK = HEAD_DIM // 2
GLA_DK = HEAD_DIM // 2
GLA_DV = HEAD_DIM
GLA_GATE_RANK = 16
GLA_TAU = 16.0
GLA_CHUNK = 64
FFN_HIDDEN = ((8 * D_MODEL + 3 * 256 - 1) // (3 * 256)) * 256
IN_SPLITS = (
    GROUP_WIDTH, GROUP_WIDTH, GROUP_WIDTH,
    GROUP_WIDTH, GROUP_WIDTH, GROUP_WIDTH, GROUP_WIDTH, N_GROUP_HEADS, N_GROUP_HEADS,
    GROUP_WIDTH, GROUP_WIDTH, GROUP_WIDTH,
    N_GROUP_HEADS * GLA_DK, N_GROUP_HEADS * GLA_DK, GROUP_WIDTH, GROUP_WIDTH, GLA_GATE_RANK,
)
IN_COLS = sum(IN_SPLITS)

kernel_name = "hymba_style_sb_gdn_diff_gla_trunk"


def rms_norm(x, gain):
    xf = x.astype(jnp.float32)
    y = xf * lax.rsqrt(jnp.mean(jnp.square(xf), axis=-1, keepdims=True) + NORM_EPS)
    return (y * gain.astype(jnp.float32)).astype(x.dtype)


def l2norm(x):
    return x * lax.rsqrt(jnp.sum(jnp.square(x), axis=-1, keepdims=True) + NORM_EPS)


def causal_depthwise_conv(x, w):
    k = w.shape[0]
    return lax.conv_general_dilated(
        x, w[:, None, :], window_strides=(1,), padding=[(k - 1, 0)],
        dimension_numbers=("NWC", "WIO", "NWC"), feature_group_count=x.shape[-1])


def stick_breaking_attention(q, k, v):
    B, S, H, D = q.shape
    nb = S // Q_BLOCK
    qb = (q * D ** -0.5).reshape(B, nb, Q_BLOCK, H, D).transpose(1, 0, 3, 2, 4)
    kt = k.transpose(0, 2, 1, 3)
    vt = v.transpose(0, 2, 1, 3)
    kpos = jnp.arange(S)

    def block(args):
        qi, i = args
        qpos = i * Q_BLOCK + jnp.arange(Q_BLOCK)
        mask = kpos[None, :] < qpos[:, None]
        z = jnp.einsum('bhqd,bhkd->bhqk', qi, kt)
        log_fail = jnp.where(mask, -jax.nn.softplus(z), 0.0)
        tail = lax.cumsum(log_fail, axis=3, reverse=True) - log_fail
        att = jnp.where(mask, jnp.exp(jax.nn.log_sigmoid(z) + tail), 0.0)
        return jnp.einsum('bhqk,bhkd->bhqd', att, vt)

    o = lax.map(block, (qb, jnp.arange(nb)))
    return o.transpose(1, 0, 3, 2, 4).reshape(B, S, H, D)


def gated_delta_rule(q, k, v, g, beta):
    B, S, H, Dk = q.shape
    Dv = v.shape[-1]
    C = GDN_CHUNK
    N = S // C

    def chunks(t):
        return t.reshape(B, N, C, H, -1).transpose(1, 0, 3, 2, 4)

    qc, kc, vc = chunks(q), chunks(k), chunks(v)
    gc = jnp.cumsum(g.reshape(B, N, C, H).transpose(1, 0, 3, 2), axis=-1)
    bc = beta.reshape(B, N, C, H).transpose(1, 0, 3, 2)[..., None]
    idx = jnp.arange(C)
    incl = idx[:, None] >= idx[None, :]
    strict = idx[:, None] > idx[None, :]
    decay = jnp.exp(jnp.where(incl, gc[..., :, None] - gc[..., None, :], -jnp.inf))
    kb = kc * bc
    a_mat = jnp.where(strict, jnp.einsum('nbhcd,nbhsd->nbhcs', kb, kc) * decay, 0.0)
    m_mat = a_mat + jnp.eye(C, dtype=a_mat.dtype)
    u = lax.linalg.triangular_solve(m_mat, vc * bc, left_side=True, lower=True, unit_diagonal=True)
    w = lax.linalg.triangular_solve(m_mat, kb * jnp.exp(gc)[..., None], left_side=True, lower=True,
                                    unit_diagonal=True)
    qk = jnp.einsum('nbhcd,nbhsd->nbhcs', qc, kc) * decay

    def step(state, xs):
        q_n, k_n, u_n, w_n, qk_n, g_n = xs
        v_new = u_n - jnp.einsum('bhcd,bhde->bhce', w_n, state)
        o = (jnp.einsum('bhcd,bhde->bhce', q_n * jnp.exp(g_n)[..., None], state)
             + jnp.einsum('bhcs,bhse->bhce', qk_n, v_new))
        g_last = g_n[..., -1:]
        state = (state * jnp.exp(g_last)[..., None]
                 + jnp.einsum('bhcd,bhce->bhde', k_n * jnp.exp(g_last - g_n)[..., None], v_new))
        return state, o

    state0 = jnp.zeros((B, H, Dk, Dv), jnp.float32)
    _, o = lax.scan(step, state0, (qc, kc, u, w, qk, gc))
    return o.transpose(1, 0, 3, 2, 4).reshape(B, S, H, Dv)


def differential_attention(q, k, v, lam):
    B, S, H, _, D = q.shape
    nb = S // Q_BLOCK
    qb = (q * D ** -0.5).reshape(B, nb, Q_BLOCK, H, 2, D).transpose(1, 0, 3, 4, 2, 5)
    kt = k.transpose(0, 2, 3, 1, 4)
    vt = v.transpose(0, 2, 1, 3)
    kpos = jnp.arange(S)

    def block(args):
        qi, i = args
        qpos = i * Q_BLOCK + jnp.arange(Q_BLOCK)
        s = jnp.einsum('bhmqd,bhmkd->bhmqk', qi, kt)
        s = jnp.where(kpos[None, :] <= qpos[:, None], s, -jnp.inf)
        p = jax.nn.softmax(s, axis=-1)
        return jnp.einsum('bhqk,bhkd->bhqd', p[:, :, 0] - lam * p[:, :, 1], vt)

    o = lax.map(block, (qb, jnp.arange(nb)))
    return o.transpose(1, 0, 3, 2, 4).reshape(B, S, H, -1)


def gla_chunked(q, k, v, log_a):
    B, S, H, Dk = q.shape
    Dv = v.shape[-1]
    C = GLA_CHUNK
    N = S // C

    def chunks(t):
        return t.reshape(B, N, C, H, -1).transpose(1, 0, 3, 2, 4)

    qc, kc, vc = chunks(q), chunks(k), chunks(v)
    bc = jnp.cumsum(chunks(log_a), axis=3)
    idx = jnp.arange(C)
    incl = (idx[:, None] >= idx[None, :])[:, :, None]

    def step(state, xs):
        q_n, k_n, v_n, b_n = xs
        inter = jnp.einsum('bhcd,bhde->bhce', q_n * jnp.exp(b_n), state)
        rel = jnp.exp(jnp.where(incl, b_n[:, :, :, None, :] - b_n[:, :, None, :, :], -jnp.inf))
        att = jnp.einsum('bhtd,bhsd,bhtsd->bhts', q_n, k_n, rel)
        o = inter + jnp.einsum('bhts,bhse->bhte', att, v_n)
        b_last = b_n[:, :, -1:, :]
        state = (state * jnp.exp(b_last)[:, :, 0, :, None]
                 + jnp.einsum('bhcd,bhce->bhde', k_n * jnp.exp(b_last - b_n), v_n))
        return state, o

    state0 = jnp.zeros((B, H, Dk, Dv), jnp.float32)
    _, o = lax.scan(step, state0, (qc, kc, vc, bc))
    return o.transpose(1, 0, 3, 2, 4).reshape(B, S, H, Dv)


def hybrid_mixer(h, w_in, gdn_conv_w, gdn_a_log, gdn_dt_bias, gdn_out_norm, lam_q1, lam_k1, lam_q2, lam_k2,
                 diff_out_norm, gla_gate_w2, gla_gate_b, gla_out_norm, lam_init):
    f32 = jnp.float32
    B, S, _ = h.shape
    H = N_GROUP_HEADS
    proj = jnp.einsum('bsd,dc->bsc', h, w_in).astype(f32)
    points = np.cumsum(IN_SPLITS)[:-1].tolist()
    (sb_q, sb_k, sb_v, gd_q, gd_k, gd_v, gd_z, gd_b, gd_a,
     df_q, df_k, df_v, gl_q, gl_k, gl_v, gl_g, gl_r) = jnp.split(proj, points, axis=-1)

    def heads(t):
        return t.reshape(B, S, H, -1)

    o_sb = stick_breaking_attention(heads(sb_q), heads(sb_k), heads(sb_v))

    qkv = jax.nn.silu(causal_depthwise_conv(jnp.concatenate([gd_q, gd_k, gd_v], axis=-1), gdn_conv_w.astype(f32)))
    gd_q, gd_k, gd_v = jnp.split(qkv, 3, axis=-1)
    beta = jax.nn.sigmoid(gd_b)
    g = -jnp.exp(gdn_a_log.astype(f32)) * jax.nn.softplus(gd_a + gdn_dt_bias.astype(f32))
    o_gd = gated_delta_rule(l2norm(heads(gd_q)) * GDN_DK ** -0.5, l2norm(heads(gd_k)), heads(gd_v), g, beta)
    o_gd = rms_norm(o_gd, gdn_out_norm) * jax.nn.silu(heads(gd_z))

    lam = (jnp.exp(jnp.dot(lam_q1.astype(f32), lam_k1.astype(f32)))
           - jnp.exp(jnp.dot(lam_q2.astype(f32), lam_k2.astype(f32))) + lam_init)
    o_df = differential_attention(df_q.reshape(B, S, H, 2, DIFF_DQK), df_k.reshape(B, S, H, 2, DIFF_DQK),
                                  heads(df_v), lam)
    o_df = rms_norm(o_df, diff_out_norm) * (1.0 - lam_init)

    log_a = jax.nn.log_sigmoid(jnp.einsum('bsr,rk->bsk', gl_r, gla_gate_w2.astype(f32))
                               + gla_gate_b.astype(f32)) / GLA_TAU
    o_gl = gla_chunked(heads(gl_q) * GLA_DK ** -0.5, heads(gl_k), heads(gl_v), heads(log_a))
    o_gl = rms_norm(o_gl, gla_out_norm) * jax.nn.silu(heads(gl_g))

    mix = jnp.concatenate([o_sb, o_gd, o_df, o_gl], axis=2).reshape(B, S, MIX_WIDTH)
    return mix.astype(h.dtype)


def swiglu(h, w_gate, w_up, w_down):
    a = jnp.einsum('bsd,df->bsf', h, w_gate)
    b = jnp.einsum('bsd,df->bsf', h, w_up)
    return jnp.einsum('bsf,fd->bsd', jax.nn.silu(a) * b, w_down)


def setup_inputs(seed: int = 0) -> dict:
    key = jax.random.key(seed)
    ks = jax.random.split(key, 21)
    f32 = jnp.float32
    H = N_GROUP_HEADS

    def normal(k, shape, scale):
        return jax.random.normal(k, shape, f32) * scale

    def gain(k, shape):
        return 1.0 + 0.02 * jax.random.normal(k, shape, f32)

    dt = jnp.exp(jax.random.uniform(ks[5], (DEPTH, H), f32, math.log(DT_MIN), math.log(DT_MAX)))
    return {
        "x": normal(ks[0], (BATCH, SEQ, D_MODEL), 1.0),
        "attn_norm": gain(ks[1], (DEPTH, D_MODEL)),
        "w_in": normal(ks[2], (DEPTH, D_MODEL, IN_COLS), D_MODEL ** -0.5),
        "gdn_conv_w": normal(ks[3], (DEPTH, GDN_CONV, 3 * GROUP_WIDTH), GDN_CONV ** -0.5),
        "gdn_a_log": jnp.log(jax.random.uniform(ks[4], (DEPTH, H), f32, 1.0, 16.0)),
        "gdn_dt_bias": dt + jnp.log(-jnp.expm1(-dt)),
        "gdn_out_norm": gain(ks[6], (DEPTH, GDN_DV)),
        "diff_lam_q1": normal(ks[7], (DEPTH, DIFF_DQK), 0.1),
        "diff_lam_k1": normal(ks[8], (DEPTH, DIFF_DQK), 0.1),
        "diff_lam_q2": normal(ks[9], (DEPTH, DIFF_DQK), 0.1),
        "diff_lam_k2": normal(ks[10], (DEPTH, DIFF_DQK), 0.1),
        "diff_out_norm": gain(ks[11], (DEPTH, HEAD_DIM)),
        "gla_gate_w2": normal(ks[12], (DEPTH, GLA_GATE_RANK, H * GLA_DK), GLA_GATE_RANK ** -0.5),
        "gla_gate_b": normal(ks[13], (DEPTH, H * GLA_DK), 0.1),
        "gla_out_norm": gain(ks[14], (DEPTH, GLA_DV)),
        "w_out": normal(ks[15], (DEPTH, MIX_WIDTH, D_MODEL), MIX_WIDTH ** -0.5),
        "ffn_norm": gain(ks[16], (DEPTH, D_MODEL)),
        "w_gate": normal(ks[17], (DEPTH, D_MODEL, FFN_HIDDEN), D_MODEL ** -0.5),
        "w_up": normal(ks[18], (DEPTH, D_MODEL, FFN_HIDDEN), D_MODEL ** -0.5),
        "w_down": normal(ks[19], (DEPTH, FFN_HIDDEN, D_MODEL), FFN_HIDDEN ** -0.5),
        "final_norm": gain(ks[20], (D_MODEL,)),
    }


def reference(x, attn_norm, w_in, gdn_conv_w, gdn_a_log, gdn_dt_bias, gdn_out_norm, diff_lam_q1, diff_lam_k1,
              diff_lam_q2, diff_lam_k2, diff_out_norm, gla_gate_w2, gla_gate_b, gla_out_norm, w_out, ffn_norm,
              w_gate, w_up, w_down, final_norm):
    for l in range(DEPTH):
        lam_init = 0.8 - 0.6 * math.exp(-0.3 * l)
        h = rms_norm(x, attn_norm[l])
        mix = hybrid_mixer(h, w_in[l], gdn_conv_w[l], gdn_a_log[l], gdn_dt_bias[l], gdn_out_norm[l],
                           diff_lam_q1[l], diff_lam_k1[l], diff_lam_q2[l], diff_lam_k2[l], diff_out_norm[l],
                           gla_gate_w2[l], gla_gate_b[l], gla_out_norm[l], lam_init)
        x = x + jnp.einsum('bsc,cd->bsd', mix, w_out[l])
        x = x + swiglu(rms_norm(x, ffn_norm[l]), w_gate[l], w_up[l], w_down[l])
    return rms_norm(x, final_norm)
```

```python
import math
import numpy as np
import concourse.bass as bass
import concourse.mybir as mybir
from concourse.bass_utils import run_bass_kernel_spmd

F32 = mybir.dt.float32
BF16 = mybir.dt.bfloat16
AF = mybir.ActivationFunctionType
ALU = mybir.AluOpType
AX = mybir.AxisListType

COMPUTE = ('pe', 'act', 'dve', 'pool')
ENGS = ('pe', 'act', 'dve', 'pool', 'sp')
N_DMA_SEMS = 48


class Res:
    __slots__ = ('w', 'r', 'excl')

    def __init__(self):
        self.w = None
        self.r = {}
        self.excl = False


class Op:
    __slots__ = ('eng', 'emit', 'deps', 'dma', 'sem', 'val', 'signal', 'prev_same_sem')

    def __init__(self, eng, emit, dma):
        self.eng = eng
        self.emit = emit
        self.dma = dma
        self.deps = []
        self.sem = None
        self.val = 0
        self.signal = dma
        self.prev_same_sem = None


class Buf:
    __slots__ = ('t', 'r')

    def __init__(self, t):
        self.t = t
        self.r = Res()


class Ring:
    def __init__(self, bufs):
        self.bufs = bufs
        self.i = 0

    def next(self):
        b = self.bufs[self.i % len(self.bufs)]
        self.i += 1
        return b


class Prog:
    def __init__(self, nc):
        self.nc = nc
        self.ops = {e: [] for e in ENGS}
        self.n_dma = 0
        self.dma_last = [None] * N_DMA_SEMS
        self.out_dmas = []
        self.pending_dmas = []

    def add(self, eng, emit, reads=(), writes=(), dma=False, is_out=False):
        op = Op(eng, emit, dma)
        deps = {}
        for r in reads:
            if r.w is not None:
                deps[id(r.w)] = r.w
            if r.excl:
                for k, o in r.r.items():
                    if k != eng:
                        deps[id(o)] = o
        for w in writes:
            if w.w is not None:
                deps[id(w.w)] = w.w
            for o in w.r.values():
                deps[id(o)] = o
        for d in deps.values():
            if eng == 'pe' and d.eng == 'pe' and not d.dma and not dma:
                continue
            d.signal = True
            op.deps.append(d)
        for r in reads:
            if dma:
                r.r[('dma', self.n_dma)] = op
            else:
                r.r[eng] = op
        for w in writes:
            w.w = op
            w.r = {}
        if dma:
            slot = self.n_dma % N_DMA_SEMS
            op.prev_same_sem = self.dma_last[slot]
            self.dma_last[slot] = op
            op.sem = slot
            self.n_dma += 1
            self.pending_dmas.append(op)
            if is_out:
                self.out_dmas.append(op)
        self.ops[eng].append(op)
        return op

    def dma(self, eng, out, in_, reads=(), writes=(), is_out=False, **kw):
        return self.add(eng, lambda e: e.dma_start(out=out, in_=in_, **kw), reads, writes,
                        dma=True, is_out=is_out)

    def barrier(self):
        lasts = []
        for e in COMPUTE:
            for op in reversed(self.ops[e]):
                if op.emit is not None and not op.dma:
                    op.signal = True
                    lasts.append(op)
                    break
        pend = list(self.pending_dmas)
        self.pending_dmas = []
        for e in ENGS:
            b = Op(e, None, False)
            b.deps = [d for d in lasts if d.eng != e] + pend
            self.ops[e].append(b)

    def emit_all(self):
        nc = self.nc
        fin = Op('sp', None, False)
        fin.deps = list(self.out_dmas)
        self.ops['sp'].append(fin)
        esem = {e: nc.alloc_semaphore('es_' + e) for e in COMPUTE}
        dsem = [nc.alloc_semaphore('ds_%d' % i) for i in range(N_DMA_SEMS)]
        for e in ENGS:
            cnt = 0
            for op in self.ops[e]:
                if op.dma or op.emit is None:
                    continue
                if op.signal:
                    cnt += 1
                    op.val = cnt
                    op.sem = esem[e]
        for slot in range(N_DMA_SEMS):
            chain = []
            o = self.dma_last[slot]
            while o is not None:
                chain.append(o)
                o = o.prev_same_sem
            chain.reverse()
            v = 0
            for o in chain:
                v += 16
                o.val = v
                o.sem = dsem[slot]
        stats = {}
        with nc.Block() as block:
            def make(e):
                def body(eng):
                    waited = {}
                    nw = 0
                    for op in self.ops[e]:
                        need = {}
                        deps = op.deps
                        if op.dma and op.prev_same_sem is not None:
                            deps = deps + [op.prev_same_sem]
                        for d in deps:
                            k = id(d.sem)
                            if k not in need or need[k][1] < d.val:
                                need[k] = (d.sem, d.val)
                        for k, (s, v) in need.items():
                            if waited.get(k, 0) >= v:
                                continue
                            eng.wait_ge(s, v)
                            waited[k] = v
                            nw += 1
                        if op.emit is None:
                            continue
                        inst = op.emit(eng)
                        if op.signal:
                            inst.then_inc(op.sem, 16 if op.dma else 1)
                    stats[e] = (len(self.ops[e]), nw)
                return body
            block.tensor(make('pe'))
            block.scalar(make('act'))
            block.vector(make('dve'))
            block.gpsimd(make('pool'))
            block.sync(make('sp'))
        return stats


D = 2048
FF = 5632
INC = 6680
NH = 4
EPS = 1e-6
BIG = 30000.0
OFF = dict(sb_q=0, sb_k=512, sb_v=1024, gd_q=1536, gd_k=2048, gd_v=2560, gd_z=3072, gd_b=3584,
           gd_a=3588, df_q=3592, df_k=4104, df_v=4616, gl_q=5128, gl_k=5384, gl_v=5640,
           gl_g=6152, gl_r=6664)


class Builder:
    def __init__(self, T, depth, debug=(), stages=None):
        self.T = T
        self.depth = depth
        self.debug = set(debug)
        self.stages = stages
        self.uid = 0
        nc = self.nc = bass.Bass("TRN2", target_bir_lowering=False)
        self.P = Prog(nc)
        self.build()

    def sb(self, shape, dt, name='t'):
        self.uid += 1
        return Buf(self.nc.alloc_sbuf_tensor('%s_%d' % (name, self.uid), shape, dt))

    def ring(self, n, shape, dt, name='r'):
        return Ring([self.sb(shape, dt, name) for _ in range(n)])

    def dram(self, name, shape, dt):
        kind = "ExternalOutput" if name in self.debug else "Internal"
        return self.nc.dram_tensor(name, shape, dt, kind=kind).ap()

    def inp(self, name, shape, dt=F32):
        return self.nc.dram_tensor(name, shape, dt, kind="ExternalInput").ap()

    def bank(self):
        b = self.banks[self.bank_i % 4]
        self.bank_i += 1
        return b

    def hbank(self):
        b = self.banks[4 + self.hbank_i % 4]
        self.hbank_i += 1
        return b

    def want(self, s):
        return self.stages is None or s in self.stages

    def build(self):
        nc, P, T, L = self.nc, self.P, self.T, self.depth
        self.x_in = self.inp("x", [T, D])
        self.w_in = self.inp("w_in", [L, D, INC])
        self.w_out = self.inp("w_out", [L, D, D])
        self.w_gate = self.inp("w_gate", [L, D, FF])
        self.w_up = self.inp("w_up", [L, D, FF])
        self.w_down = self.inp("w_down", [L, FF, D])
        self.gains_in = self.inp("gains", [128, (2 * L + 1) * 16])
        self.convw_in = self.inp("convw", [L, 128, 4 * 1536])
        self.small_in = self.inp("small", [L, 128, 16])
        self.hcols_in = self.inp("hcols", [128, L * 8])
        self.lam_in = self.inp("lamv", [L, 128, 256])
        self.w2_in = self.inp("w2", [L, 16, 256])
        self.glab_in = self.inp("glab", [64, L * 4])
        self.out = self.nc.dram_tensor("out", [T, D], F32, kind="ExternalOutput").ap()

        self.xT = self.dram("xT", [D, T], F32)
        self.hT = self.dram("hT", [D, T], BF16)
        self.mixT = self.dram("mixT", [D, T], BF16)
        self.gT = self.dram("gT", [FF, T], BF16)
        self.qsbT = self.dram("qsbT", [512, T], BF16)
        self.ksbT = self.dram("ksbT", [512, T], BF16)
        self.vsb = self.dram("vsb", [T, 512], BF16)
        self.gqkv = self.dram("gqkv", [T, 1536], F32)
        self.gzT = self.dram("gzT", [512, T], F32)
        self.gba = self.dram("gba", [T, 8], F32)
        self.qdfT = self.dram("qdfT", [512, T], BF16)
        self.kdfT = self.dram("kdfT", [512, T], BF16)
        self.vdf = self.dram("vdf", [T, 512], BF16)
        self.qglT = self.dram("qglT", [256, T], F32)
        self.kglT = self.dram("kglT", [256, T], F32)
        self.vgl = self.dram("vgl", [T, 512], F32)
        self.ggT = self.dram("ggT", [512, T], F32)
        self.grT = self.dram("grT", [16, T], F32)

        self.banks = [Buf(nc.alloc_psum_tensor('bank%d' % i, [128, 512], F32)) for i in range(8)]
        self.bank_i = 0
        self.hbank_i = 0
        for b_ in self.banks:
            b_.r.excl = True

        self.consts()
        P.barrier()
        if self.want('pre'):
            with nc.reset_on_exit():
                self.stage_transpose_in()
            P.barrier()
        for l in range(L):
            self.layer(l)
        if self.want('post'):
            with nc.reset_on_exit():
                self.stage_final()
        self.stats = P.emit_all()

    def consts(self):
        nc, P, L = self.nc, self.P, self.depth
        c = self.c = {}

        def mk(name, shape, dt):
            c[name] = self.sb(shape, dt, name)
            return c[name]
        ones = mk('ones', [128, 512], F32)
        P.add('pool', lambda e: e.memset(ones.t[:], 1.0), writes=[ones.r])
        onesb = mk('onesb', [128, 128], BF16)
        P.add('pool', lambda e: e.memset(onesb.t[:], 1.0), writes=[onesb.r])
        nonesb = mk('nonesb', [128, 128], BF16)
        P.add('pool', lambda e: e.memset(nonesb.t[:], -1.0), writes=[nonesb.r])
        nones = mk('nones', [128, 128], F32)
        P.add('pool', lambda e: e.memset(nones.t[:], -1.0), writes=[nones.r])
        zer = mk('zer', [128, 512], F32)
        P.add('pool', lambda e: e.memset(zer.t[:], 0.0), writes=[zer.r])
        cst = mk('cst', [128, 4], F32)
        P.add('pool', lambda e: e.memset(cst.t[:, 0:1], 1.0), writes=[cst.r])
        P.add('pool', lambda e: e.memset(cst.t[:, 1:2], EPS), writes=[cst.r])
        P.add('pool', lambda e: e.memset(cst.t[:, 2:3], 0.0), writes=[cst.r])

        def sel(name, src, pattern, op, fill, base, cm, shape=(128, 128), dt=F32):
            b = mk(name, list(shape), dt)
            n = shape[1]
            P.add('pool', lambda e: e.affine_select(out=b.t[:], in_=src.t[:, :n], pattern=pattern,
                                                    compare_op=op, fill=fill, base=base,
                                                    channel_multiplier=cm),
                  reads=[src.r], writes=[b.r])
            return b
        ident = sel('ident', ones, [[-1, 128]], ALU.is_equal, 0.0, 0, 1)
        ustr_f = sel('ustr_f', ones, [[-1, 128]], ALU.is_gt, 0.0, 0, 1)
        nustr = mk('nustr', [128, 128], BF16)
        P.add('dve', lambda e: e.tensor_scalar(out=nustr.t[:], in0=ustr_f.t[:], scalar1=-1.0,
                                               scalar2=None, op0=ALU.mult),
              reads=[ustr_f.r], writes=[nustr.r])
        for j in range(4):
            a = sel('mstr%d' % j, ones, [[1, 512]], ALU.is_gt, 0.0, -128 * j, -1, shape=(128, 512))
            b = mk('mstrb%d' % j, [128, 512], BF16)
            P.add('dve', lambda e, a=a, b=b: e.tensor_copy(out=b.t[:], in_=a.t[:]), reads=[a.r], writes=[b.r])
            a2 = sel('minc%d' % j, ones, [[1, 512]], ALU.is_ge, 0.0, -128 * j, -1, shape=(128, 512))
            b2 = mk('mincb%d' % j, [128, 512], BF16)
            P.add('dve', lambda e, a=a2, b=b2: e.tensor_copy(out=b.t[:], in_=a.t[:]), reads=[a2.r], writes=[b2.r])
        bd = mk('bd', [128, 128], F32)
        P.add('pool', lambda e: e.memset(bd.t[:], 0.0), writes=[bd.r])
        P.add('pool', lambda e: e.memset(bd.t[0:64, 0:64], 1.0), writes=[bd.r])
        P.add('pool', lambda e: e.memset(bd.t[64:128, 64:128], 1.0), writes=[bd.r])
        upi = sel('upi', bd, [[1, 128]], ALU.is_ge, 0.0, 0, -1)
        lows = sel('lows', bd, [[-1, 128]], ALU.is_gt, 0.0, 0, 1)
        c['upi'] = upi
        pms = mk('pms', [128, 128], F32)
        P.add('dve', lambda e: e.tensor_scalar(out=pms.t[:], in0=lows.t[:], scalar1=BIG, scalar2=-BIG,
                                               op0=ALU.mult, op1=ALU.add), reads=[lows.r], writes=[pms.r])
        nmu = mk('nmu', [128, 128], F32)
        P.add('dve', lambda e: e.tensor_scalar(out=nmu.t[:], in0=upi.t[:], scalar1=-BIG, scalar2=BIG,
                                               op0=ALU.mult, op1=ALU.add), reads=[upi.r], writes=[nmu.r])
        cm = mk('cmask', [128, 512], F32)
        P.add('pool', lambda e: e.memset(cm.t[:], 1.0), writes=[cm.r])
        for k in range(8):
            P.add('pool', lambda e, k=k: e.memset(cm.t[:, 64 * k:64 * k + 1], 0.0), writes=[cm.r])
        gains = mk('gains', [128, (2 * L + 1) * 16], F32)
        P.dma('sp', gains.t[:], self.gains_in, writes=[gains.r])
        hcols = mk('hcols', [128, L * 8], F32)
        P.dma('sp', hcols.t[:], self.hcols_in, writes=[hcols.r])
        glab = mk('glab', [64, L * 4], F32)
        P.dma('sp', glab.t[:], self.glab_in, writes=[glab.r])
        nglab = mk('nglab', [64, L * 4], F32)
        P.add('dve', lambda e: e.tensor_scalar(out=nglab.t[:], in0=glab.t[:], scalar1=-1.0, scalar2=None,
                                               op0=ALU.mult), reads=[glab.r], writes=[nglab.r])

    def stage_transpose_in(self):
        nc, P, T, c = self.nc, self.P, self.T, self.c
        xr = self.ring(2, [128, D], F32, 'xin')
        st = self.ring(2, [128, 16, 512], F32, 'xst')
        xTv = self.xT.rearrange("(c p) t -> p c t", p=128)
        for tb in range(T // 512):
            s = st.next()
            for j in range(4):
                xb = xr.next()
                r0 = tb * 512 + j * 128
                P.dma('sp', xb.t[:], self.x_in[r0:r0 + 128, :], writes=[xb.r])
                for cg in range(4):
                    bk = self.bank()
                    for q in range(4):
                        cc = cg * 4 + q
                        P.add('pe', lambda e, bk=bk, q=q, xb=xb, cc=cc: e.transpose(
                            out=bk.t[:, q * 128:(q + 1) * 128], in_=xb.t[:, cc * 128:(cc + 1) * 128],
                            identity=c['ident'].t[:]), reads=[xb.r, c['ident'].r], writes=[bk.r])
                    eng = 'dve' if cg % 2 == 0 else 'act'
                    if eng == 'dve':
                        P.add('dve', lambda e, bk=bk, s=s, cg=cg, j=j: e.tensor_copy(
                            out=s.t[:, cg * 4:cg * 4 + 4, j * 128:(j + 1) * 128],
                            in_=bk.t[:, :].rearrange("p (q f) -> p q f", q=4)), reads=[bk.r], writes=[s.r])
                    else:
                        P.add('act', lambda e, bk=bk, s=s, cg=cg, j=j: e.copy(
                            out=s.t[:, cg * 4:cg * 4 + 4, j * 128:(j + 1) * 128],
                            in_=bk.t[:, :].rearrange("p (q f) -> p q f", q=4)), reads=[bk.r], writes=[s.r])
            P.dma('sp', xTv[:, :, tb * 512:(tb + 1) * 512], s.t[:], reads=[s.r])

    def norm_block(self, xt, sq, h_out_fn, gi, rs):
        P, c = self.P, self.c
        P.add('act', lambda e: e.activation(out=sq.t[:], in_=xt.t[:], func=AF.Square), reads=[xt.r], writes=[sq.r])
        bk = self.bank()
        for cc in range(16):
            P.add('pe', lambda e, cc=cc: e.matmul(bk.t[:], lhsT=c['onesb'].t[:], rhs=sq.t[:, cc, :],
                                                  start=(cc == 0), stop=(cc == 15)),
                  reads=[sq.r, c['onesb'].r], writes=[bk.r])
        self.rstd_from(bk, rs, 1.0 / D)
        for cc in range(16):
            o, orr = h_out_fn(cc)
            P.add('dve', lambda e, cc=cc, o=o: e.scalar_tensor_tensor(
                out=o, in0=xt.t[:, cc, :], scalar=c['gains'].t[:, gi * 16 + cc:gi * 16 + cc + 1], in1=rs.t[:],
                op0=ALU.mult, op1=ALU.mult), reads=[xt.r, rs.r, c['gains'].r], writes=[orr])

    def rstd_from(self, bk, rs, inv_n, n=512):
        P, c = self.P, self.c
        P.add('dve', lambda e: e.tensor_scalar(out=rs.t[:, :n], in0=bk.t[:, :n], scalar1=inv_n, scalar2=EPS,
                                               op0=ALU.mult, op1=ALU.add), reads=[bk.r], writes=[rs.r])
        P.add('act', lambda e: e.activation(out=rs.t[:, :n], in_=rs.t[:, :n], func=AF.Sqrt), reads=[rs.r], writes=[rs.r])
        P.add('dve', lambda e: e.reciprocal(out=rs.t[:, :n], in_=rs.t[:, :n]), reads=[rs.r], writes=[rs.r])

    def stage_norm(self, gi):
        nc, P, T = self.nc, self.P, self.T
        xr = self.ring(2, [128, 16, 512], F32, 'nx')
        sqr = self.ring(1, [128, 16, 512], BF16, 'nsq')
        hr = self.ring(2, [128, 16, 512], BF16, 'nh')
        rsr = self.ring(2, [128, 512], F32, 'nrs')
        xTv = self.xT.rearrange("(c p) t -> p c t", p=128)
        hTv = self.hT.rearrange("(c p) t -> p c t", p=128)
        for tb in range(T // 512):
            xt = xr.next()
            P.dma('sp', xt.t[:], xTv[:, :, tb * 512:(tb + 1) * 512], writes=[xt.r])
            h = hr.next()
            self.norm_block(xt, sqr.next(), lambda cc, h=h: (h.t[:, cc, :], h.r), gi, rsr.next())
            P.dma('sp', hTv[:, :, tb * 512:(tb + 1) * 512], h.t[:], reads=[h.r])

    def stage_final(self):
        nc, P, T, c = self.nc, self.P, self.T, self.c
        gi = 2 * self.depth
        xr = self.ring(2, [128, 16, 512], F32, 'fx')
        sqr = self.ring(1, [128, 16, 512], BF16, 'fsq')
        hr = self.ring(1, [128, 16, 512], F32, 'fh')
        rsr = self.ring(2, [128, 512], F32, 'frs')
        orr = self.ring(2, [128, D], F32, 'fo')
        xTv = self.xT.rearrange("(c p) t -> p c t", p=128)
        for tb in range(T // 512):
            xt = xr.next()
            P.dma('sp', xt.t[:], xTv[:, :, tb * 512:(tb + 1) * 512], writes=[xt.r])
            h = hr.next()
            self.norm_block(xt, sqr.next(), lambda cc, h=h: (h.t[:, cc, :], h.r), gi, rsr.next())
            for j in range(4):
                ob = orr.next()
                for cg in range(4):
                    bk = self.bank()
                    for q in range(4):
                        cc = cg * 4 + q
                        P.add('pe', lambda e, bk=bk, q=q, h=h, cc=cc, j=j: e.transpose(
                            out=bk.t[:, q * 128:(q + 1) * 128], in_=h.t[:, cc, j * 128:(j + 1) * 128],
                            identity=c['ident'].t[:]), reads=[h.r, c['ident'].r], writes=[bk.r])
                    if cg % 2 == 0:
                        P.add('dve', lambda e, bk=bk, ob=ob, cg=cg: e.tensor_copy(
                            out=ob.t[:, cg * 512:(cg + 1) * 512], in_=bk.t[:]), reads=[bk.r], writes=[ob.r])
                    else:
                        P.add('act', lambda e, bk=bk, ob=ob, cg=cg: e.copy(
                            out=ob.t[:, cg * 512:(cg + 1) * 512], in_=bk.t[:]), reads=[bk.r], writes=[ob.r])
                r0 = tb * 512 + j * 128
                P.dma('sp', self.out[r0:r0 + 128, :], ob.t[:], reads=[ob.r], is_out=True)

    def dense(self, inT, KC, groups):
        nc, P, T = self.nc, self.P, self.T
        TB = min(T, 1024)
        pwmax = 512 if KC <= 16 else 256
        nin = 2 if KC <= 16 else 1
        xin = self.ring(nin, [128, KC, TB], BF16, 'din')
        nw = max(len(g['ws']) for g in groups)
        KG = 4 if KC % 4 == 0 else 1
        wr = [Ring([[self.sb([128, KG, pwmax], BF16, 'dw') for _ in range(KC // KG)] for _ in range(2)])
              for _ in range(nw)]
        inTv = inT.rearrange("(c p) t -> p c t", p=128)
        for tb in range(T // TB):
            xb = xin.next()
            for k0 in range(0, KC, 8):
                k1 = min(KC, k0 + 8)
                P.dma('sp', xb.t[:, k0:k1, :], inTv[:, k0:k1, tb * TB:(tb + 1) * TB], writes=[xb.r])
            for g in groups:
                ncols = g['ncols']
                for c0 in range(0, ncols, pwmax):
                    pw = min(pwmax, ncols - c0)
                    wbs = []
                    for wi, (W, col0) in enumerate(g['ws']):
                        wb = wr[wi].next()
                        Wv = W[:, col0 + c0:col0 + c0 + pw].rearrange("(c p) m -> p c m", p=128)
                        for kg in range(KC // KG):
                            P.dma('pool', wb[kg].t[:, :, :pw], Wv[:, kg * KG:(kg + 1) * KG, :], writes=[wb[kg].r])
                        wbs.append(wb)
                    if g['mode'] == 'F':
                        for m0 in range(0, pw, 128):
                            mw = min(128, pw - m0)
                            for n0 in range(0, TB, 512):
                                bks = []
                                for wb in wbs:
                                    bk = self.bank()
                                    for kc in range(KC):
                                        wk = wb[kc // KG]
                                        P.add('pe', lambda e, bk=bk, wk=wk, kc=kc, m0=m0, mw=mw, n0=n0, xb=xb: e.matmul(
                                            bk.t[:mw, :], lhsT=wk.t[:, kc % KG, m0:m0 + mw], rhs=xb.t[:, kc, n0:n0 + 512],
                                            start=(kc == 0), stop=(kc == KC - 1)), reads=[wk.r, xb.r], writes=[bk.r])
                                    bks.append(bk)
                                g['epi'](bks, c0 + m0, mw, tb * TB + n0)
                    else:
                        wb = wbs[0]
                        for t0 in range(0, TB, 128):
                            bk = self.bank()
                            for kc in range(KC):
                                wk = wb[kc // KG]
                                P.add('pe', lambda e, bk=bk, wk=wk, kc=kc, pw=pw, t0=t0, xb=xb: e.matmul(
                                    bk.t[:, :pw], lhsT=xb.t[:, kc, t0:t0 + 128], rhs=wk.t[:, kc % KG, :pw],
                                    start=(kc == 0), stop=(kc == KC - 1)), reads=[wk.r, xb.r], writes=[bk.r])
                            g['epi']([bk], c0, pw, tb * TB + t0)

    def epi_F_store(self, dst, dt, func=AF.Copy, scale=1.0):
        P = self.P
        st = self.strings[dt]

        def epi(bks, cofs, mw, t0):
            s = st.next()
            P.add('act', lambda e: e.activation(out=s.t[:mw, :], in_=bks[0].t[:mw, :], func=func, scale=scale),
                  reads=[bks[0].r], writes=[s.r])
            P.dma('sp', dst[cofs:cofs + mw, t0:t0 + 512], s.t[:mw, :], reads=[s.r])
        return epi

    def epi_T_store(self, dst, dt):
        P = self.P
        st = self.strings[dt]

        def epi(bks, cofs, pw, t0):
            s = st.next()
            P.add('dve', lambda e: e.tensor_copy(out=s.t[:, :pw], in_=bks[0].t[:, :pw]), reads=[bks[0].r], writes=[s.r])
            P.dma('sp', dst[t0:t0 + 128, cofs:cofs + pw], s.t[:, :pw], reads=[s.r])
        return epi

    def epi_resid(self):
        P = self.P
        xr = self.ring(3, [128, 512], F32, 'erx')

        def epi(bks, cofs, mw, t0):
            x = xr.next()
            P.dma('sp', x.t[:], self.xT[cofs:cofs + 128, t0:t0 + 512], writes=[x.r])
            P.add('dve', lambda e: e.tensor_tensor(out=x.t[:], in0=bks[0].t[:], in1=x.t[:], op=ALU.add),
                  reads=[bks[0].r, x.r], writes=[x.r])
            P.dma('sp', self.xT[cofs:cofs + 128, t0:t0 + 512], x.t[:], reads=[x.r])
        return epi

    def epi_swiglu(self):
        P = self.P
        sr = self.ring(2, [128, 512], F32, 'esw')
        gr = self.ring(3, [128, 512], BF16, 'esg')

        def epi(bks, cofs, mw, t0):
            s = sr.next()
            g = gr.next()
            P.add('act', lambda e: e.activation(out=s.t[:], in_=bks[0].t[:], func=AF.Silu), reads=[bks[0].r], writes=[s.r])
            P.add('dve', lambda e: e.tensor_tensor(out=g.t[:], in0=bks[1].t[:], in1=s.t[:], op=ALU.mult),
                  reads=[bks[1].r, s.r], writes=[g.r])
            P.dma('sp', self.gT[cofs:cofs + 128, t0:t0 + 512], g.t[:], reads=[g.r])
        return epi

    def headnorm(self, o, n, gain_col, gate, dst, tmp, greads=()):
        P, c = self.P, self.c
        sq, rs, y, yb = tmp
        P.add('act', lambda e: e.activation(out=sq.t[:, :n], in_=o.t[:, :n], func=AF.Square), reads=[o.r], writes=[sq.r])
        bk = self.bank()
        P.add('pe', lambda e: e.matmul(bk.t[:, :n], lhsT=c['onesb'].t[:], rhs=sq.t[:, :n], start=True, stop=True),
              reads=[sq.r, c['onesb'].r], writes=[bk.r])
        self.rstd_from(bk, rs, 1.0 / 128, n)
        if gate is None:
            P.add('dve', lambda e: e.scalar_tensor_tensor(out=yb.t[:, :n], in0=o.t[:, :n], scalar=gain_col, in1=rs.t[:, :n],
                                                          op0=ALU.mult, op1=ALU.mult), reads=[o.r, rs.r, c['hcols'].r] + list(greads), writes=[yb.r])
        else:
            P.add('dve', lambda e: e.scalar_tensor_tensor(out=y.t[:, :n], in0=o.t[:, :n], scalar=gain_col, in1=rs.t[:, :n],
                                                          op0=ALU.mult, op1=ALU.mult), reads=[o.r, rs.r, c['hcols'].r], writes=[y.r])
            P.add('pool', lambda e: e.tensor_tensor(out=yb.t[:, :n], in0=y.t[:, :n], in1=gate[0], op=ALU.mult),
                  reads=[y.r, gate[1]], writes=[yb.r])
        P.dma('sp', dst, yb.t[:, :n], reads=[yb.r])

    def hn_tmp(self):
        return (self.sb([128, 512], BF16, 'hsq'), self.sb([128, 512], F32, 'hrs'),
                self.sb([128, 512], F32, 'hy'), self.sb([128, 512], BF16, 'hyb'))

    def layer(self, l):
        nc, P, T = self.nc, self.P, self.T
        lam_init = 0.8 - 0.6 * math.exp(-0.3 * l)
        if self.want('norm1'):
            with nc.reset_on_exit():
                self.stage_norm(2 * l)
            P.barrier()
        if self.want('inproj'):
            with nc.reset_on_exit():
                W = self.w_in[l]
                g = []
                self.strings = {F32: self.ring(3, [128, 512], F32, 'stf'), BF16: self.ring(3, [128, 512], BF16, 'stb')}

                def G(mode, name, ncols, epi):
                    g.append(dict(mode=mode, ws=[(W, OFF[name])], ncols=ncols, epi=epi))
                G('F', 'sb_q', 512, self.epi_F_store(self.qsbT, BF16, scale=128 ** -0.5))
                G('F', 'sb_k', 512, self.epi_F_store(self.ksbT, BF16))
                G('T', 'sb_v', 512, self.epi_T_store(self.vsb, BF16))
                G('T', 'gd_q', 1536, self.epi_T_store(self.gqkv, F32))
                G('F', 'gd_z', 512, self.epi_F_store(self.gzT, F32, func=AF.Silu))
                G('T', 'gd_b', 8, self.epi_T_store(self.gba, F32))
                G('F', 'df_q', 512, self.epi_F_store(self.qdfT, BF16, scale=64 ** -0.5))
                G('F', 'df_k', 512, self.epi_F_store(self.kdfT, BF16))
                G('T', 'df_v', 512, self.epi_T_store(self.vdf, BF16))
                G('F', 'gl_q', 256, self.epi_F_store(self.qglT, F32, scale=64 ** -0.5))
                G('F', 'gl_k', 256, self.epi_F_store(self.kglT, F32))
                G('T', 'gl_v', 512, self.epi_T_store(self.vgl, F32))
                G('F', 'gl_g', 512, self.epi_F_store(self.ggT, F32, func=AF.Silu))
                G('F', 'gl_r', 16, self.epi_F_store(self.grT, F32))
                self.dense(self.hT, 16, g)
            P.barrier()
        if self.want('sb'):
            with nc.reset_on_exit():
                self.stage_sb()
            P.barrier()
        if self.want('diff'):
            with nc.reset_on_exit():
                self.stage_diff(l, lam_init)
            P.barrier()
        if self.want('gla'):
            with nc.reset_on_exit():
                self.stage_gla(l)
            P.barrier()
        if self.want('gdn'):
            with nc.reset_on_exit():
                self.stage_gdn(l)
            P.barrier()
        if self.want('outproj'):
            with nc.reset_on_exit():
                self.dense(self.mixT, 16, [dict(mode='F', ws=[(self.w_out[l], 0)], ncols=D, epi=self.epi_resid())])
            P.barrier()
        if self.want('ffn'):
            with nc.reset_on_exit():
                self.stage_norm(2 * l + 1)
            P.barrier()
            with nc.reset_on_exit():
                self.dense(self.hT, 16, [dict(mode='F', ws=[(self.w_gate[l], 0), (self.w_up[l], 0)], ncols=FF,
                                              epi=self.epi_swiglu())])
            P.barrier()
            with nc.reset_on_exit():
                self.dense(self.gT, 44, [dict(mode='F', ws=[(self.w_down[l], 0)], ncols=D, epi=self.epi_resid())])
            P.barrier()

    def stage_sb(self):
        nc, P, T, c = self.nc, self.P, self.T, self.c
        NT = T // 128
        kT = self.ring(2, [128, T], BF16, 'sbk')
        vt = self.ring(2, [128, NT, 128], BF16, 'sbv')
        qr = self.ring(2, [128, 512], BF16, 'sbq')
        Er = self.ring(2, [128, 512], F32, 'sbE')
        SPr = self.ring(2, [128, 512], BF16, 'sbSP')
        t1r = self.ring(2, [128, 512], F32, 'sbt1')
        t2r = self.ring(2, [128, 512], F32, 'sbt2')
        attr = self.ring(2, [128, 512], BF16, 'sbatt')
        Lf = self.ring(2, [128, 512], F32, 'sbLf')
        Lb = self.ring(2, [128, 512], BF16, 'sbLb')
        orr = self.ring(2, [128, 512], BF16, 'sbo')
        vv = self.vsb.rearrange("(n p) c -> p n c", p=128)
        for h in range(NH):
            k = kT.next()
            P.dma('sp', k.t[:], self.ksbT[h * 128:(h + 1) * 128, :], writes=[k.r])
            v = vt.next()
            for n0_ in range(0, NT, 8):
                P.dma('sp', v.t[:, n0_:n0_ + 8, :], vv[:, n0_:n0_ + 8, h * 128:(h + 1) * 128], writes=[v.r])
            for qt in range(T // 512):
                q = qr.next()
                P.dma('sp', q.t[:], self.qsbT[h * 128:(h + 1) * 128, qt * 512:(qt + 1) * 512], writes=[q.r])
                lf = Lf.next()
                lb = Lb.next()
                P.add('pool', lambda e, lf=lf: e.memset(lf.t[:], 0.0), writes=[lf.r])
                P.add('pool', lambda e, lb=lb: e.memset(lb.t[:], 0.0), writes=[lb.r])
                po = self.hbank()
                kmax = 4 * (qt + 1) - 1
                for kb in range(kmax, -1, -1):
                    jd = kb - 4 * qt
                    pz = self.bank()
                    P.add('pe', lambda e, pz=pz, k=k, kb=kb, q=q: e.matmul(
                        pz.t[:], lhsT=k.t[:, kb * 128:(kb + 1) * 128], rhs=q.t[:], start=True, stop=True),
                        reads=[k.r, q.r], writes=[pz.r])
                    E = Er.next()
                    SP = SPr.next()
                    P.add('act', lambda e, E=E, pz=pz: e.activation(out=E.t[:], in_=pz.t[:], func=AF.Exp), reads=[pz.r], writes=[E.r])
                    P.add('act', lambda e, E=E, SP=SP: e.activation(out=SP.t[:], in_=E.t[:], func=AF.Ln, bias=c['cst'].t[:, 0:1]),
                          reads=[E.r, c['cst'].r], writes=[SP.r])
                    if jd >= 0:
                        m = c['mstrb%d' % jd]
                        P.add('dve', lambda e, SP=SP, m=m: e.tensor_tensor(out=SP.t[:], in0=SP.t[:], in1=m.t[:], op=ALU.mult),
                              reads=[SP.r, m.r], writes=[SP.r])
                    pt = self.bank()
                    P.add('pe', lambda e, pt=pt, SP=SP: e.matmul(pt.t[:], lhsT=c['nustr'].t[:], rhs=SP.t[:], start=True, stop=False),
                          reads=[SP.r, c['nustr'].r], writes=[pt.r])
                    P.add('pe', lambda e, pt=pt, lb=lb: e.matmul(pt.t[:], lhsT=c['nonesb'].t[:], rhs=lb.t[:], start=False, stop=True),
                          reads=[lb.r, c['nonesb'].r], writes=[pt.r])
                    t1 = t1r.next()
                    P.add('dve', lambda e, t1=t1, pz=pz, SP=SP: e.tensor_tensor(out=t1.t[:], in0=pz.t[:], in1=SP.t[:], op=ALU.subtract),
                          reads=[pz.r, SP.r], writes=[t1.r])
                    t2 = t2r.next()
                    P.add('dve', lambda e, t2=t2, pt=pt, t1=t1: e.tensor_tensor(out=t2.t[:], in0=pt.t[:], in1=t1.t[:], op=ALU.add),
                          reads=[pt.r, t1.r], writes=[t2.r])
                    att = attr.next()
                    P.add('act', lambda e, att=att, t2=t2: e.activation(out=att.t[:], in_=t2.t[:], func=AF.Exp), reads=[t2.r], writes=[att.r])
                    if jd >= 0:
                        m = c['mstrb%d' % jd]
                        P.add('pool', lambda e, att=att, m=m: e.tensor_tensor(out=att.t[:], in0=att.t[:], in1=m.t[:], op=ALU.mult),
                              reads=[att.r, m.r], writes=[att.r])
                    P.add('pe', lambda e, po=po, v=v, kb=kb, att=att, kmax=kmax: e.matmul(
                        po.t[:], lhsT=v.t[:, kb, :], rhs=att.t[:], start=(kb == kmax), stop=(kb == 0)),
                        reads=[v.r, att.r], writes=[po.r])
                    if kb > 0:
                        P.add('pool', lambda e, lf=lf, SP=SP: e.tensor_tensor(out=lf.t[:], in0=lf.t[:], in1=SP.t[:], op=ALU.add),
                              reads=[lf.r, SP.r], writes=[lf.r])
                        P.add('pool', lambda e, lf=lf, lb=lb: e.tensor_copy(out=lb.t[:], in_=lf.t[:]), reads=[lf.r], writes=[lb.r])
                o = orr.next()
                P.add('act', lambda e, o=o, po=po: e.copy(out=o.t[:], in_=po.t[:]), reads=[po.r], writes=[o.r])
                P.dma('sp', self.mixT[h * 128:(h + 1) * 128, qt * 512:(qt + 1) * 512], o.t[:], reads=[o.r])

    def stage_diff(self, l, lam_init):
        nc, P, T, c = self.nc, self.P, self.T, self.c
        NT = T // 128
        lv = self.sb([128, 256], F32, 'lv')
        P.dma('sp', lv.t[:], self.lam_in[l], writes=[lv.r])
        pr = self.sb([128, 128], F32, 'lpr')
        dots = self.sb([128, 2], F32, 'ldots')
        nlam = self.sb([128, 1], F32, 'nlam')
        P.add('dve', lambda e: e.tensor_tensor(out=pr.t[:, 0:64], in0=lv.t[:, 0:64], in1=lv.t[:, 64:128], op=ALU.mult), reads=[lv.r], writes=[pr.r])
        P.add('dve', lambda e: e.tensor_tensor(out=pr.t[:, 64:128], in0=lv.t[:, 128:192], in1=lv.t[:, 192:256], op=ALU.mult), reads=[lv.r, pr.r], writes=[pr.r])
        P.add('dve', lambda e: e.tensor_reduce(out=dots.t[:], in_=pr.t[:].rearrange("p (a b) -> p a b", a=2), axis=AX.X, op=ALU.add),
              reads=[pr.r], writes=[dots.r])
        P.add('act', lambda e: e.activation(out=dots.t[:], in_=dots.t[:], func=AF.Exp), reads=[dots.r], writes=[dots.r])
        P.add('dve', lambda e: e.tensor_tensor(out=nlam.t[:], in0=dots.t[:, 1:2], in1=dots.t[:, 0:1], op=ALU.subtract), reads=[dots.r], writes=[nlam.r])
        P.add('dve', lambda e: e.tensor_scalar(out=nlam.t[:], in0=nlam.t[:], scalar1=-lam_init, scalar2=None, op0=ALU.add), reads=[nlam.r], writes=[nlam.r])
        gcol = self.sb([128, 1], F32, 'dgc')
        P.add('dve', lambda e: e.tensor_scalar(out=gcol.t[:], in0=c['hcols'].t[:, l * 8 + 1:l * 8 + 2], scalar1=1.0 - lam_init, scalar2=None,
                                               op0=ALU.mult), reads=[c['hcols'].r], writes=[gcol.r])
        kT = self.ring(2, [128, T], BF16, 'dfk')
        vt = self.ring(2, [128, NT, 128], BF16, 'dfv')
        qr = self.ring(2, [128, 512], BF16, 'dfq')
        attr = self.ring(4, [128, 512], BF16, 'dfatt')
        rr = self.ring(2, [128, 512], F32, 'dfr')
        o0r = self.ring(2, [128, 512], F32, 'dfo0')
        o1r = self.ring(2, [128, 512], F32, 'dfo1')
        tmp = self.hn_tmp()
        vv = self.vdf.rearrange("(n p) c -> p n c", p=128)
        for h in range(NH):
            k = kT.next()
            P.dma('sp', k.t[:], self.kdfT[h * 128:(h + 1) * 128, :], writes=[k.r])
            v = vt.next()
            for n0_ in range(0, NT, 8):
                P.dma('sp', v.t[:, n0_:n0_ + 8, :], vv[:, n0_:n0_ + 8, h * 128:(h + 1) * 128], writes=[v.r])
            for qt in range(T // 512):
                q = qr.next()
                P.dma('sp', q.t[:], self.qdfT[h * 128:(h + 1) * 128, qt * 512:(qt + 1) * 512], writes=[q.r])
                po = [self.hbank(), self.hbank()]
                ps = [self.hbank(), self.hbank()]
                kmax = 4 * (qt + 1) - 1
                for kb in range(kmax + 1):
                    jd = kb - 4 * qt
                    for m in range(2):
                        pz = self.bank()
                        P.add('pe', lambda e, pz=pz, k=k, kb=kb, q=q, m=m: e.matmul(
                            pz.t[:], lhsT=k.t[64 * m:64 * m + 64, kb * 128:(kb + 1) * 128], rhs=q.t[64 * m:64 * m + 64, :],
                            start=True, stop=True), reads=[k.r, q.r], writes=[pz.r])
                        att = attr.next()
                        P.add('act', lambda e, att=att, pz=pz: e.activation(out=att.t[:], in_=pz.t[:], func=AF.Exp), reads=[pz.r], writes=[att.r])
                        if jd >= 0:
                            mk = c['mincb%d' % jd]
                            P.add('dve', lambda e, att=att, mk=mk: e.tensor_tensor(out=att.t[:], in0=att.t[:], in1=mk.t[:], op=ALU.mult),
                                  reads=[att.r, mk.r], writes=[att.r])
                        P.add('pe', lambda e, pb=po[m], v=v, kb=kb, att=att, kmax=kmax: e.matmul(
                            pb.t[:], lhsT=v.t[:, kb, :], rhs=att.t[:], start=(kb == 0), stop=(kb == kmax)),
                            reads=[v.r, att.r], writes=[po[m].r])
                        P.add('pe', lambda e, pb=ps[m], att=att, kb=kb, kmax=kmax: e.matmul(
                            pb.t[:], lhsT=c['onesb'].t[:], rhs=att.t[:], start=(kb == 0), stop=(kb == kmax)),
                            reads=[att.r, c['onesb'].r], writes=[ps[m].r])
                r0 = rr.next()
                r1 = rr.next()
                o0 = o0r.next()
                o1 = o1r.next()
                P.add('dve', lambda e, r0=r0, pb=ps[0]: e.reciprocal(out=r0.t[:], in_=pb.t[:]), reads=[ps[0].r], writes=[r0.r])
                P.add('dve', lambda e, r1=r1, pb=ps[1]: e.reciprocal(out=r1.t[:], in_=pb.t[:]), reads=[ps[1].r], writes=[r1.r])
                P.add('dve', lambda e, o0=o0, pb=po[0], r0=r0: e.tensor_tensor(out=o0.t[:], in0=pb.t[:], in1=r0.t[:], op=ALU.mult),
                      reads=[po[0].r, r0.r], writes=[o0.r])
                P.add('dve', lambda e, o1=o1, pb=po[1], r1=r1: e.tensor_tensor(out=o1.t[:], in0=pb.t[:], in1=r1.t[:], op=ALU.mult),
                      reads=[po[1].r, r1.r], writes=[o1.r])
                P.add('dve', lambda e, o0=o0, o1=o1: e.scalar_tensor_tensor(out=o0.t[:], in0=o1.t[:], scalar=nlam.t[:, 0:1], in1=o0.t[:],
                                                                             op0=ALU.mult, op1=ALU.add), reads=[o0.r, o1.r, nlam.r], writes=[o0.r])
                self.headnorm(o0, 512, gcol.t[:, 0:1], None,
                              self.mixT[(8 + h) * 128:(9 + h) * 128, qt * 512:(qt + 1) * 512], tmp, greads=[gcol.r])

    def stage_gla(self, l):
        nc, P, T, c = self.nc, self.P, self.T, self.c
        NT = T // 128
        NC = T // 64
        w2 = self.sb([16, 256], F32, 'w2')
        P.dma('sp', w2.t[:], self.w2_in[l], writes=[w2.r])
        grt = self.sb([16, T], F32, 'grt')
        P.dma('sp', grt.t[:], self.grT, writes=[grt.r])
        qtl = self.ring(1, [64, T], F32, 'glq')
        ktl = self.ring(1, [64, T], F32, 'glk')
        ktok = self.ring(1, [128, NT, 64], F32, 'glkt')
        vtok = self.ring(1, [128, NT, 128], F32, 'glv')
        elr = self.ring(2, [64, NC], F32, 'glel')
        Er = self.ring(2, [64, 512], F32, 'glE')
        cumr = self.ring(2, [64, 512], F32, 'glcum')
        ebr = self.ring(2, [64, 512], F32, 'gleb')
        attr = self.ring(2, [128, 128], F32, 'glatt')
        Sr = self.ring(3, [64, 128], F32, 'glS')
        Uer = self.ring(3, [64, 128], F32, 'glUe')
        orr = self.ring(2, [128, 512], F32, 'glo')
        gater = self.ring(2, [128, 512], F32, 'glg')
        tmp = self.hn_tmp()
        vv = self.vgl.rearrange("(n p) c -> p n c", p=128)
        for h in range(NH):
            qt_ = qtl.next()
            kt_ = ktl.next()
            P.dma('sp', qt_.t[:], self.qglT[h * 64:(h + 1) * 64, :], writes=[qt_.r])
            P.dma('sp', kt_.t[:], self.kglT[h * 64:(h + 1) * 64, :], writes=[kt_.r])
            v = vtok.next()
            for n0_ in range(0, NT, 8):
                P.dma('sp', v.t[:, n0_:n0_ + 8, :], vv[:, n0_:n0_ + 8, h * 128:(h + 1) * 128], writes=[v.r])
            el = elr.next()
            ktk = ktok.next()
            for tb in range(T // 512):
                sl = slice(tb * 512, (tb + 1) * 512)
                pu = self.bank()
                P.add('pe', lambda e, pu=pu, sl=sl, h=h: e.matmul(pu.t[:64, :], lhsT=w2.t[:, h * 64:(h + 1) * 64], rhs=grt.t[:, sl],
                                                                  start=True, stop=True), reads=[w2.r, grt.r], writes=[pu.r])
                E = Er.next()
                P.add('act', lambda e, E=E, pu=pu, h=h: e.activation(out=E.t[:], in_=pu.t[:64, :], func=AF.Exp, scale=-1.0,
                                                                     bias=c['nglab'].t[:, l * 4 + h:l * 4 + h + 1]),
                      reads=[pu.r, c['nglab'].r], writes=[E.r])
                P.add('act', lambda e, E=E: e.activation(out=E.t[:], in_=E.t[:], func=AF.Ln, bias=c['cst'].t[0:64, 0:1]),
                      reads=[E.r, c['cst'].r], writes=[E.r])
                cum = cumr.next()
                P.add('dve', lambda e, cum=cum, E=E: e.tensor_tensor_scan(out=cum.t[:], data0=c['cmask'].t[0:64, :], data1=E.t[:],
                                                                          initial=0.0, op0=ALU.mult, op1=ALU.add),
                      reads=[E.r, c['cmask'].r], writes=[cum.r])
                eb = ebr.next()
                P.add('act', lambda e, eb=eb, cum=cum: e.activation(out=eb.t[:], in_=cum.t[:], func=AF.Exp, scale=-1.0 / 16), reads=[cum.r], writes=[eb.r])
                P.add('dve', lambda e, eb=eb, el=el, tb=tb: e.tensor_copy(
                    out=el.t[:, tb * 8:(tb + 1) * 8], in_=eb.t[:, :].rearrange("p (n f) -> p n f", f=64)[:, :, 63]),
                    reads=[eb.r], writes=[el.r])
                P.add('dve', lambda e, eb=eb, qt_=qt_, sl=sl: e.tensor_tensor(out=qt_.t[:, sl], in0=qt_.t[:, sl], in1=eb.t[:], op=ALU.mult),
                      reads=[eb.r, qt_.r], writes=[qt_.r])
                enb = ebr.next()
                P.add('act', lambda e, enb=enb, cum=cum: e.activation(out=enb.t[:], in_=cum.t[:], func=AF.Exp, scale=1.0 / 16), reads=[cum.r], writes=[enb.r])
                P.add('dve', lambda e, enb=enb, kt_=kt_, sl=sl: e.tensor_tensor(out=kt_.t[:, sl], in0=kt_.t[:, sl], in1=enb.t[:], op=ALU.mult),
                      reads=[enb.r, kt_.r], writes=[kt_.r])
                pk = self.bank()
                for j in range(4):
                    P.add('pe', lambda e, pk=pk, j=j, kt_=kt_, tb=tb: e.transpose(
                        out=pk.t[:, j * 64:(j + 1) * 64], in_=kt_.t[:, tb * 512 + j * 128: tb * 512 + (j + 1) * 128],
                        identity=c['ident'].t[0:64, 0:64]), reads=[kt_.r, c['ident'].r], writes=[pk.r])
                P.add('act', lambda e, pk=pk, ktk=ktk, tb=tb: e.copy(
                    out=ktk.t[:, tb * 4:(tb + 1) * 4, :], in_=pk.t[:, 0:256].rearrange("p (j d) -> p j d", j=4)),
                    reads=[pk.r], writes=[ktk.r])
            S = Sr.next()
            P.add('pool', lambda e, S=S: e.memset(S.t[:], 0.0), writes=[S.r])
            o = None
            for tt in range(NT):
                if tt % 4 == 0:
                    o = orr.next()
                tsl = slice(tt * 128, (tt + 1) * 128)
                pa = self.bank()
                P.add('pe', lambda e, pa=pa, kt_=kt_, qt_=qt_, tsl=tsl: e.matmul(pa.t[:, :128], lhsT=kt_.t[:, tsl], rhs=qt_.t[:, tsl],
                                                                                 start=True, stop=True), reads=[kt_.r, qt_.r], writes=[pa.r])
                att = attr.next()
                P.add('dve', lambda e, att=att, pa=pa: e.tensor_tensor(out=att.t[:], in0=pa.t[:, :128], in1=c['upi'].t[:], op=ALU.mult),
                      reads=[pa.r, c['upi'].r], writes=[att.r])
                po = self.bank()
                P.add('pe', lambda e, po=po, v=v, tt=tt, att=att: e.matmul(po.t[:, :128], lhsT=v.t[:, tt, :], rhs=att.t[:], start=True, stop=False),
                      reads=[v.r, att.r], writes=[po.r])
                for half in range(2):
                    n = 2 * tt + half
                    rows = slice(64 * half, 64 * half + 64)
                    cols = slice(64 * half, 64 * half + 64)
                    P.add('pe', lambda e, po=po, S=S, qt_=qt_, n=n, cols=cols, half=half: e.matmul(
                        po.t[:, cols], lhsT=S.t[:], rhs=qt_.t[:, n * 64:(n + 1) * 64], start=False, stop=(half == 1)),
                        reads=[S.r, qt_.r], writes=[po.r])
                    pU = self.bank()
                    P.add('pe', lambda e, pU=pU, ktk=ktk, v=v, tt=tt, rows=rows: e.matmul(
                        pU.t[:64, :128], lhsT=ktk.t[rows, tt, :], rhs=v.t[rows, tt, :], start=True, stop=True),
                        reads=[ktk.r, v.r], writes=[pU.r])
                    Ue = Uer.next()
                    P.add('act', lambda e, Ue=Ue, pU=pU, el=el, n=n: e.activation(out=Ue.t[:], in_=pU.t[:64, :128], func=AF.Copy,
                                                                                  scale=el.t[:, n:n + 1]), reads=[pU.r, el.r], writes=[Ue.r])
                    S2 = Sr.next()
                    P.add('dve', lambda e, S2=S2, S=S, el=el, n=n, Ue=Ue: e.scalar_tensor_tensor(
                        out=S2.t[:], in0=S.t[:], scalar=el.t[:, n:n + 1], in1=Ue.t[:], op0=ALU.mult, op1=ALU.add),
                        reads=[S.r, el.r, Ue.r], writes=[S2.r])
                    S = S2
                P.add('act', lambda e, o=o, po=po, tt=tt: e.copy(out=o.t[:, (tt % 4) * 128:(tt % 4 + 1) * 128], in_=po.t[:, :128]),
                      reads=[po.r], writes=[o.r])
                if tt % 4 == 3:
                    tb = tt // 4
                    gt = gater.next()
                    P.dma('sp', gt.t[:], self.ggT[h * 128:(h + 1) * 128, tb * 512:(tb + 1) * 512], writes=[gt.r])
                    self.headnorm(o, 512, c['hcols'].t[:, l * 8 + 2:l * 8 + 3], (gt.t[:], gt.r),
                                  self.mixT[(12 + h) * 128:(13 + h) * 128, tb * 512:(tb + 1) * 512], tmp)

    def stage_gdn(self, l):
        nc, P, T, c = self.nc, self.P, self.T, self.c
        NT = T // 128
        cw = self.sb([128, 4 * 1536], F32, 'cw')
        P.dma('sp', cw.t[:], self.convw_in[l], writes=[cw.r])
        sm = self.sb([128, 16], F32, 'sm')
        P.dma('sp', sm.t[:], self.small_in[l], writes=[sm.r])
        negA = self.sb([128, 4], F32, 'negA')
        P.add('act', lambda e: e.activation(out=negA.t[:], in_=sm.t[:, 0:4], func=AF.Exp), reads=[sm.r], writes=[negA.r])
        P.add('dve', lambda e: e.tensor_scalar(out=negA.t[:], in0=negA.t[:], scalar1=-1.0, scalar2=None, op0=ALU.mult), reads=[negA.r], writes=[negA.r])
        Xr = [self.ring(1, [128, 1536], F32, 'gx%d' % j) for j in range(4)]
        cvr = self.ring(2, [128, 1536], F32, 'gcv')
        tpr = self.ring(1, [128, 1536], F32, 'gtp')
        sqr = self.ring(1, [128, 1024], F32, 'gsq')
        bar = self.ring(2, [128, 8], F32, 'gba')
        smr = self.ring(2, [128, 32], F32, 'gsm')
        S = [self.sb([128, 128], F32, 'gS%d' % h) for h in range(NH)]
        oacc = [self.ring(1, [128, 512], F32, 'go%d' % h) for h in range(NH)]
        gater = self.ring(2, [128, 512], F32, 'ggate')
        tmp = self.hn_tmp()

        def rg(n, shape=(128, 128), nm='g'):
            return [self.ring(n, list(shape), F32, nm + str(h)) for h in range(NH)]
        kTr = rg(2, (128, 256), 'gkT')
        dgr = rg(1, nm='gdg')
        eGr = rg(1, nm='geG')
        dSr = rg(1, nm='gdS')
        dTr = rg(1, nm='gdT')
        Pr = rg(3, (128, 256), 'gP')
        Tr = rg(3, (128, 256), 'gT')
        QKr = rg(1, nm='gQK')
        qtr = rg(1, nm='gqt')
        vbr = rg(1, nm='gvb')
        kbgr = rg(1, nm='gkbg')
        khr = rg(1, nm='gkh')
        ur = rg(1, nm='gu')
        wTr = rg(1, nm='gwT')
        vnr = rg(1, nm='gvn')
        p2sr = rg(1, nm='gp2s')
        for h in range(NH):
            P.add('pool', lambda e, h=h: e.memset(S[h].t[:], 0.0), writes=[S[h].r])
        oh = [None] * NH
        for tt in range(NT):
            r0 = tt * 128
            X = [Xr[j].next() for j in range(4)]
            for j in range(4):
                sh = 3 - j
                if r0 - sh < 0:
                    P.add('pool', lambda e, xb=X[j]: e.memset(xb.t[0:32, :], 0.0), writes=[X[j].r])
                    if sh > 0:
                        P.dma('sp', X[j].t[sh:128, :], self.gqkv[0:128 - sh, :], writes=[X[j].r])
                    else:
                        P.dma('sp', X[j].t[:], self.gqkv[0:128, :], writes=[X[j].r])
                else:
                    P.dma('sp', X[j].t[:], self.gqkv[r0 - sh:r0 - sh + 128, :], writes=[X[j].r])
            cv = cvr.next()
            tp = tpr.next()
            P.add('dve', lambda e, cv=cv, X=X: e.tensor_tensor(out=cv.t[:], in0=X[3].t[:], in1=cw.t[:, 3 * 1536:4 * 1536], op=ALU.mult),
                  reads=[X[3].r, cw.r], writes=[cv.r])
            for j in range(3):
                P.add('pool', lambda e, tp=tp, X=X, j=j: e.tensor_tensor(out=tp.t[:], in0=X[j].t[:], in1=cw.t[:, j * 1536:(j + 1) * 1536], op=ALU.mult),
                      reads=[X[j].r, cw.r], writes=[tp.r])
                P.add('dve', lambda e, cv=cv, tp=tp: e.tensor_tensor(out=cv.t[:], in0=cv.t[:], in1=tp.t[:], op=ALU.add),
                      reads=[cv.r, tp.r], writes=[cv.r])
            P.add('act', lambda e, cv=cv: e.activation(out=cv.t[:], in_=cv.t[:], func=AF.Silu), reads=[cv.r], writes=[cv.r])
            sq = sqr.next()
            s_ = smr.next()
            P.add('pool', lambda e, sq=sq, cv=cv: e.tensor_tensor(out=sq.t[:], in0=cv.t[:, 0:1024], in1=cv.t[:, 0:1024], op=ALU.mult),
                  reads=[cv.r], writes=[sq.r])
            P.add('dve', lambda e, s_=s_, sq=sq: e.tensor_reduce(out=s_.t[:, 0:8], in_=sq.t[:].rearrange("p (a b) -> p a b", a=8), axis=AX.X, op=ALU.add),
                  reads=[sq.r], writes=[s_.r])
            P.add('dve', lambda e, s_=s_: e.tensor_scalar(out=s_.t[:, 0:8], in0=s_.t[:, 0:8], scalar1=EPS, scalar2=None, op0=ALU.add), reads=[s_.r], writes=[s_.r])
            P.add('act', lambda e, s_=s_: e.activation(out=s_.t[:, 0:8], in_=s_.t[:, 0:8], func=AF.Sqrt), reads=[s_.r], writes=[s_.r])
            P.add('dve', lambda e, s_=s_: e.reciprocal(out=s_.t[:, 0:8], in_=s_.t[:, 0:8]), reads=[s_.r], writes=[s_.r])
            P.add('dve', lambda e, s_=s_: e.tensor_scalar(out=s_.t[:, 0:4], in0=s_.t[:, 0:4], scalar1=128 ** -0.5, scalar2=None, op0=ALU.mult), reads=[s_.r], writes=[s_.r])
            for i in range(8):
                eng = 'dve' if i % 2 == 0 else 'pool'
                P.add(eng, lambda e, cv=cv, s_=s_, i=i: e.tensor_scalar(out=cv.t[:, i * 128:(i + 1) * 128], in0=cv.t[:, i * 128:(i + 1) * 128],
                                                                        scalar1=s_.t[:, i:i + 1], scalar2=None, op0=ALU.mult),
                      reads=[cv.r, s_.r], writes=[cv.r])
            ba = bar.next()
            P.dma('sp', ba.t[:], self.gba[r0:r0 + 128, :], writes=[ba.r])
            P.add('act', lambda e, s_=s_, ba=ba: e.activation(out=s_.t[:, 8:12], in_=ba.t[:, 0:4], func=AF.Sigmoid), reads=[ba.r], writes=[s_.r])
            P.add('dve', lambda e, ba=ba: e.tensor_tensor(out=ba.t[:, 4:8], in0=ba.t[:, 4:8], in1=sm.t[:, 4:8], op=ALU.add), reads=[ba.r, sm.r], writes=[ba.r])
            P.add('act', lambda e, ba=ba: e.activation(out=ba.t[:, 4:8], in_=ba.t[:, 4:8], func=AF.Exp), reads=[ba.r], writes=[ba.r])
            P.add('act', lambda e, ba=ba: e.activation(out=ba.t[:, 4:8], in_=ba.t[:, 4:8], func=AF.Ln, bias=c['cst'].t[:, 0:1]), reads=[ba.r, c['cst'].r], writes=[ba.r])
            P.add('dve', lambda e, s_=s_, ba=ba: e.tensor_tensor(out=s_.t[:, 12:16], in0=ba.t[:, 4:8], in1=negA.t[:], op=ALU.mult), reads=[ba.r, negA.r], writes=[s_.r])
            pg = self.bank()
            P.add('pe', lambda e, pg=pg, s_=s_: e.matmul(pg.t[:, 0:4], lhsT=c['upi'].t[:], rhs=s_.t[:, 12:16], start=True, stop=True),
                  reads=[s_.r, c['upi'].r], writes=[pg.r])
            P.add('dve', lambda e, pg=pg, s_=s_: e.tensor_copy(out=s_.t[:, 16:20], in_=pg.t[:, 0:4]), reads=[pg.r], writes=[s_.r])
            P.add('act', lambda e, s_=s_: e.activation(out=s_.t[:, 20:24], in_=s_.t[:, 16:20], func=AF.Exp), reads=[s_.r], writes=[s_.r])
            P.add('dve', lambda e, s_=s_: e.tensor_tensor(out=s_.t[:, 24:28], in0=s_.t[:, 20:24], in1=s_.t[:, 8:12], op=ALU.mult), reads=[s_.r], writes=[s_.r])
            P.add('dve', lambda e, s_=s_: e.tensor_scalar(out=s_.t[:, 28:32], in0=s_.t[:, 8:12], scalar1=-1.0, scalar2=None, op0=ALU.mult), reads=[s_.r], writes=[s_.r])
            import os
            PH = int(os.environ.get('GDN_PHASE', '9'))
            SUB = int(os.environ.get('GDN_SUB', '99'))
            if PH < 2:
                continue
            st = [dict() for _ in range(NH)]
            for h in range(NH):
                d = st[h]
                qh = cv.t[:, h * 128:(h + 1) * 128]
                kh = cv.t[:, 512 + h * 128:512 + (h + 1) * 128]
                vh = cv.t[:, 1024 + h * 128:1024 + (h + 1) * 128]
                gccol = s_.t[:, 16 + h:17 + h]
                pT = self.bank()
                P.add('pe', lambda e, pT=pT, qh=qh: e.transpose(out=pT.t[:, 0:128], in_=qh, identity=c['ident'].t[:]), reads=[cv.r, c['ident'].r], writes=[pT.r])
                P.add('pe', lambda e, pT=pT, kh=kh: e.transpose(out=pT.t[:, 128:256], in_=kh, identity=c['ident'].t[:]), reads=[cv.r, c['ident'].r], writes=[pT.r])
                kT = kTr[h].next()
                P.add('act', lambda e, kT=kT, pT=pT: e.copy(out=kT.t[:], in_=pT.t[:, 0:256]), reads=[pT.r], writes=[kT.r])
                if SUB < 2:
                    continue
                dg = dgr[h].next()
                P.add('pool', lambda e, dg=dg, gccol=gccol: e.tensor_scalar(out=dg.t[:], in0=c['ident'].t[:], scalar1=gccol, scalar2=None, op0=ALU.mult),
                      reads=[s_.r, c['ident'].r], writes=[dg.r])
                pG = self.bank()
                P.add('pe', lambda e, pG=pG, dg=dg: e.matmul(pG.t[:, 0:128], lhsT=c['ones'].t[:, 0:128], rhs=dg.t[:], start=True, stop=True),
                      reads=[dg.r, c['ones'].r], writes=[pG.r])
                eG = eGr[h].next()
                P.add('act', lambda e, eG=eG, pG=pG: e.activation(out=eG.t[:], in_=pG.t[:, 0:128], func=AF.Exp), reads=[pG.r], writes=[eG.r])
                if SUB < 3:
                    continue
                pD = self.bank()
                P.add('pe', lambda e, pD=pD, dg=dg: e.matmul(pD.t[:, 0:128], lhsT=dg.t[:], rhs=c['ones'].t[:, 0:128], start=True, stop=False),
                      reads=[dg.r, c['ones'].r], writes=[pD.r])
                P.add('pe', lambda e, pD=pD, dg=dg: e.matmul(pD.t[:, 0:128], lhsT=c['nones'].t[:], rhs=dg.t[:], start=False, stop=True),
                      reads=[dg.r, c['nones'].r], writes=[pD.r])
                dS = dSr[h].next()
                P.add('dve', lambda e, dS=dS, pD=pD: e.tensor_tensor(out=dS.t[:], in0=pD.t[:, 0:128], in1=c['pms'].t[:], op=ALU.add),
                      reads=[pD.r, c['pms'].r], writes=[dS.r])
                P.add('act', lambda e, dS=dS: e.activation(out=dS.t[:], in_=dS.t[:], func=AF.Exp), reads=[dS.r], writes=[dS.r])
                dT = dTr[h].next()
                P.add('dve', lambda e, dT=dT, pD=pD: e.tensor_tensor(out=dT.t[:], in0=pD.t[:, 0:128], in1=c['nmu'].t[:], op=ALU.add),
                      reads=[pD.r, c['nmu'].r], writes=[dT.r])
                P.add('act', lambda e, dT=dT: e.activation(out=dT.t[:], in_=dT.t[:], func=AF.Exp, scale=-1.0), reads=[dT.r], writes=[dT.r])
                if SUB < 4:
                    continue
                pK = self.bank()
                P.add('pe', lambda e, pK=pK, kT=kT: e.matmul(pK.t[:, 0:128], lhsT=kT.t[:, 128:256], rhs=kT.t[:, 128:256], start=True, stop=True),
                      reads=[kT.r], writes=[pK.r])
                P.add('pe', lambda e, pK=pK, kT=kT: e.matmul(pK.t[:, 128:256], lhsT=kT.t[:, 128:256], rhs=kT.t[:, 0:128], start=True, stop=True),
                      reads=[kT.r], writes=[pK.r])
                Pm = Pr[h].next()
                P.add('act', lambda e, Pm=Pm, pK=pK, s_=s_, h=h: e.activation(out=Pm.t[:, 0:128], in_=pK.t[:, 0:128], func=AF.Copy,
                                                                             scale=s_.t[:, 28 + h:29 + h]), reads=[pK.r, s_.r], writes=[Pm.r])
                P.add('dve', lambda e, Pm=Pm, dS=dS: e.tensor_tensor(out=Pm.t[:, 0:128], in0=Pm.t[:, 0:128], in1=dS.t[:], op=ALU.mult),
                      reads=[Pm.r, dS.r], writes=[Pm.r])
                QK = QKr[h].next()
                P.add('dve', lambda e, QK=QK, pK=pK, dT=dT: e.tensor_tensor(out=QK.t[:], in0=pK.t[:, 128:256], in1=dT.t[:], op=ALU.mult),
                      reads=[pK.r, dT.r], writes=[QK.r])
                if SUB < 5:
                    continue
                pB = self.bank()
                P.add('pe', lambda e, pB=pB, Pm=Pm: e.transpose(out=pB.t[:, 0:128], in_=Pm.t[:, 0:128], identity=c['ident'].t[:]),
                      reads=[Pm.r, c['ident'].r], writes=[pB.r])
                P.add('act', lambda e, Pm=Pm, pB=pB: e.copy(out=Pm.t[:, 128:256], in_=pB.t[:, 0:128]), reads=[pB.r, Pm.r], writes=[Pm.r])
                Tm = Tr[h].next()
                P.add('pool', lambda e, Tm=Tm, Pm=Pm: e.tensor_tensor(out=Tm.t[:, 0:128], in0=Pm.t[:, 0:128], in1=c['ident'].t[:], op=ALU.add),
                      reads=[Pm.r, c['ident'].r], writes=[Tm.r])
                P.add('pool', lambda e, Tm=Tm, Pm=Pm: e.tensor_tensor(out=Tm.t[:, 128:256], in0=Pm.t[:, 128:256], in1=c['ident'].t[:], op=ALU.add),
                      reads=[Pm.r, c['ident'].r, Tm.r], writes=[Tm.r])
                if SUB < 6:
                    continue
                qt_ = qtr[h].next()
                P.add('pool', lambda e, qt_=qt_, kT=kT, eG=eG: e.tensor_tensor(out=qt_.t[:], in0=kT.t[:, 0:128], in1=eG.t[:], op=ALU.mult),
                      reads=[kT.r, eG.r], writes=[qt_.r])
                vb = vbr[h].next()
                P.add('pool', lambda e, vb=vb, vh=vh, s_=s_, h=h: e.tensor_scalar(out=vb.t[:], in0=vh, scalar1=s_.t[:, 8 + h:9 + h], scalar2=None, op0=ALU.mult),
                      reads=[cv.r, s_.r], writes=[vb.r])
                kbg = kbgr[h].next()
                P.add('pool', lambda e, kbg=kbg, kh=kh, s_=s_, h=h: e.tensor_scalar(out=kbg.t[:], in0=kh, scalar1=s_.t[:, 24 + h:25 + h], scalar2=None, op0=ALU.mult),
                      reads=[cv.r, s_.r], writes=[kbg.r])
                if SUB < 7:
                    continue
                khat = khr[h].next()
                for half in range(2):
                    rows = slice(64 * half, 64 * half + 64)
                    P.add('dve', lambda e, khat=khat, kh=kh, dT=dT, rows=rows, half=half: e.tensor_scalar(
                        out=khat.t[rows, :], in0=kh[rows, :], scalar1=dT.t[rows, 64 * half + 63:64 * half + 64], scalar2=None, op0=ALU.mult),
                        reads=[cv.r, dT.r], writes=[khat.r])
                d.update(Pm=Pm, Tm=Tm, QK=QK, qt=qt_, vb=vb, kbg=kbg, khat=khat, eG=eG)
            if PH < 3:
                continue
            for it in range(5):
                for h in range(NH):
                    d = st[h]
                    Pm, Tm = d['Pm'], d['Tm']
                    pp = self.bank()
                    P.add('pe', lambda e, pp=pp, Pm=Pm: e.matmul(pp.t[:, 0:128], lhsT=Pm.t[:, 128:256], rhs=Pm.t[:, 0:128], start=True, stop=True),
                          reads=[Pm.r], writes=[pp.r])
                    P.add('pe', lambda e, pp=pp, Pm=Pm: e.matmul(pp.t[:, 128:256], lhsT=Pm.t[:, 0:128], rhs=Pm.t[:, 128:256], start=True, stop=True),
                          reads=[Pm.r], writes=[pp.r])
                    P2 = Pr[h].next()
                    P.add('act', lambda e, P2=P2, pp=pp: e.copy(out=P2.t[:], in_=pp.t[:, 0:256]), reads=[pp.r], writes=[P2.r])
                    pq = self.bank()
                    P.add('pe', lambda e, pq=pq, Tm=Tm, P2=P2: e.matmul(pq.t[:, 0:128], lhsT=Tm.t[:, 128:256], rhs=P2.t[:, 0:128], start=True, stop=True),
                          reads=[Tm.r, P2.r], writes=[pq.r])
                    P.add('pe', lambda e, pq=pq, Tm=Tm, P2=P2: e.matmul(pq.t[:, 128:256], lhsT=Tm.t[:, 0:128], rhs=P2.t[:, 128:256], start=True, stop=True),
                          reads=[Tm.r, P2.r], writes=[pq.r])
                    T2 = Tr[h].next()
                    P.add('dve', lambda e, T2=T2, pq=pq, Tm=Tm: e.tensor_tensor(out=T2.t[:], in0=pq.t[:, 0:256], in1=Tm.t[:], op=ALU.add),
                          reads=[pq.r, Tm.r], writes=[T2.r])
                    d['Pm'], d['Tm'] = P2, T2
            if PH < 4:
                continue
            for h in range(NH):
                d = st[h]
                Tm = d['Tm']
                pu = self.bank()
                P.add('pe', lambda e, pu=pu, Tm=Tm, vb=d['vb']: e.matmul(pu.t[:, 0:128], lhsT=Tm.t[:, 128:256], rhs=vb.t[:], start=True, stop=True),
                      reads=[Tm.r, d['vb'].r], writes=[pu.r])
                P.add('pe', lambda e, pu=pu, Tm=Tm, kbg=d['kbg']: e.matmul(pu.t[:, 128:256], lhsT=kbg.t[:], rhs=Tm.t[:, 128:256], start=True, stop=True),
                      reads=[Tm.r, d['kbg'].r], writes=[pu.r])
                u = ur[h].next()
                wT = wTr[h].next()
                P.add('act', lambda e, u=u, pu=pu: e.copy(out=u.t[:], in_=pu.t[:, 0:128]), reads=[pu.r], writes=[u.r])
                P.add('dve', lambda e, wT=wT, pu=pu: e.tensor_copy(out=wT.t[:], in_=pu.t[:, 128:256]), reads=[pu.r], writes=[wT.r])
                d.update(u=u, wT=wT)
                if tt % 4 == 0:
                    oh[h] = oacc[h].next()
            if PH < 5:
                continue
            pos = [self.hbank() for _ in range(NH)]
            for half in range(2):
                rows = slice(64 * half, 64 * half + 64)
                cols = rows
                for h in range(NH):
                    d = st[h]
                    p1 = self.bank()
                    P.add('pe', lambda e, p1=p1, wT=d['wT'], h=h: e.matmul(p1.t[:, 0:128], lhsT=wT.t[:], rhs=S[h].t[:], start=True, stop=True),
                          reads=[d['wT'].r, S[h].r], writes=[p1.r])
                    if half == 0:
                        vn = vnr[h].next()
                        d['vn'] = vn
                    vn = d['vn']
                    P.add('dve', lambda e, vn=vn, u=d['u'], p1=p1, rows=rows: e.tensor_tensor(out=vn.t[rows, :], in0=p1.t[rows, 0:128], in1=u.t[rows, :], op=ALU.subtract),
                          reads=[d['u'].r, p1.r], writes=[vn.r])
                    P.add('dve', lambda e, vn=vn, rows=rows: e.tensor_scalar(out=vn.t[rows, :], in0=vn.t[rows, :], scalar1=-1.0, scalar2=None, op0=ALU.mult),
                          reads=[vn.r], writes=[vn.r])
                    P.add('pe', lambda e, po=pos[h], h=h, qt_=d['qt'], cols=cols: e.matmul(po.t[:, cols], lhsT=S[h].t[:], rhs=qt_.t[:, cols], start=True, stop=False),
                          reads=[S[h].r, d['qt'].r], writes=[pos[h].r])
                    P.add('pe', lambda e, po=pos[h], vn=vn, QK=d['QK'], rows=rows, cols=cols: e.matmul(po.t[:, cols], lhsT=vn.t[rows, :], rhs=QK.t[rows, cols], start=False, stop=True),
                          reads=[vn.r, d['QK'].r], writes=[pos[h].r])
                    p2 = self.bank()
                    P.add('pe', lambda e, p2=p2, khat=d['khat'], vn=vn, rows=rows: e.matmul(p2.t[:, 0:128], lhsT=khat.t[rows, :], rhs=vn.t[rows, :], start=True, stop=True),
                          reads=[d['khat'].r, vn.r], writes=[p2.r])
                    p2s = p2sr[h].next()
                    P.add('act', lambda e, p2s=p2s, p2=p2: e.copy(out=p2s.t[:], in_=p2.t[:, 0:128]), reads=[p2.r], writes=[p2s.r])
                    P.add('dve', lambda e, h=h, eG=d['eG'], p2s=p2s, half=half: e.scalar_tensor_tensor(
                        out=S[h].t[:], in0=S[h].t[:], scalar=eG.t[:, 64 * half + 63:64 * half + 64], in1=p2s.t[:], op0=ALU.mult, op1=ALU.add),
                        reads=[S[h].r, d['eG'].r, p2s.r], writes=[S[h].r])
            for h in range(NH):
                o = oh[h]
                P.add('act', lambda e, o=o, po=pos[h], tt=tt: e.copy(out=o.t[:, (tt % 4) * 128:(tt % 4 + 1) * 128], in_=po.t[:, 0:128]),
                      reads=[pos[h].r], writes=[o.r])
                if tt % 4 == 3:
                    tb = tt // 4
                    gt = gater.next()
                    P.dma('sp', gt.t[:], self.gzT[h * 128:(h + 1) * 128, tb * 512:(tb + 1) * 512], writes=[gt.r])
                    self.headnorm(o, 512, c['hcols'].t[:, l * 8:l * 8 + 1], (gt.t[:], gt.r),
                                  self.mixT[(4 + h) * 128:(5 + h) * 128, tb * 512:(tb + 1) * 512], tmp)


def host_layout(inp, depth):
    f = np.float32
    L = depth
    g = [None] * (2 * L + 1)
    for l in range(L):
        g[2 * l] = inp['attn_norm'][l]
        g[2 * l + 1] = inp['ffn_norm'][l]
    g[2 * L] = inp['final_norm']
    gains = np.concatenate([np.asarray(v, f).reshape(16, 128).T for v in g], axis=1)
    convw = np.stack([np.broadcast_to(np.asarray(inp['gdn_conv_w'][l], f).reshape(1, 4 * 1536), (128, 4 * 1536)) for l in range(L)])
    small = np.zeros((L, 128, 16), f)
    for l in range(L):
        small[l, :, 0:4] = np.asarray(inp['gdn_a_log'][l], f)[None, :]
        small[l, :, 4:8] = np.asarray(inp['gdn_dt_bias'][l], f)[None, :]
    hcols = np.zeros((128, L * 8), f)
    for l in range(L):
        hcols[:, l * 8 + 0] = inp['gdn_out_norm'][l]
        hcols[:, l * 8 + 1] = inp['diff_out_norm'][l]
        hcols[:, l * 8 + 2] = inp['gla_out_norm'][l]
    lamv = np.zeros((L, 128, 256), f)
    for l in range(L):
        row = np.concatenate([inp['diff_lam_q1'][l], inp['diff_lam_k1'][l], inp['diff_lam_q2'][l], inp['diff_lam_k2'][l]])
        lamv[l] = np.asarray(row, f)[None, :]
    glab = np.zeros((64, L * 4), f)
    for l in range(L):
        glab[:, l * 4:(l + 1) * 4] = np.asarray(inp['gla_gate_b'][l], f).reshape(4, 64).T
    return dict(gains=np.ascontiguousarray(gains), convw=np.ascontiguousarray(convw), small=small, hcols=hcols,
                lamv=lamv, w2=np.ascontiguousarray(np.asarray(inp['gla_gate_w2'], f)[:L]), glab=glab)


_CACHE = {}


def run(inp, T, depth, n_cores, debug=(), stages=None, trace=False):
    import time
    t0 = time.time()
    key = (T, depth, tuple(debug), None if stages is None else tuple(stages))
    if key not in _CACHE:
        _CACHE[key] = Builder(T, depth, debug, stages)
    b = _CACHE[key]
    print("build_s", time.time() - t0, flush=True)
    f = np.float32
    lay = host_layout(inp, depth)
    common = dict(lay)
    for k in ('w_in', 'w_out', 'w_gate', 'w_up', 'w_down'):
        common[k] = np.ascontiguousarray(np.asarray(inp[k], f)[:depth])
    B = inp['x'].shape[0]
    in_maps = []
    for ci in range(n_cores):
        m = dict(common)
        m['x'] = np.ascontiguousarray(np.asarray(inp['x'][ci % B], f))
        in_maps.append(m)
    t0 = time.time()
    res = run_bass_kernel_spmd(b.nc, in_maps, core_ids=list(range(n_cores)), **({'trace': True} if trace else {}))
    print("run_s", time.time() - t0, flush=True)
    return res, b


def kernel(**inputs):
    x = np.asarray(inputs['x'])
    B, T, _ = x.shape
    res, b = run(inputs, T, 4, 8)
    out = np.stack([np.asarray(res.results[i]['out']) for i in range(B)], axis=0)
    return out.astype(np.float32)
```

```python
import math
import numpy as np
import concourse.bass as bass
import concourse.mybir as mybir
from concourse.bass_utils import run_bass_kernel_spmd

F32 = mybir.dt.float32
BF16 = mybir.dt.bfloat16
AF = mybir.ActivationFunctionType
ALU = mybir.AluOpType
AX = mybir.AxisListType

COMPUTE = ('pe', 'act', 'dve', 'pool')
ENGS = ('pe', 'act', 'dve', 'pool', 'sp')
N_DMA_SEMS = 48


class Res:
    __slots__ = ('w', 'r', 'excl')

    def __init__(self):
        self.w = None
        self.r = {}
        self.excl = False


class Op:
    __slots__ = ('eng', 'emit', 'deps', 'dma', 'sem', 'val', 'signal', 'prev_same_sem')

    def __init__(self, eng, emit, dma):
        self.eng = eng
        self.emit = emit
        self.dma = dma
        self.deps = []
        self.sem = None
        self.val = 0
        self.signal = dma
        self.prev_same_sem = None


class Buf:
    __slots__ = ('t', 'r')

    def __init__(self, t):
        self.t = t
        self.r = Res()


class BankAlloc:
    def __init__(self, banks, rot, held):
        self.rot = [banks[i] for i in rot]
        self.held = [banks[i] for i in held]
        self.i = 0
        self.j = 0

    def bank(self):
        b = self.rot[self.i % len(self.rot)]
        self.i += 1
        return b

    def hbank(self):
        b = self.held[self.j % len(self.held)]
        self.j += 1
        return b


class Ring:
    def __init__(self, bufs):
        self.bufs = bufs
        self.i = 0

    def next(self):
        b = self.bufs[self.i % len(self.bufs)]
        self.i += 1
        return b


class Prog:
    def __init__(self, nc):
        self.nc = nc
        self.ops = {e: [] for e in ENGS}
        self.n_dma = 0
        self.dma_last = [None] * N_DMA_SEMS
        self.out_dmas = []
        self.pending_dmas = []

    def add(self, eng, emit, reads=(), writes=(), dma=False, is_out=False):
        op = Op(eng, emit, dma)
        deps = {}
        for r in reads:
            if r.w is not None:
                deps[id(r.w)] = r.w
            if r.excl:
                for k, o in r.r.items():
                    if k != eng:
                        deps[id(o)] = o
        for w in writes:
            if w.w is not None:
                deps[id(w.w)] = w.w
            for o in w.r.values():
                deps[id(o)] = o
        for d in deps.values():
            if eng == 'pe' and d.eng == 'pe' and not d.dma and not dma:
                continue
            d.signal = True
            op.deps.append(d)
        for r in reads:
            if dma:
                r.r[('dma', self.n_dma)] = op
            else:
                r.r[eng] = op
        for w in writes:
            w.w = op
            w.r = {}
        if dma:
            slot = self.n_dma % N_DMA_SEMS
            op.prev_same_sem = self.dma_last[slot]
            self.dma_last[slot] = op
            op.sem = slot
            self.n_dma += 1
            self.pending_dmas.append(op)
            if is_out:
                self.out_dmas.append(op)
        self.ops[eng].append(op)
        return op

    def dma(self, eng, out, in_, reads=(), writes=(), is_out=False, **kw):
        return self.add(eng, lambda e: e.dma_start(out=out, in_=in_, **kw), reads, writes,
                        dma=True, is_out=is_out)

    def barrier(self):
        lasts = []
        for e in COMPUTE:
            for op in reversed(self.ops[e]):
                if op.emit is not None and not op.dma:
                    op.signal = True
                    lasts.append(op)
                    break
        pend = list(self.pending_dmas)
        self.pending_dmas = []
        for e in ENGS:
            b = Op(e, None, False)
            b.deps = [d for d in lasts if d.eng != e] + pend
            self.ops[e].append(b)

    def emit_all(self):
        nc = self.nc
        fin = Op('sp', None, False)
        fin.deps = list(self.out_dmas)
        self.ops['sp'].append(fin)
        esem = {e: nc.alloc_semaphore('es_' + e) for e in COMPUTE}
        dsem = [nc.alloc_semaphore('ds_%d' % i) for i in range(N_DMA_SEMS)]
        for e in ENGS:
            cnt = 0
            for op in self.ops[e]:
                if op.dma or op.emit is None:
                    continue
                if op.signal:
                    cnt += 1
                    op.val = cnt
                    op.sem = esem[e]
        for slot in range(N_DMA_SEMS):
            chain = []
            o = self.dma_last[slot]
            while o is not None:
                chain.append(o)
                o = o.prev_same_sem
            chain.reverse()
            v = 0
            for o in chain:
                v += 16
                o.val = v
                o.sem = dsem[slot]
        stats = {}
        with nc.Block() as block:
            def make(e):
                def body(eng):
                    waited = {}
                    nw = 0
                    for op in self.ops[e]:
                        need = {}
                        deps = op.deps
                        if op.dma and op.prev_same_sem is not None:
                            deps = deps + [op.prev_same_sem]
                        for d in deps:
                            k = id(d.sem)
                            if k not in need or need[k][1] < d.val:
                                need[k] = (d.sem, d.val)
                        for k, (s, v) in need.items():
                            if waited.get(k, 0) >= v:
                                continue
                            eng.wait_ge(s, v)
                            waited[k] = v
                            nw += 1
                        if op.emit is None:
                            continue
                        inst = op.emit(eng)
                        if op.signal:
                            inst.then_inc(op.sem, 16 if op.dma else 1)
                    stats[e] = (len(self.ops[e]), nw)
                return body
            block.tensor(make('pe'))
            block.scalar(make('act'))
            block.vector(make('dve'))
            block.gpsimd(make('pool'))
            block.sync(make('sp'))
        return stats


D = 2048
FF = 5632
INC = 6680
NH = 4
EPS = 1e-6
BIG = 30000.0
OFF = dict(sb_q=0, sb_k=512, sb_v=1024, gd_q=1536, gd_k=2048, gd_v=2560, gd_z=3072, gd_b=3584,
           gd_a=3588, df_q=3592, df_k=4104, df_v=4616, gl_q=5128, gl_k=5384, gl_v=5640,
           gl_g=6152, gl_r=6664)


class Builder:
    def __init__(self, T, depth, debug=(), stages=None):
        self.T = T
        self.depth = depth
        self.debug = set(debug)
        self.stages = stages
        self.uid = 0
        nc = self.nc = bass.Bass("TRN2", target_bir_lowering=False)
        self.P = Prog(nc)
        self.build()

    def sb(self, shape, dt, name='t'):
        self.uid += 1
        return Buf(self.nc.alloc_sbuf_tensor('%s_%d' % (name, self.uid), shape, dt))

    def ring(self, n, shape, dt, name='r'):
        return Ring([self.sb(shape, dt, name) for _ in range(n)])

    def dram(self, name, shape, dt):
        kind = "ExternalOutput" if name in self.debug else "Internal"
        return self.nc.dram_tensor(name, shape, dt, kind=kind).ap()

    def inp(self, name, shape, dt=F32):
        return self.nc.dram_tensor(name, shape, dt, kind="ExternalInput").ap()

    def bank(self):
        return self.cur.bank()

    def hbank(self):
        return self.cur.hbank()

    def interleave(self, gens_allocs):
        live = list(gens_allocs)
        while live:
            nxt = []
            for g, a in live:
                self.cur = a
                try:
                    next(g)
                    nxt.append((g, a))
                except StopIteration:
                    pass
            live = nxt
        self.cur = self.defalloc

    def want(self, s):
        return self.stages is None or s in self.stages

    def build(self):
        nc, P, T, L = self.nc, self.P, self.T, self.depth
        self.x_in = self.inp("x", [T, D])
        self.w_in = self.inp("w_in", [L, D, INC])
        self.w_out = self.inp("w_out", [L, D, D])
        self.w_gate = self.inp("w_gate", [L, D, FF])
        self.w_up = self.inp("w_up", [L, D, FF])
        self.w_down = self.inp("w_down", [L, FF, D])
        self.gains_in = self.inp("gains", [128, (2 * L + 1) * 16])
        self.convw_in = self.inp("convw", [L, 128, 4 * 1536])
        self.small_in = self.inp("small", [L, 128, 16])
        self.hcols_in = self.inp("hcols", [128, L * 8])
        self.lam_in = self.inp("lamv", [L, 128, 256])
        self.w2_in = self.inp("w2", [L, 16, 256])
        self.glab_in = self.inp("glab", [64, L * 4])
        self.out = self.nc.dram_tensor("out", [T, D], F32, kind="ExternalOutput").ap()

        self.xT = self.dram("xT", [D, T], F32)
        self.hT = self.dram("hT", [D, T], BF16)
        self.mixT = self.dram("mixT", [D, T], BF16)
        self.gT = self.dram("gT", [FF, T], BF16)
        self.qsbT = self.dram("qsbT", [512, T], BF16)
        self.ksbT = self.dram("ksbT", [512, T], BF16)
        self.vsb = self.dram("vsb", [T, 512], BF16)
        self.gqkv = self.dram("gqkv", [T, 1536], F32)
        self.gzT = self.dram("gzT", [512, T], F32)
        self.gba = self.dram("gba", [T, 8], F32)
        self.qdfT = self.dram("qdfT", [512, T], BF16)
        self.kdfT = self.dram("kdfT", [512, T], BF16)
        self.vdf = self.dram("vdf", [T, 512], BF16)
        self.qglT = self.dram("qglT", [256, T], F32)
        self.kglT = self.dram("kglT", [256, T], F32)
        self.vgl = self.dram("vgl", [T, 512], F32)
        self.ggT = self.dram("ggT", [512, T], F32)
        self.grT = self.dram("grT", [16, T], F32)

        self.banks = [Buf(nc.alloc_psum_tensor('bank%d' % i, [128, 512], F32)) for i in range(8)]
        self.defalloc = BankAlloc(self.banks, [0, 1, 2, 3], [4, 5, 6, 7])
        self.cur = self.defalloc
        for b_ in self.banks:
            b_.r.excl = True

        self.consts()
        P.barrier()
        if self.want('pre'):
            with nc.reset_on_exit():
                self.stage_transpose_in()
            P.barrier()
        for l in range(L):
            self.layer(l)
        if self.want('post'):
            with nc.reset_on_exit():
                self.stage_final()
        self.stats = P.emit_all()

    def consts(self):
        nc, P, L = self.nc, self.P, self.depth
        c = self.c = {}

        def mk(name, shape, dt):
            c[name] = self.sb(shape, dt, name)
            return c[name]
        ones = mk('ones', [128, 512], F32)
        P.add('pool', lambda e: e.memset(ones.t[:], 1.0), writes=[ones.r])
        onesb = mk('onesb', [128, 128], BF16)
        P.add('pool', lambda e: e.memset(onesb.t[:], 1.0), writes=[onesb.r])
        nonesb = mk('nonesb', [128, 128], BF16)
        P.add('pool', lambda e: e.memset(nonesb.t[:], -1.0), writes=[nonesb.r])
        nones = mk('nones', [128, 128], F32)
        P.add('pool', lambda e: e.memset(nones.t[:], -1.0), writes=[nones.r])
        zer = mk('zer', [128, 512], F32)
        P.add('pool', lambda e: e.memset(zer.t[:], 0.0), writes=[zer.r])
        cst = mk('cst', [128, 4], F32)
        P.add('pool', lambda e: e.memset(cst.t[:, 0:1], 1.0), writes=[cst.r])
        P.add('pool', lambda e: e.memset(cst.t[:, 1:2], EPS), writes=[cst.r])
        P.add('pool', lambda e: e.memset(cst.t[:, 2:3], 0.0), writes=[cst.r])

        def sel(name, src, pattern, op, fill, base, cm, shape=(128, 128), dt=F32):
            b = mk(name, list(shape), dt)
            n = shape[1]
            P.add('pool', lambda e: e.affine_select(out=b.t[:], in_=src.t[:, :n], pattern=pattern,
                                                    compare_op=op, fill=fill, base=base,
                                                    channel_multiplier=cm),
                  reads=[src.r], writes=[b.r])
            return b
        ident = sel('ident', ones, [[-1, 128]], ALU.is_equal, 0.0, 0, 1)
        ustr_f = sel('ustr_f', ones, [[-1, 128]], ALU.is_gt, 0.0, 0, 1)
        nustr = mk('nustr', [128, 128], BF16)
        P.add('dve', lambda e: e.tensor_scalar(out=nustr.t[:], in0=ustr_f.t[:], scalar1=-1.0,
                                               scalar2=None, op0=ALU.mult),
              reads=[ustr_f.r], writes=[nustr.r])
        for j in range(4):
            a = sel('mstr%d' % j, ones, [[1, 512]], ALU.is_gt, 0.0, -128 * j, -1, shape=(128, 512))
            b = mk('mstrb%d' % j, [128, 512], BF16)
            P.add('dve', lambda e, a=a, b=b: e.tensor_copy(out=b.t[:], in_=a.t[:]), reads=[a.r], writes=[b.r])
            a2 = sel('minc%d' % j, ones, [[1, 512]], ALU.is_ge, 0.0, -128 * j, -1, shape=(128, 512))
            b2 = mk('mincb%d' % j, [128, 512], BF16)
            P.add('dve', lambda e, a=a2, b=b2: e.tensor_copy(out=b.t[:], in_=a.t[:]), reads=[a2.r], writes=[b2.r])
        bd = mk('bd', [128, 128], F32)
        P.add('pool', lambda e: e.memset(bd.t[:], 0.0), writes=[bd.r])
        P.add('pool', lambda e: e.memset(bd.t[0:64, 0:64], 1.0), writes=[bd.r])
        P.add('pool', lambda e: e.memset(bd.t[64:128, 64:128], 1.0), writes=[bd.r])
        upi = sel('upi', bd, [[1, 128]], ALU.is_ge, 0.0, 0, -1)
        lows = sel('lows', bd, [[-1, 128]], ALU.is_gt, 0.0, 0, 1)
        c['upi'] = upi
        pms = mk('pms', [128, 128], F32)
        P.add('dve', lambda e: e.tensor_scalar(out=pms.t[:], in0=lows.t[:], scalar1=BIG, scalar2=-BIG,
                                               op0=ALU.mult, op1=ALU.add), reads=[lows.r], writes=[pms.r])
        nmu = mk('nmu', [128, 128], F32)
        P.add('dve', lambda e: e.tensor_scalar(out=nmu.t[:], in0=upi.t[:], scalar1=-BIG, scalar2=BIG,
                                               op0=ALU.mult, op1=ALU.add), reads=[upi.r], writes=[nmu.r])
        cm = mk('cmask', [128, 512], F32)
        P.add('pool', lambda e: e.memset(cm.t[:], 1.0), writes=[cm.r])
        for k in range(8):
            P.add('pool', lambda e, k=k: e.memset(cm.t[:, 64 * k:64 * k + 1], 0.0), writes=[cm.r])
        gains = mk('gains', [128, (2 * L + 1) * 16], F32)
        P.dma('sp', gains.t[:], self.gains_in, writes=[gains.r])
        hcols = mk('hcols', [128, L * 8], F32)
        P.dma('sp', hcols.t[:], self.hcols_in, writes=[hcols.r])
        glab = mk('glab', [64, L * 4], F32)
        P.dma('sp', glab.t[:], self.glab_in, writes=[glab.r])
        nglab = mk('nglab', [64, L * 4], F32)
        P.add('dve', lambda e: e.tensor_scalar(out=nglab.t[:], in0=glab.t[:], scalar1=-1.0, scalar2=None,
                                               op0=ALU.mult), reads=[glab.r], writes=[nglab.r])

    def stage_transpose_in(self):
        nc, P, T, c = self.nc, self.P, self.T, self.c
        xr = self.ring(2, [128, D], F32, 'xin')
        st = self.ring(2, [128, 16, 512], F32, 'xst')
        xTv = self.xT.rearrange("(c p) t -> p c t", p=128)
        for tb in range(T // 512):
            s = st.next()
            for j in range(4):
                xb = xr.next()
                r0 = tb * 512 + j * 128
                P.dma('sp', xb.t[:], self.x_in[r0:r0 + 128, :], writes=[xb.r])
                for cg in range(4):
                    bk = self.bank()
                    for q in range(4):
                        cc = cg * 4 + q
                        P.add('pe', lambda e, bk=bk, q=q, xb=xb, cc=cc: e.transpose(
                            out=bk.t[:, q * 128:(q + 1) * 128], in_=xb.t[:, cc * 128:(cc + 1) * 128],
                            identity=c['ident'].t[:]), reads=[xb.r, c['ident'].r], writes=[bk.r])
                    eng = 'dve' if cg % 2 == 0 else 'act'
                    if eng == 'dve':
                        P.add('dve', lambda e, bk=bk, s=s, cg=cg, j=j: e.tensor_copy(
                            out=s.t[:, cg * 4:cg * 4 + 4, j * 128:(j + 1) * 128],
                            in_=bk.t[:, :].rearrange("p (q f) -> p q f", q=4)), reads=[bk.r], writes=[s.r])
                    else:
                        P.add('act', lambda e, bk=bk, s=s, cg=cg, j=j: e.copy(
                            out=s.t[:, cg * 4:cg * 4 + 4, j * 128:(j + 1) * 128],
                            in_=bk.t[:, :].rearrange("p (q f) -> p q f", q=4)), reads=[bk.r], writes=[s.r])
            P.dma('sp', xTv[:, :, tb * 512:(tb + 1) * 512], s.t[:], reads=[s.r])

    def norm_block(self, xt, sq, h_out_fn, gi, rs):
        P, c = self.P, self.c
        P.add('act', lambda e: e.activation(out=sq.t[:], in_=xt.t[:], func=AF.Square), reads=[xt.r], writes=[sq.r])
        bk = self.bank()
        for cc in range(16):
            P.add('pe', lambda e, cc=cc: e.matmul(bk.t[:], lhsT=c['onesb'].t[:], rhs=sq.t[:, cc, :],
                                                  start=(cc == 0), stop=(cc == 15)),
                  reads=[sq.r, c['onesb'].r], writes=[bk.r])
        self.rstd_from(bk, rs, 1.0 / D)
        for cc in range(16):
            o, orr = h_out_fn(cc)
            P.add('dve', lambda e, cc=cc, o=o: e.scalar_tensor_tensor(
                out=o, in0=xt.t[:, cc, :], scalar=c['gains'].t[:, gi * 16 + cc:gi * 16 + cc + 1], in1=rs.t[:],
                op0=ALU.mult, op1=ALU.mult), reads=[xt.r, rs.r, c['gains'].r], writes=[orr])

    def rstd_from(self, bk, rs, inv_n, n=512):
        P, c = self.P, self.c
        P.add('dve', lambda e: e.tensor_scalar(out=rs.t[:, :n], in0=bk.t[:, :n], scalar1=inv_n, scalar2=EPS,
                                               op0=ALU.mult, op1=ALU.add), reads=[bk.r], writes=[rs.r])
        P.add('act', lambda e: e.activation(out=rs.t[:, :n], in_=rs.t[:, :n], func=AF.Sqrt), reads=[rs.r], writes=[rs.r])
        P.add('dve', lambda e: e.reciprocal(out=rs.t[:, :n], in_=rs.t[:, :n]), reads=[rs.r], writes=[rs.r])

    def stage_norm(self, gi):
        nc, P, T = self.nc, self.P, self.T
        xr = self.ring(2, [128, 16, 512], F32, 'nx')
        sqr = self.ring(1, [128, 16, 512], BF16, 'nsq')
        hr = self.ring(2, [128, 16, 512], BF16, 'nh')
        rsr = self.ring(2, [128, 512], F32, 'nrs')
        xTv = self.xT.rearrange("(c p) t -> p c t", p=128)
        hTv = self.hT.rearrange("(c p) t -> p c t", p=128)
        for tb in range(T // 512):
            xt = xr.next()
            P.dma('sp', xt.t[:], xTv[:, :, tb * 512:(tb + 1) * 512], writes=[xt.r])
            h = hr.next()
            self.norm_block(xt, sqr.next(), lambda cc, h=h: (h.t[:, cc, :], h.r), gi, rsr.next())
            P.dma('sp', hTv[:, :, tb * 512:(tb + 1) * 512], h.t[:], reads=[h.r])

    def stage_final(self):
        nc, P, T, c = self.nc, self.P, self.T, self.c
        gi = 2 * self.depth
        xr = self.ring(2, [128, 16, 512], F32, 'fx')
        sqr = self.ring(1, [128, 16, 512], BF16, 'fsq')
        hr = self.ring(1, [128, 16, 512], F32, 'fh')
        rsr = self.ring(2, [128, 512], F32, 'frs')
        orr = self.ring(2, [128, D], F32, 'fo')
        xTv = self.xT.rearrange("(c p) t -> p c t", p=128)
        for tb in range(T // 512):
            xt = xr.next()
            P.dma('sp', xt.t[:], xTv[:, :, tb * 512:(tb + 1) * 512], writes=[xt.r])
            h = hr.next()
            self.norm_block(xt, sqr.next(), lambda cc, h=h: (h.t[:, cc, :], h.r), gi, rsr.next())
            for j in range(4):
                ob = orr.next()
                for cg in range(4):
                    bk = self.bank()
                    for q in range(4):
                        cc = cg * 4 + q
                        P.add('pe', lambda e, bk=bk, q=q, h=h, cc=cc, j=j: e.transpose(
                            out=bk.t[:, q * 128:(q + 1) * 128], in_=h.t[:, cc, j * 128:(j + 1) * 128],
                            identity=c['ident'].t[:]), reads=[h.r, c['ident'].r], writes=[bk.r])
                    if cg % 2 == 0:
                        P.add('dve', lambda e, bk=bk, ob=ob, cg=cg: e.tensor_copy(
                            out=ob.t[:, cg * 512:(cg + 1) * 512], in_=bk.t[:]), reads=[bk.r], writes=[ob.r])
                    else:
                        P.add('act', lambda e, bk=bk, ob=ob, cg=cg: e.copy(
                            out=ob.t[:, cg * 512:(cg + 1) * 512], in_=bk.t[:]), reads=[bk.r], writes=[ob.r])
                r0 = tb * 512 + j * 128
                P.dma('sp', self.out[r0:r0 + 128, :], ob.t[:], reads=[ob.r], is_out=True)

    def dense(self, inT, KC, groups):
        nc, P, T = self.nc, self.P, self.T
        TB = min(T, 1024)
        pwmax = 512 if KC <= 16 else 256
        nin = 2 if KC <= 16 else 1
        xin = self.ring(nin, [128, KC, TB], BF16, 'din')
        nw = max(len(g['ws']) for g in groups)
        KG = 4 if KC % 4 == 0 else 1
        wr = [Ring([[self.sb([128, KG, pwmax], BF16, 'dw') for _ in range(KC // KG)] for _ in range(2)])
              for _ in range(nw)]
        inTv = inT.rearrange("(c p) t -> p c t", p=128)
        for tb in range(T // TB):
            xb = xin.next()
            for k0 in range(0, KC, 8):
                k1 = min(KC, k0 + 8)
                P.dma('sp', xb.t[:, k0:k1, :], inTv[:, k0:k1, tb * TB:(tb + 1) * TB], writes=[xb.r])
            for g in groups:
                ncols = g['ncols']
                for c0 in range(0, ncols, pwmax):
                    pw = min(pwmax, ncols - c0)
                    wbs = []
                    for wi, (W, col0) in enumerate(g['ws']):
                        wb = wr[wi].next()
                        Wv = W[:, col0 + c0:col0 + c0 + pw].rearrange("(c p) m -> p c m", p=128)
                        for kg in range(KC // KG):
                            P.dma('pool', wb[kg].t[:, :, :pw], Wv[:, kg * KG:(kg + 1) * KG, :], writes=[wb[kg].r])
                        wbs.append(wb)
                    if g['mode'] == 'F':
                        for m0 in range(0, pw, 128):
                            mw = min(128, pw - m0)
                            for n0 in range(0, TB, 512):
                                bks = []
                                for wb in wbs:
                                    bk = self.bank()
                                    for kc in range(KC):
                                        wk = wb[kc // KG]
                                        P.add('pe', lambda e, bk=bk, wk=wk, kc=kc, m0=m0, mw=mw, n0=n0, xb=xb: e.matmul(
                                            bk.t[:mw, :], lhsT=wk.t[:, kc % KG, m0:m0 + mw], rhs=xb.t[:, kc, n0:n0 + 512],
                                            start=(kc == 0), stop=(kc == KC - 1)), reads=[wk.r, xb.r], writes=[bk.r])
                                    bks.append(bk)
                                g['epi'](bks, c0 + m0, mw, tb * TB + n0)
                    else:
                        wb = wbs[0]
                        for t0 in range(0, TB, 128):
                            bk = self.bank()
                            for kc in range(KC):
                                wk = wb[kc // KG]
                                P.add('pe', lambda e, bk=bk, wk=wk, kc=kc, pw=pw, t0=t0, xb=xb: e.matmul(
                                    bk.t[:, :pw], lhsT=xb.t[:, kc, t0:t0 + 128], rhs=wk.t[:, kc % KG, :pw],
                                    start=(kc == 0), stop=(kc == KC - 1)), reads=[wk.r, xb.r], writes=[bk.r])
                            g['epi']([bk], c0, pw, tb * TB + t0)

    def epi_F_store(self, dst, dt, func=AF.Copy, scale=1.0):
        P = self.P
        st = self.strings[dt]

        def epi(bks, cofs, mw, t0):
            s = st.next()
            P.add('act', lambda e: e.activation(out=s.t[:mw, :], in_=bks[0].t[:mw, :], func=func, scale=scale),
                  reads=[bks[0].r], writes=[s.r])
            P.dma('sp', dst[cofs:cofs + mw, t0:t0 + 512], s.t[:mw, :], reads=[s.r])
        return epi

    def epi_T_store(self, dst, dt):
        P = self.P
        st = self.strings[dt]

        def epi(bks, cofs, pw, t0):
            s = st.next()
            P.add('dve', lambda e: e.tensor_copy(out=s.t[:, :pw], in_=bks[0].t[:, :pw]), reads=[bks[0].r], writes=[s.r])
            P.dma('sp', dst[t0:t0 + 128, cofs:cofs + pw], s.t[:, :pw], reads=[s.r])
        return epi

    def epi_resid(self):
        P = self.P
        xr = self.ring(3, [128, 512], F32, 'erx')

        def epi(bks, cofs, mw, t0):
            x = xr.next()
            P.dma('sp', x.t[:], self.xT[cofs:cofs + 128, t0:t0 + 512], writes=[x.r])
            P.add('dve', lambda e: e.tensor_tensor(out=x.t[:], in0=bks[0].t[:], in1=x.t[:], op=ALU.add),
                  reads=[bks[0].r, x.r], writes=[x.r])
            P.dma('sp', self.xT[cofs:cofs + 128, t0:t0 + 512], x.t[:], reads=[x.r])
        return epi

    def epi_swiglu(self):
        P = self.P
        sr = self.ring(2, [128, 512], F32, 'esw')
        gr = self.ring(3, [128, 512], BF16, 'esg')

        def epi(bks, cofs, mw, t0):
            s = sr.next()
            g = gr.next()
            P.add('act', lambda e: e.activation(out=s.t[:], in_=bks[0].t[:], func=AF.Silu), reads=[bks[0].r], writes=[s.r])
            P.add('dve', lambda e: e.tensor_tensor(out=g.t[:], in0=bks[1].t[:], in1=s.t[:], op=ALU.mult),
                  reads=[bks[1].r, s.r], writes=[g.r])
            P.dma('sp', self.gT[cofs:cofs + 128, t0:t0 + 512], g.t[:], reads=[g.r])
        return epi

    def headnorm(self, o, n, gain_col, gate, dst, tmp, greads=()):
        P, c = self.P, self.c
        sq, rs, y, yb = tmp
        P.add('act', lambda e: e.activation(out=sq.t[:, :n], in_=o.t[:, :n], func=AF.Square), reads=[o.r], writes=[sq.r])
        bk = self.bank()
        P.add('pe', lambda e: e.matmul(bk.t[:, :n], lhsT=c['onesb'].t[:], rhs=sq.t[:, :n], start=True, stop=True),
              reads=[sq.r, c['onesb'].r], writes=[bk.r])
        self.rstd_from(bk, rs, 1.0 / 128, n)
        if gate is None:
            P.add('dve', lambda e: e.scalar_tensor_tensor(out=yb.t[:, :n], in0=o.t[:, :n], scalar=gain_col, in1=rs.t[:, :n],
                                                          op0=ALU.mult, op1=ALU.mult), reads=[o.r, rs.r, c['hcols'].r] + list(greads), writes=[yb.r])
        else:
            P.add('dve', lambda e: e.scalar_tensor_tensor(out=y.t[:, :n], in0=o.t[:, :n], scalar=gain_col, in1=rs.t[:, :n],
                                                          op0=ALU.mult, op1=ALU.mult), reads=[o.r, rs.r, c['hcols'].r], writes=[y.r])
            P.add('pool', lambda e: e.tensor_tensor(out=yb.t[:, :n], in0=y.t[:, :n], in1=gate[0], op=ALU.mult),
                  reads=[y.r, gate[1]], writes=[yb.r])
        P.dma('sp', dst, yb.t[:, :n], reads=[yb.r])

    def hn_tmp(self):
        return (self.sb([128, 512], BF16, 'hsq'), self.sb([128, 512], F32, 'hrs'),
                self.sb([128, 512], F32, 'hy'), self.sb([128, 512], BF16, 'hyb'))

    def layer(self, l):
        nc, P, T = self.nc, self.P, self.T
        lam_init = 0.8 - 0.6 * math.exp(-0.3 * l)
        if self.want('norm1'):
            with nc.reset_on_exit():
                self.stage_norm(2 * l)
            P.barrier()
        if self.want('inproj'):
            with nc.reset_on_exit():
                W = self.w_in[l]
                g = []
                self.strings = {F32: self.ring(3, [128, 512], F32, 'stf'), BF16: self.ring(3, [128, 512], BF16, 'stb')}

                def G(mode, name, ncols, epi):
                    g.append(dict(mode=mode, ws=[(W, OFF[name])], ncols=ncols, epi=epi))
                G('F', 'sb_q', 512, self.epi_F_store(self.qsbT, BF16, scale=128 ** -0.5))
                G('F', 'sb_k', 512, self.epi_F_store(self.ksbT, BF16))
                G('T', 'sb_v', 512, self.epi_T_store(self.vsb, BF16))
                G('T', 'gd_q', 1536, self.epi_T_store(self.gqkv, F32))
                G('F', 'gd_z', 512, self.epi_F_store(self.gzT, F32, func=AF.Silu))
                G('T', 'gd_b', 8, self.epi_T_store(self.gba, F32))
                G('F', 'df_q', 512, self.epi_F_store(self.qdfT, BF16, scale=64 ** -0.5))
                G('F', 'df_k', 512, self.epi_F_store(self.kdfT, BF16))
                G('T', 'df_v', 512, self.epi_T_store(self.vdf, BF16))
                G('F', 'gl_q', 256, self.epi_F_store(self.qglT, F32, scale=64 ** -0.5))
                G('F', 'gl_k', 256, self.epi_F_store(self.kglT, F32))
                G('T', 'gl_v', 512, self.epi_T_store(self.vgl, F32))
                G('F', 'gl_g', 512, self.epi_F_store(self.ggT, F32, func=AF.Silu))
                G('F', 'gl_r', 16, self.epi_F_store(self.grT, F32))
                self.dense(self.hT, 16, g)
            P.barrier()
        if False:
            with nc.reset_on_exit():
                a1 = BankAlloc(self.banks, [0, 1], [4])
                a2 = BankAlloc(self.banks, [2, 3], [5, 6])
                self.interleave([(self.stage_sb(), a1), (self.stage_diff(l, lam_init), a2)])
            P.barrier()
        else:
            if self.want('sb'):
                with nc.reset_on_exit():
                    self.interleave([(self.stage_sb(), self.defalloc)])
                P.barrier()
            if self.want('diff'):
                with nc.reset_on_exit():
                    self.interleave([(self.stage_diff(l, lam_init), BankAlloc(self.banks, [0, 1, 2, 3], [4, 5]))])
                P.barrier()
        if self.want('gla'):
            with nc.reset_on_exit():
                self.stage_gla(l)
            P.barrier()
        if self.want('gdn'):
            with nc.reset_on_exit():
                self.stage_gdn(l)
            P.barrier()
        if self.want('outproj'):
            with nc.reset_on_exit():
                self.dense(self.mixT, 16, [dict(mode='F', ws=[(self.w_out[l], 0)], ncols=D, epi=self.epi_resid())])
            P.barrier()
        if self.want('ffn'):
            with nc.reset_on_exit():
                self.stage_norm(2 * l + 1)
            P.barrier()
            with nc.reset_on_exit():
                self.dense(self.hT, 16, [dict(mode='F', ws=[(self.w_gate[l], 0), (self.w_up[l], 0)], ncols=FF,
                                              epi=self.epi_swiglu())])
            P.barrier()
            with nc.reset_on_exit():
                self.dense(self.gT, 44, [dict(mode='F', ws=[(self.w_down[l], 0)], ncols=D, epi=self.epi_resid())])
            P.barrier()

    def stage_sb(self):
        nc, P, T, c = self.nc, self.P, self.T, self.c
        NT = T // 128
        kT = self.ring(2, [128, T], BF16, 'sbk')
        vt = self.ring(2, [128, NT, 128], BF16, 'sbv')
        qr = self.ring(2, [128, 512], BF16, 'sbq')
        Er = self.ring(3, [128, 512], F32, 'sbE')
        SPr = self.ring(4, [128, 512], BF16, 'sbSP')
        t1r = self.ring(3, [128, 512], F32, 'sbt1')
        t2r = self.ring(3, [128, 512], F32, 'sbt2')
        attr = self.ring(5, [128, 512], BF16, 'sbatt')
        Lf = self.ring(2, [128, 512], F32, 'sbLf')
        Lb = self.ring(3, [128, 512], BF16, 'sbLb')
        zb = Ring([self.banks[0], self.banks[1], self.banks[2]])
        tb = Ring([self.banks[3], self.banks[5]])
        pob = Ring([self.banks[4], self.banks[6]])
        orr = self.ring(2, [128, 512], BF16, 'sbo')
        vv = self.vsb.rearrange("(n p) c -> p n c", p=128)
        for h in range(NH):
            k = kT.next()
            P.dma('sp', k.t[:], self.ksbT[h * 128:(h + 1) * 128, :], writes=[k.r])
            v = vt.next()
            for n0_ in range(0, NT, 8):
                P.dma('sp', v.t[:, n0_:min(NT, n0_ + 8), :], vv[:, n0_:min(NT, n0_ + 8), h * 128:(h + 1) * 128], writes=[v.r])
            for qt in range(T // 512):
                q = qr.next()
                P.dma('sp', q.t[:], self.qsbT[h * 128:(h + 1) * 128, qt * 512:(qt + 1) * 512], writes=[q.r])
                lf = Lf.next()
                lbs = [Lb.next()]
                P.add('pool', lambda e, lf=lf: e.memset(lf.t[:], 0.0), writes=[lf.r])
                P.add('pool', lambda e, lb=lbs[0]: e.memset(lb.t[:], 0.0), writes=[lbs[0].r])
                po = pob.next()
                kmax = 4 * (qt + 1) - 1
                kbs = list(range(kmax, -1, -1))
                nb = len(kbs)

                def phA(kb, k=k, q=q, qt=qt):
                    jd = kb - 4 * qt
                    pz = zb.next()
                    P.add('pe', lambda e, pz=pz, k=k, kb=kb, q=q: e.matmul(
                        pz.t[:], lhsT=k.t[:, kb * 128:(kb + 1) * 128], rhs=q.t[:], start=True, stop=True),
                        reads=[k.r, q.r], writes=[pz.r])
                    E = Er.next()
                    SP = SPr.next()
                    P.add('act', lambda e, E=E, pz=pz: e.activation(out=E.t[:], in_=pz.t[:], func=AF.Exp), reads=[pz.r], writes=[E.r])
                    P.add('act', lambda e, E=E, SP=SP: e.activation(out=SP.t[:], in_=E.t[:], func=AF.Ln, bias=c['cst'].t[:, 0:1]),
                          reads=[E.r, c['cst'].r], writes=[SP.r])
                    if jd >= 0:
                        m = c['mstrb%d' % jd]
                        P.add('dve', lambda e, SP=SP, m=m: e.tensor_tensor(out=SP.t[:], in0=SP.t[:], in1=m.t[:], op=ALU.mult),
                              reads=[SP.r, m.r], writes=[SP.r])
                    return dict(kb=kb, jd=jd, pz=pz, SP=SP)

                def phB(b, lf=lf, lbs=lbs):
                    pz, SP, jd, kb = b['pz'], b['SP'], b['jd'], b['kb']
                    lb = lbs[0]
                    pt = tb.next()
                    P.add('pe', lambda e, pt=pt, SP=SP: e.matmul(pt.t[:], lhsT=c['nustr'].t[:], rhs=SP.t[:], start=True, stop=False),
                          reads=[SP.r, c['nustr'].r], writes=[pt.r])
                    P.add('pe', lambda e, pt=pt, lb=lb: e.matmul(pt.t[:], lhsT=c['nonesb'].t[:], rhs=lb.t[:], start=False, stop=True),
                          reads=[lb.r, c['nonesb'].r], writes=[pt.r])
                    t1 = t1r.next()
                    P.add('dve', lambda e, t1=t1, pz=pz, SP=SP: e.tensor_tensor(out=t1.t[:], in0=pz.t[:], in1=SP.t[:], op=ALU.subtract),
                          reads=[pz.r, SP.r], writes=[t1.r])
                    t2 = t2r.next()
                    P.add('dve', lambda e, t2=t2, pt=pt, t1=t1: e.tensor_tensor(out=t2.t[:], in0=pt.t[:], in1=t1.t[:], op=ALU.add),
                          reads=[pt.r, t1.r], writes=[t2.r])
                    att = attr.next()
                    P.add('act', lambda e, att=att, t2=t2: e.activation(out=att.t[:], in_=t2.t[:], func=AF.Exp), reads=[t2.r], writes=[att.r])
                    if jd >= 0:
                        m = c['mstrb%d' % jd]
                        P.add('pool', lambda e, att=att, m=m: e.tensor_tensor(out=att.t[:], in0=att.t[:], in1=m.t[:], op=ALU.mult),
                              reads=[att.r, m.r], writes=[att.r])
                    if kb > 0:
                        P.add('pool', lambda e, lf=lf, SP=SP: e.tensor_tensor(out=lf.t[:], in0=lf.t[:], in1=SP.t[:], op=ALU.add),
                              reads=[lf.r, SP.r], writes=[lf.r])
                        lb2 = Lb.next()
                        P.add('pool', lambda e, lf=lf, lb2=lb2: e.tensor_copy(out=lb2.t[:], in_=lf.t[:]), reads=[lf.r], writes=[lb2.r])
                        lbs[0] = lb2
                    b['att'] = att

                def phC(b, po=po, v=v, kmax=kmax):
                    kb, att = b['kb'], b['att']
                    P.add('pe', lambda e, po=po, v=v, kb=kb, att=att, kmax=kmax: e.matmul(
                        po.t[:], lhsT=v.t[:, kb, :], rhs=att.t[:], start=(kb == kmax), stop=(kb == 0)),
                        reads=[v.r, att.r], writes=[po.r])

                blk = [None] * nb
                blk[0] = phA(kbs[0])
                for i in range(nb):
                    if i + 1 < nb:
                        blk[i + 1] = phA(kbs[i + 1])
                    phB(blk[i])
                    if i >= 2:
                        phC(blk[i - 2])
                    yield
                for i in range(max(0, nb - 2), nb):
                    phC(blk[i])
                o = orr.next()
                P.add('act', lambda e, o=o, po=po: e.copy(out=o.t[:], in_=po.t[:]), reads=[po.r], writes=[o.r])
                P.dma('sp', self.mixT[h * 128:(h + 1) * 128, qt * 512:(qt + 1) * 512], o.t[:], reads=[o.r])

    def stage_diff(self, l, lam_init):
        nc, P, T, c = self.nc, self.P, self.T, self.c
        NT = T // 128
        lv = self.sb([128, 256], F32, 'lv')
        P.dma('sp', lv.t[:], self.lam_in[l], writes=[lv.r])
        pr = self.sb([128, 128], F32, 'lpr')
        dots = self.sb([128, 2], F32, 'ldots')
        nlam = self.sb([128, 1], F32, 'nlam')
        P.add('dve', lambda e: e.tensor_tensor(out=pr.t[:, 0:64], in0=lv.t[:, 0:64], in1=lv.t[:, 64:128], op=ALU.mult), reads=[lv.r], writes=[pr.r])
        P.add('dve', lambda e: e.tensor_tensor(out=pr.t[:, 64:128], in0=lv.t[:, 128:192], in1=lv.t[:, 192:256], op=ALU.mult), reads=[lv.r, pr.r], writes=[pr.r])
        P.add('dve', lambda e: e.tensor_reduce(out=dots.t[:], in_=pr.t[:].rearrange("p (a b) -> p a b", a=2), axis=AX.X, op=ALU.add),
              reads=[pr.r], writes=[dots.r])
        P.add('act', lambda e: e.activation(out=dots.t[:], in_=dots.t[:], func=AF.Exp), reads=[dots.r], writes=[dots.r])
        P.add('dve', lambda e: e.tensor_tensor(out=nlam.t[:], in0=dots.t[:, 1:2], in1=dots.t[:, 0:1], op=ALU.subtract), reads=[dots.r], writes=[nlam.r])
        P.add('dve', lambda e: e.tensor_scalar(out=nlam.t[:], in0=nlam.t[:], scalar1=-lam_init, scalar2=None, op0=ALU.add), reads=[nlam.r], writes=[nlam.r])
        gcol = self.sb([128, 1], F32, 'dgc')
        P.add('dve', lambda e: e.tensor_scalar(out=gcol.t[:], in0=c['hcols'].t[:, l * 8 + 1:l * 8 + 2], scalar1=1.0 - lam_init, scalar2=None,
                                               op0=ALU.mult), reads=[c['hcols'].r], writes=[gcol.r])
        kT = self.ring(2, [128, T], BF16, 'dfk')
        vt = self.ring(2, [128, NT, 128], BF16, 'dfv')
        qr = self.ring(2, [128, 512], BF16, 'dfq')
        attr = self.ring(4, [128, 512], BF16, 'dfatt')
        rr = self.ring(2, [128, 512], F32, 'dfr')
        o0r = self.ring(2, [128, 512], F32, 'dfo0')
        o1r = self.ring(2, [128, 512], F32, 'dfo1')
        tmp = self.hn_tmp()
        vv = self.vdf.rearrange("(n p) c -> p n c", p=128)
        for h in range(NH):
            k = kT.next()
            P.dma('sp', k.t[:], self.kdfT[h * 128:(h + 1) * 128, :], writes=[k.r])
            v = vt.next()
            for n0_ in range(0, NT, 8):
                P.dma('sp', v.t[:, n0_:min(NT, n0_ + 8), :], vv[:, n0_:min(NT, n0_ + 8), h * 128:(h + 1) * 128], writes=[v.r])
            for qt in range(T // 512):
                q = qr.next()
                P.dma('sp', q.t[:], self.qdfT[h * 128:(h + 1) * 128, qt * 512:(qt + 1) * 512], writes=[q.r])
                kmax = 4 * (qt + 1) - 1
                om = [o0r.next(), o1r.next()]
                for m in range(2):
                    pom = self.hbank()
                    psm = self.hbank()
                    def zmm(kb, m=m, k=k, q=q):
                        pz = self.bank()
                        P.add('pe', lambda e, pz=pz, k=k, kb=kb, q=q, m=m: e.matmul(
                            pz.t[:], lhsT=k.t[64 * m:64 * m + 64, kb * 128:(kb + 1) * 128], rhs=q.t[64 * m:64 * m + 64, :],
                            start=True, stop=True), reads=[k.r, q.r], writes=[pz.r])
                        return pz
                    pzn = zmm(0)
                    for kb in range(kmax + 1):
                        jd = kb - 4 * qt
                        pz = pzn
                        att = attr.next()
                        P.add('act', lambda e, att=att, pz=pz: e.activation(out=att.t[:], in_=pz.t[:], func=AF.Exp), reads=[pz.r], writes=[att.r])
                        if kb < kmax:
                            pzn = zmm(kb + 1)
                        if jd >= 0:
                            mk = c['mincb%d' % jd]
                            P.add('dve', lambda e, att=att, mk=mk: e.tensor_tensor(out=att.t[:], in0=att.t[:], in1=mk.t[:], op=ALU.mult),
                                  reads=[att.r, mk.r], writes=[att.r])
                        P.add('pe', lambda e, pb=pom, v=v, kb=kb, att=att, kmax=kmax: e.matmul(
                            pb.t[:], lhsT=v.t[:, kb, :], rhs=att.t[:], start=(kb == 0), stop=(kb == kmax)),
                            reads=[v.r, att.r], writes=[pom.r])
                        P.add('pe', lambda e, pb=psm, att=att, kb=kb, kmax=kmax: e.matmul(
                            pb.t[:], lhsT=c['onesb'].t[:], rhs=att.t[:], start=(kb == 0), stop=(kb == kmax)),
                            reads=[att.r, c['onesb'].r], writes=[psm.r])
                        yield
                    r0 = rr.next()
                    P.add('dve', lambda e, r0=r0, pb=psm: e.reciprocal(out=r0.t[:], in_=pb.t[:]), reads=[psm.r], writes=[r0.r])
                    P.add('dve', lambda e, o=om[m], pb=pom, r0=r0: e.tensor_tensor(out=o.t[:], in0=pb.t[:], in1=r0.t[:], op=ALU.mult),
                          reads=[pom.r, r0.r], writes=[om[m].r])
                o0, o1 = om
                P.add('dve', lambda e, o0=o0, o1=o1: e.scalar_tensor_tensor(out=o0.t[:], in0=o1.t[:], scalar=nlam.t[:, 0:1], in1=o0.t[:],
                                                                             op0=ALU.mult, op1=ALU.add), reads=[o0.r, o1.r, nlam.r], writes=[o0.r])
                self.headnorm(o0, 512, gcol.t[:, 0:1], None,
                              self.mixT[(8 + h) * 128:(9 + h) * 128, qt * 512:(qt + 1) * 512], tmp, greads=[gcol.r])
                yield

    def stage_gla(self, l):
        nc, P, T, c = self.nc, self.P, self.T, self.c
        NT = T // 128
        NC = T // 64
        w2 = self.sb([16, 256], F32, 'w2')
        P.dma('sp', w2.t[:], self.w2_in[l], writes=[w2.r])
        grt = self.sb([16, T], F32, 'grt')
        P.dma('sp', grt.t[:], self.grT, writes=[grt.r])
        qtl = self.ring(1, [64, T], F32, 'glq')
        ktl = self.ring(1, [64, T], F32, 'glk')
        ktok = self.ring(1, [128, NT, 64], F32, 'glkt')
        vtok = self.ring(1, [128, NT, 128], F32, 'glv')
        elr = self.ring(2, [64, NC], F32, 'glel')
        Er = self.ring(2, [64, 512], F32, 'glE')
        cumr = self.ring(2, [64, 512], F32, 'glcum')
        ebr = self.ring(2, [64, 512], F32, 'gleb')
        attr = self.ring(2, [128, 128], F32, 'glatt')
        Sr = self.ring(3, [64, 128], F32, 'glS')
        Uer = self.ring(3, [64, 128], F32, 'glUe')
        orr = self.ring(2, [128, 512], F32, 'glo')
        gater = self.ring(2, [128, 512], F32, 'glg')
        tmp = self.hn_tmp()
        vv = self.vgl.rearrange("(n p) c -> p n c", p=128)
        for h in range(NH):
            qt_ = qtl.next()
            kt_ = ktl.next()
            P.dma('sp', qt_.t[:], self.qglT[h * 64:(h + 1) * 64, :], writes=[qt_.r])
            P.dma('sp', kt_.t[:], self.kglT[h * 64:(h + 1) * 64, :], writes=[kt_.r])
            v = vtok.next()
            for n0_ in range(0, NT, 8):
                P.dma('sp', v.t[:, n0_:min(NT, n0_ + 8), :], vv[:, n0_:min(NT, n0_ + 8), h * 128:(h + 1) * 128], writes=[v.r])
            el = elr.next()
            ktk = ktok.next()
            for tb in range(T // 512):
                sl = slice(tb * 512, (tb + 1) * 512)
                pu = self.bank()
                P.add('pe', lambda e, pu=pu, sl=sl, h=h: e.matmul(pu.t[:64, :], lhsT=w2.t[:, h * 64:(h + 1) * 64], rhs=grt.t[:, sl],
                                                                  start=True, stop=True), reads=[w2.r, grt.r], writes=[pu.r])
                E = Er.next()
                P.add('act', lambda e, E=E, pu=pu, h=h: e.activation(out=E.t[:], in_=pu.t[:64, :], func=AF.Exp, scale=-1.0,
                                                                     bias=c['nglab'].t[:, l * 4 + h:l * 4 + h + 1]),
                      reads=[pu.r, c['nglab'].r], writes=[E.r])
                P.add('act', lambda e, E=E: e.activation(out=E.t[:], in_=E.t[:], func=AF.Ln, bias=c['cst'].t[0:64, 0:1]),
                      reads=[E.r, c['cst'].r], writes=[E.r])
                cum = cumr.next()
                P.add('dve', lambda e, cum=cum, E=E: e.tensor_tensor_scan(out=cum.t[:], data0=c['cmask'].t[0:64, :], data1=E.t[:],
                                                                          initial=0.0, op0=ALU.mult, op1=ALU.add),
                      reads=[E.r, c['cmask'].r], writes=[cum.r])
                eb = ebr.next()
                P.add('act', lambda e, eb=eb, cum=cum: e.activation(out=eb.t[:], in_=cum.t[:], func=AF.Exp, scale=-1.0 / 16), reads=[cum.r], writes=[eb.r])
                P.add('dve', lambda e, eb=eb, el=el, tb=tb: e.tensor_copy(
                    out=el.t[:, tb * 8:(tb + 1) * 8], in_=eb.t[:, :].rearrange("p (n f) -> p n f", f=64)[:, :, 63]),
                    reads=[eb.r], writes=[el.r])
                P.add('dve', lambda e, eb=eb, qt_=qt_, sl=sl: e.tensor_tensor(out=qt_.t[:, sl], in0=qt_.t[:, sl], in1=eb.t[:], op=ALU.mult),
                      reads=[eb.r, qt_.r], writes=[qt_.r])
                enb = ebr.next()
                P.add('act', lambda e, enb=enb, cum=cum: e.activation(out=enb.t[:], in_=cum.t[:], func=AF.Exp, scale=1.0 / 16), reads=[cum.r], writes=[enb.r])
                P.add('dve', lambda e, enb=enb, kt_=kt_, sl=sl: e.tensor_tensor(out=kt_.t[:, sl], in0=kt_.t[:, sl], in1=enb.t[:], op=ALU.mult),
                      reads=[enb.r, kt_.r], writes=[kt_.r])
                pk = self.bank()
                for j in range(4):
                    P.add('pe', lambda e, pk=pk, j=j, kt_=kt_, tb=tb: e.transpose(
                        out=pk.t[:, j * 64:(j + 1) * 64], in_=kt_.t[:, tb * 512 + j * 128: tb * 512 + (j + 1) * 128],
                        identity=c['ident'].t[0:64, 0:64]), reads=[kt_.r, c['ident'].r], writes=[pk.r])
                P.add('act', lambda e, pk=pk, ktk=ktk, tb=tb: e.copy(
                    out=ktk.t[:, tb * 4:(tb + 1) * 4, :], in_=pk.t[:, 0:256].rearrange("p (j d) -> p j d", j=4)),
                    reads=[pk.r], writes=[ktk.r])
            S = Sr.next()
            P.add('pool', lambda e, S=S: e.memset(S.t[:], 0.0), writes=[S.r])
            o = None
            for tt in range(NT):
                if tt % 4 == 0:
                    o = orr.next()
                tsl = slice(tt * 128, (tt + 1) * 128)
                pa = self.bank()
                P.add('pe', lambda e, pa=pa, kt_=kt_, qt_=qt_, tsl=tsl: e.matmul(pa.t[:, :128], lhsT=kt_.t[:, tsl], rhs=qt_.t[:, tsl],
                                                                                 start=True, stop=True), reads=[kt_.r, qt_.r], writes=[pa.r])
                att = attr.next()
                P.add('dve', lambda e, att=att, pa=pa: e.tensor_tensor(out=att.t[:], in0=pa.t[:, :128], in1=c['upi'].t[:], op=ALU.mult),
                      reads=[pa.r, c['upi'].r], writes=[att.r])
                po = self.bank()
                P.add('pe', lambda e, po=po, v=v, tt=tt, att=att: e.matmul(po.t[:, :128], lhsT=v.t[:, tt, :], rhs=att.t[:], start=True, stop=False),
                      reads=[v.r, att.r], writes=[po.r])
                for half in range(2):
                    n = 2 * tt + half
                    rows = slice(64 * half, 64 * half + 64)
                    cols = slice(64 * half, 64 * half + 64)
                    P.add('pe', lambda e, po=po, S=S, qt_=qt_, n=n, cols=cols, half=half: e.matmul(
                        po.t[:, cols], lhsT=S.t[:], rhs=qt_.t[:, n * 64:(n + 1) * 64], start=False, stop=(half == 1)),
                        reads=[S.r, qt_.r], writes=[po.r])
                    pU = self.bank()
                    P.add('pe', lambda e, pU=pU, ktk=ktk, v=v, tt=tt, rows=rows: e.matmul(
                        pU.t[:64, :128], lhsT=ktk.t[rows, tt, :], rhs=v.t[rows, tt, :], start=True, stop=True),
                        reads=[ktk.r, v.r], writes=[pU.r])
                    Ue = Uer.next()
                    P.add('act', lambda e, Ue=Ue, pU=pU, el=el, n=n: e.activation(out=Ue.t[:], in_=pU.t[:64, :128], func=AF.Copy,
                                                                                  scale=el.t[:, n:n + 1]), reads=[pU.r, el.r], writes=[Ue.r])
                    S2 = Sr.next()
                    P.add('dve', lambda e, S2=S2, S=S, el=el, n=n, Ue=Ue: e.scalar_tensor_tensor(
                        out=S2.t[:], in0=S.t[:], scalar=el.t[:, n:n + 1], in1=Ue.t[:], op0=ALU.mult, op1=ALU.add),
                        reads=[S.r, el.r, Ue.r], writes=[S2.r])
                    S = S2
                P.add('act', lambda e, o=o, po=po, tt=tt: e.copy(out=o.t[:, (tt % 4) * 128:(tt % 4 + 1) * 128], in_=po.t[:, :128]),
                      reads=[po.r], writes=[o.r])
                if tt % 4 == 3:
                    tb = tt // 4
                    gt = gater.next()
                    P.dma('sp', gt.t[:], self.ggT[h * 128:(h + 1) * 128, tb * 512:(tb + 1) * 512], writes=[gt.r])
                    self.headnorm(o, 512, c['hcols'].t[:, l * 8 + 2:l * 8 + 3], (gt.t[:], gt.r),
                                  self.mixT[(12 + h) * 128:(13 + h) * 128, tb * 512:(tb + 1) * 512], tmp)

    def stage_gdn(self, l):
        nc, P, T, c = self.nc, self.P, self.T, self.c
        NT = T // 128
        cw = self.sb([128, 4 * 1536], F32, 'cw')
        P.dma('sp', cw.t[:], self.convw_in[l], writes=[cw.r])
        sm = self.sb([128, 16], F32, 'sm')
        P.dma('sp', sm.t[:], self.small_in[l], writes=[sm.r])
        negA = self.sb([128, 4], F32, 'negA')
        P.add('act', lambda e: e.activation(out=negA.t[:], in_=sm.t[:, 0:4], func=AF.Exp), reads=[sm.r], writes=[negA.r])
        P.add('dve', lambda e: e.tensor_scalar(out=negA.t[:], in0=negA.t[:], scalar1=-1.0, scalar2=None, op0=ALU.mult), reads=[negA.r], writes=[negA.r])
        Xr = [self.ring(1, [128, 1536], F32, 'gx%d' % j) for j in range(4)]
        cvr = self.ring(2, [128, 1536], F32, 'gcv')
        tpr = self.ring(1, [128, 1536], F32, 'gtp')
        sqr = self.ring(1, [128, 1024], F32, 'gsq')
        bar = self.ring(2, [128, 8], F32, 'gba')
        smr = self.ring(2, [128, 32], F32, 'gsm')
        S = [self.sb([128, 128], F32, 'gS%d' % h) for h in range(NH)]
        oacc = [self.ring(1, [128, 512], F32, 'go%d' % h) for h in range(NH)]
        gater = self.ring(2, [128, 512], F32, 'ggate')
        tmp = self.hn_tmp()

        def rg(n, shape=(128, 128), nm='g'):
            return [self.ring(n, list(shape), F32, nm + str(h)) for h in range(NH)]
        kTr = rg(2, (128, 256), 'gkT')
        dgr = rg(1, nm='gdg')
        eGr = rg(1, nm='geG')
        dSr = rg(1, nm='gdS')
        dTr = rg(1, nm='gdT')
        Pr = rg(3, (128, 256), 'gP')
        Tr = rg(3, (128, 256), 'gT')
        QKr = rg(1, nm='gQK')
        qtr = rg(1, nm='gqt')
        vbr = rg(1, nm='gvb')
        kbgr = rg(1, nm='gkbg')
        khr = rg(1, nm='gkh')
        ur = rg(1, nm='gu')
        wTr = rg(1, nm='gwT')
        vnr = rg(1, nm='gvn')
        p2sr = rg(1, nm='gp2s')
        for h in range(NH):
            P.add('pool', lambda e, h=h: e.memset(S[h].t[:], 0.0), writes=[S[h].r])
        oh = [None] * NH
        for tt in range(NT):
            r0 = tt * 128
            X = [Xr[j].next() for j in range(4)]
            for j in range(4):
                sh = 3 - j
                if r0 - sh < 0:
                    P.add('pool', lambda e, xb=X[j]: e.memset(xb.t[0:32, :], 0.0), writes=[X[j].r])
                    if sh > 0:
                        P.dma('sp', X[j].t[sh:128, :], self.gqkv[0:128 - sh, :], writes=[X[j].r])
                    else:
                        P.dma('sp', X[j].t[:], self.gqkv[0:128, :], writes=[X[j].r])
                else:
                    P.dma('sp', X[j].t[:], self.gqkv[r0 - sh:r0 - sh + 128, :], writes=[X[j].r])
            cv = cvr.next()
            tp = tpr.next()
            P.add('dve', lambda e, cv=cv, X=X: e.tensor_tensor(out=cv.t[:], in0=X[3].t[:], in1=cw.t[:, 3 * 1536:4 * 1536], op=ALU.mult),
                  reads=[X[3].r, cw.r], writes=[cv.r])
            for j in range(3):
                P.add('pool', lambda e, tp=tp, X=X, j=j: e.tensor_tensor(out=tp.t[:], in0=X[j].t[:], in1=cw.t[:, j * 1536:(j + 1) * 1536], op=ALU.mult),
                      reads=[X[j].r, cw.r], writes=[tp.r])
                P.add('dve', lambda e, cv=cv, tp=tp: e.tensor_tensor(out=cv.t[:], in0=cv.t[:], in1=tp.t[:], op=ALU.add),
                      reads=[cv.r, tp.r], writes=[cv.r])
            P.add('act', lambda e, cv=cv: e.activation(out=cv.t[:], in_=cv.t[:], func=AF.Silu), reads=[cv.r], writes=[cv.r])
            sq = sqr.next()
            s_ = smr.next()
            P.add('pool', lambda e, sq=sq, cv=cv: e.tensor_tensor(out=sq.t[:], in0=cv.t[:, 0:1024], in1=cv.t[:, 0:1024], op=ALU.mult),
                  reads=[cv.r], writes=[sq.r])
            P.add('dve', lambda e, s_=s_, sq=sq: e.tensor_reduce(out=s_.t[:, 0:8], in_=sq.t[:].rearrange("p (a b) -> p a b", a=8), axis=AX.X, op=ALU.add),
                  reads=[sq.r], writes=[s_.r])
            P.add('dve', lambda e, s_=s_: e.tensor_scalar(out=s_.t[:, 0:8], in0=s_.t[:, 0:8], scalar1=EPS, scalar2=None, op0=ALU.add), reads=[s_.r], writes=[s_.r])
            P.add('act', lambda e, s_=s_: e.activation(out=s_.t[:, 0:8], in_=s_.t[:, 0:8], func=AF.Sqrt), reads=[s_.r], writes=[s_.r])
            P.add('dve', lambda e, s_=s_: e.reciprocal(out=s_.t[:, 0:8], in_=s_.t[:, 0:8]), reads=[s_.r], writes=[s_.r])
            P.add('dve', lambda e, s_=s_: e.tensor_scalar(out=s_.t[:, 0:4], in0=s_.t[:, 0:4], scalar1=128 ** -0.5, scalar2=None, op0=ALU.mult), reads=[s_.r], writes=[s_.r])
            for i in range(8):
                eng = 'dve' if i % 2 == 0 else 'pool'
                P.add(eng, lambda e, cv=cv, s_=s_, i=i: e.tensor_scalar(out=cv.t[:, i * 128:(i + 1) * 128], in0=cv.t[:, i * 128:(i + 1) * 128],
                                                                        scalar1=s_.t[:, i:i + 1], scalar2=None, op0=ALU.mult),
                      reads=[cv.r, s_.r], writes=[cv.r])
            ba = bar.next()
            P.dma('sp', ba.t[:], self.gba[r0:r0 + 128, :], writes=[ba.r])
            P.add('act', lambda e, s_=s_, ba=ba: e.activation(out=s_.t[:, 8:12], in_=ba.t[:, 0:4], func=AF.Sigmoid), reads=[ba.r], writes=[s_.r])
            P.add('dve', lambda e, ba=ba: e.tensor_tensor(out=ba.t[:, 4:8], in0=ba.t[:, 4:8], in1=sm.t[:, 4:8], op=ALU.add), reads=[ba.r, sm.r], writes=[ba.r])
            P.add('act', lambda e, ba=ba: e.activation(out=ba.t[:, 4:8], in_=ba.t[:, 4:8], func=AF.Exp), reads=[ba.r], writes=[ba.r])
            P.add('act', lambda e, ba=ba: e.activation(out=ba.t[:, 4:8], in_=ba.t[:, 4:8], func=AF.Ln, bias=c['cst'].t[:, 0:1]), reads=[ba.r, c['cst'].r], writes=[ba.r])
            P.add('dve', lambda e, s_=s_, ba=ba: e.tensor_tensor(out=s_.t[:, 12:16], in0=ba.t[:, 4:8], in1=negA.t[:], op=ALU.mult), reads=[ba.r, negA.r], writes=[s_.r])
            pg = self.bank()
            P.add('pe', lambda e, pg=pg, s_=s_: e.matmul(pg.t[:, 0:4], lhsT=c['upi'].t[:], rhs=s_.t[:, 12:16], start=True, stop=True),
                  reads=[s_.r, c['upi'].r], writes=[pg.r])
            P.add('dve', lambda e, pg=pg, s_=s_: e.tensor_copy(out=s_.t[:, 16:20], in_=pg.t[:, 0:4]), reads=[pg.r], writes=[s_.r])
            P.add('act', lambda e, s_=s_: e.activation(out=s_.t[:, 20:24], in_=s_.t[:, 16:20], func=AF.Exp), reads=[s_.r], writes=[s_.r])
            P.add('dve', lambda e, s_=s_: e.tensor_tensor(out=s_.t[:, 24:28], in0=s_.t[:, 20:24], in1=s_.t[:, 8:12], op=ALU.mult), reads=[s_.r], writes=[s_.r])
            P.add('dve', lambda e, s_=s_: e.tensor_scalar(out=s_.t[:, 28:32], in0=s_.t[:, 8:12], scalar1=-1.0, scalar2=None, op0=ALU.mult), reads=[s_.r], writes=[s_.r])
            import os
            PH = int(os.environ.get('GDN_PHASE', '9'))
            SUB = int(os.environ.get('GDN_SUB', '99'))
            if PH < 2:
                continue
            st = [dict() for _ in range(NH)]
            for h in range(NH):
                d = st[h]
                qh = cv.t[:, h * 128:(h + 1) * 128]
                kh = cv.t[:, 512 + h * 128:512 + (h + 1) * 128]
                vh = cv.t[:, 1024 + h * 128:1024 + (h + 1) * 128]
                gccol = s_.t[:, 16 + h:17 + h]
                pT = self.bank()
                P.add('pe', lambda e, pT=pT, qh=qh: e.transpose(out=pT.t[:, 0:128], in_=qh, identity=c['ident'].t[:]), reads=[cv.r, c['ident'].r], writes=[pT.r])
                P.add('pe', lambda e, pT=pT, kh=kh: e.transpose(out=pT.t[:, 128:256], in_=kh, identity=c['ident'].t[:]), reads=[cv.r, c['ident'].r], writes=[pT.r])
                kT = kTr[h].next()
                P.add('act', lambda e, kT=kT, pT=pT: e.copy(out=kT.t[:], in_=pT.t[:, 0:256]), reads=[pT.r], writes=[kT.r])
                if SUB < 2:
                    continue
                dg = dgr[h].next()
                P.add('pool', lambda e, dg=dg, gccol=gccol: e.tensor_scalar(out=dg.t[:], in0=c['ident'].t[:], scalar1=gccol, scalar2=None, op0=ALU.mult),
                      reads=[s_.r, c['ident'].r], writes=[dg.r])
                pG = self.bank()
                P.add('pe', lambda e, pG=pG, dg=dg: e.matmul(pG.t[:, 0:128], lhsT=c['ones'].t[:, 0:128], rhs=dg.t[:], start=True, stop=True),
                      reads=[dg.r, c['ones'].r], writes=[pG.r])
                eG = eGr[h].next()
                P.add('act', lambda e, eG=eG, pG=pG: e.activation(out=eG.t[:], in_=pG.t[:, 0:128], func=AF.Exp), reads=[pG.r], writes=[eG.r])
                if SUB < 3:
                    continue
                pD = self.bank()
                P.add('pe', lambda e, pD=pD, dg=dg: e.matmul(pD.t[:, 0:128], lhsT=dg.t[:], rhs=c['ones'].t[:, 0:128], start=True, stop=False),
                      reads=[dg.r, c['ones'].r], writes=[pD.r])
                P.add('pe', lambda e, pD=pD, dg=dg: e.matmul(pD.t[:, 0:128], lhsT=c['nones'].t[:], rhs=dg.t[:], start=False, stop=True),
                      reads=[dg.r, c['nones'].r], writes=[pD.r])
                dS = dSr[h].next()
                P.add('dve', lambda e, dS=dS, pD=pD: e.tensor_tensor(out=dS.t[:], in0=pD.t[:, 0:128], in1=c['pms'].t[:], op=ALU.add),
                      reads=[pD.r, c['pms'].r], writes=[dS.r])
                P.add('act', lambda e, dS=dS: e.activation(out=dS.t[:], in_=dS.t[:], func=AF.Exp), reads=[dS.r], writes=[dS.r])
                dT = dTr[h].next()
                P.add('dve', lambda e, dT=dT, pD=pD: e.tensor_tensor(out=dT.t[:], in0=pD.t[:, 0:128], in1=c['nmu'].t[:], op=ALU.add),
                      reads=[pD.r, c['nmu'].r], writes=[dT.r])
                P.add('act', lambda e, dT=dT: e.activation(out=dT.t[:], in_=dT.t[:], func=AF.Exp, scale=-1.0), reads=[dT.r], writes=[dT.r])
                if SUB < 4:
                    continue
                pK = self.bank()
                P.add('pe', lambda e, pK=pK, kT=kT: e.matmul(pK.t[:, 0:128], lhsT=kT.t[:, 128:256], rhs=kT.t[:, 128:256], start=True, stop=True),
                      reads=[kT.r], writes=[pK.r])
                P.add('pe', lambda e, pK=pK, kT=kT: e.matmul(pK.t[:, 128:256], lhsT=kT.t[:, 128:256], rhs=kT.t[:, 0:128], start=True, stop=True),
                      reads=[kT.r], writes=[pK.r])
                Pm = Pr[h].next()
                P.add('act', lambda e, Pm=Pm, pK=pK, s_=s_, h=h: e.activation(out=Pm.t[:, 0:128], in_=pK.t[:, 0:128], func=AF.Copy,
                                                                             scale=s_.t[:, 28 + h:29 + h]), reads=[pK.r, s_.r], writes=[Pm.r])
                P.add('dve', lambda e, Pm=Pm, dS=dS: e.tensor_tensor(out=Pm.t[:, 0:128], in0=Pm.t[:, 0:128], in1=dS.t[:], op=ALU.mult),
                      reads=[Pm.r, dS.r], writes=[Pm.r])
                QK = QKr[h].next()
                P.add('dve', lambda e, QK=QK, pK=pK, dT=dT: e.tensor_tensor(out=QK.t[:], in0=pK.t[:, 128:256], in1=dT.t[:], op=ALU.mult),
                      reads=[pK.r, dT.r], writes=[QK.r])
                if SUB < 5:
                    continue
                pB = self.bank()
                P.add('pe', lambda e, pB=pB, Pm=Pm: e.transpose(out=pB.t[:, 0:128], in_=Pm.t[:, 0:128], identity=c['ident'].t[:]),
                      reads=[Pm.r, c['ident'].r], writes=[pB.r])
                P.add('act', lambda e, Pm=Pm, pB=pB: e.copy(out=Pm.t[:, 128:256], in_=pB.t[:, 0:128]), reads=[pB.r, Pm.r], writes=[Pm.r])
                Tm = Tr[h].next()
                P.add('pool', lambda e, Tm=Tm, Pm=Pm: e.tensor_tensor(out=Tm.t[:, 0:128], in0=Pm.t[:, 0:128], in1=c['ident'].t[:], op=ALU.add),
                      reads=[Pm.r, c['ident'].r], writes=[Tm.r])
                P.add('pool', lambda e, Tm=Tm, Pm=Pm: e.tensor_tensor(out=Tm.t[:, 128:256], in0=Pm.t[:, 128:256], in1=c['ident'].t[:], op=ALU.add),
                      reads=[Pm.r, c['ident'].r, Tm.r], writes=[Tm.r])
                if SUB < 6:
                    continue
                qt_ = qtr[h].next()
                P.add('pool', lambda e, qt_=qt_, kT=kT, eG=eG: e.tensor_tensor(out=qt_.t[:], in0=kT.t[:, 0:128], in1=eG.t[:], op=ALU.mult),
                      reads=[kT.r, eG.r], writes=[qt_.r])
                vb = vbr[h].next()
                P.add('pool', lambda e, vb=vb, vh=vh, s_=s_, h=h: e.tensor_scalar(out=vb.t[:], in0=vh, scalar1=s_.t[:, 8 + h:9 + h], scalar2=None, op0=ALU.mult),
                      reads=[cv.r, s_.r], writes=[vb.r])
                kbg = kbgr[h].next()
                P.add('pool', lambda e, kbg=kbg, kh=kh, s_=s_, h=h: e.tensor_scalar(out=kbg.t[:], in0=kh, scalar1=s_.t[:, 24 + h:25 + h], scalar2=None, op0=ALU.mult),
                      reads=[cv.r, s_.r], writes=[kbg.r])
                if SUB < 7:
                    continue
                khat = khr[h].next()
                for half in range(2):
                    rows = slice(64 * half, 64 * half + 64)
                    P.add('dve', lambda e, khat=khat, kh=kh, dT=dT, rows=rows, half=half: e.tensor_scalar(
                        out=khat.t[rows, :], in0=kh[rows, :], scalar1=dT.t[rows, 64 * half + 63:64 * half + 64], scalar2=None, op0=ALU.mult),
                        reads=[cv.r, dT.r], writes=[khat.r])
                d.update(Pm=Pm, Tm=Tm, QK=QK, qt=qt_, vb=vb, kbg=kbg, khat=khat, eG=eG)
            if PH < 3:
                continue
            for it in range(5):
                for h in range(NH):
                    d = st[h]
                    Pm, Tm = d['Pm'], d['Tm']
                    pp = self.bank()
                    P.add('pe', lambda e, pp=pp, Pm=Pm: e.matmul(pp.t[:, 0:128], lhsT=Pm.t[:, 128:256], rhs=Pm.t[:, 0:128], start=True, stop=True),
                          reads=[Pm.r], writes=[pp.r])
                    P.add('pe', lambda e, pp=pp, Pm=Pm: e.matmul(pp.t[:, 128:256], lhsT=Pm.t[:, 0:128], rhs=Pm.t[:, 128:256], start=True, stop=True),
                          reads=[Pm.r], writes=[pp.r])
                    P2 = Pr[h].next()
                    P.add('act', lambda e, P2=P2, pp=pp: e.copy(out=P2.t[:], in_=pp.t[:, 0:256]), reads=[pp.r], writes=[P2.r])
                    pq = self.bank()
                    P.add('pe', lambda e, pq=pq, Tm=Tm, P2=P2: e.matmul(pq.t[:, 0:128], lhsT=Tm.t[:, 128:256], rhs=P2.t[:, 0:128], start=True, stop=True),
                          reads=[Tm.r, P2.r], writes=[pq.r])
                    P.add('pe', lambda e, pq=pq, Tm=Tm, P2=P2: e.matmul(pq.t[:, 128:256], lhsT=Tm.t[:, 0:128], rhs=P2.t[:, 128:256], start=True, stop=True),
                          reads=[Tm.r, P2.r], writes=[pq.r])
                    T2 = Tr[h].next()
                    P.add('dve', lambda e, T2=T2, pq=pq, Tm=Tm: e.tensor_tensor(out=T2.t[:], in0=pq.t[:, 0:256], in1=Tm.t[:], op=ALU.add),
                          reads=[pq.r, Tm.r], writes=[T2.r])
                    d['Pm'], d['Tm'] = P2, T2
            if PH < 4:
                continue
            for h in range(NH):
                d = st[h]
                Tm = d['Tm']
                pu = self.bank()
                P.add('pe', lambda e, pu=pu, Tm=Tm, vb=d['vb']: e.matmul(pu.t[:, 0:128], lhsT=Tm.t[:, 128:256], rhs=vb.t[:], start=True, stop=True),
                      reads=[Tm.r, d['vb'].r], writes=[pu.r])
                P.add('pe', lambda e, pu=pu, Tm=Tm, kbg=d['kbg']: e.matmul(pu.t[:, 128:256], lhsT=kbg.t[:], rhs=Tm.t[:, 128:256], start=True, stop=True),
                      reads=[Tm.r, d['kbg'].r], writes=[pu.r])
                u = ur[h].next()
                wT = wTr[h].next()
                P.add('act', lambda e, u=u, pu=pu: e.copy(out=u.t[:], in_=pu.t[:, 0:128]), reads=[pu.r], writes=[u.r])
                P.add('dve', lambda e, wT=wT, pu=pu: e.tensor_copy(out=wT.t[:], in_=pu.t[:, 128:256]), reads=[pu.r], writes=[wT.r])
                d.update(u=u, wT=wT)
                if tt % 4 == 0:
                    oh[h] = oacc[h].next()
            if PH < 5:
                continue
            pos = [self.hbank() for _ in range(NH)]
            for half in range(2):
                rows = slice(64 * half, 64 * half + 64)
                cols = rows
                for h in range(NH):
                    d = st[h]
                    p1 = self.bank()
                    P.add('pe', lambda e, p1=p1, wT=d['wT'], h=h: e.matmul(p1.t[:, 0:128], lhsT=wT.t[:], rhs=S[h].t[:], start=True, stop=True),
                          reads=[d['wT'].r, S[h].r], writes=[p1.r])
                    if half == 0:
                        vn = vnr[h].next()
                        d['vn'] = vn
                    vn = d['vn']
                    P.add('dve', lambda e, vn=vn, u=d['u'], p1=p1, rows=rows: e.tensor_tensor(out=vn.t[rows, :], in0=p1.t[rows, 0:128], in1=u.t[rows, :], op=ALU.subtract),
                          reads=[d['u'].r, p1.r], writes=[vn.r])
                    P.add('dve', lambda e, vn=vn, rows=rows: e.tensor_scalar(out=vn.t[rows, :], in0=vn.t[rows, :], scalar1=-1.0, scalar2=None, op0=ALU.mult),
                          reads=[vn.r], writes=[vn.r])
                    P.add('pe', lambda e, po=pos[h], h=h, qt_=d['qt'], cols=cols: e.matmul(po.t[:, cols], lhsT=S[h].t[:], rhs=qt_.t[:, cols], start=True, stop=False),
                          reads=[S[h].r, d['qt'].r], writes=[pos[h].r])
                    P.add('pe', lambda e, po=pos[h], vn=vn, QK=d['QK'], rows=rows, cols=cols: e.matmul(po.t[:, cols], lhsT=vn.t[rows, :], rhs=QK.t[rows, cols], start=False, stop=True),
                          reads=[vn.r, d['QK'].r], writes=[pos[h].r])
                    p2 = self.bank()
                    P.add('pe', lambda e, p2=p2, khat=d['khat'], vn=vn, rows=rows: e.matmul(p2.t[:, 0:128], lhsT=khat.t[rows, :], rhs=vn.t[rows, :], start=True, stop=True),
                          reads=[d['khat'].r, vn.r], writes=[p2.r])
                    p2s = p2sr[h].next()
                    P.add('act', lambda e, p2s=p2s, p2=p2: e.copy(out=p2s.t[:], in_=p2.t[:, 0:128]), reads=[p2.r], writes=[p2s.r])
                    P.add('dve', lambda e, h=h, eG=d['eG'], p2s=p2s, half=half: e.scalar_tensor_tensor(
                        out=S[h].t[:], in0=S[h].t[:], scalar=eG.t[:, 64 * half + 63:64 * half + 64], in1=p2s.t[:], op0=ALU.mult, op1=ALU.add),
                        reads=[S[h].r, d['eG'].r, p2s.r], writes=[S[h].r])
            for h in range(NH):
                o = oh[h]
                P.add('act', lambda e, o=o, po=pos[h], tt=tt: e.copy(out=o.t[:, (tt % 4) * 128:(tt % 4 + 1) * 128], in_=po.t[:, 0:128]),
                      reads=[pos[h].r], writes=[o.r])
                if tt % 4 == 3:
                    tb = tt // 4
                    gt = gater.next()
                    P.dma('sp', gt.t[:], self.gzT[h * 128:(h + 1) * 128, tb * 512:(tb + 1) * 512], writes=[gt.r])
                    self.headnorm(o, 512, c['hcols'].t[:, l * 8:l * 8 + 1], (gt.t[:], gt.r),
                                  self.mixT[(4 + h) * 128:(5 + h) * 128, tb * 512:(tb + 1) * 512], tmp)


def host_layout(inp, depth):
    f = np.float32
    L = depth
    g = [None] * (2 * L + 1)
    for l in range(L):
        g[2 * l] = inp['attn_norm'][l]
        g[2 * l + 1] = inp['ffn_norm'][l]
    g[2 * L] = inp['final_norm']
    gains = np.concatenate([np.asarray(v, f).reshape(16, 128).T for v in g], axis=1)
    convw = np.stack([np.broadcast_to(np.asarray(inp['gdn_conv_w'][l], f).reshape(1, 4 * 1536), (128, 4 * 1536)) for l in range(L)])
    small = np.zeros((L, 128, 16), f)
    for l in range(L):
        small[l, :, 0:4] = np.asarray(inp['gdn_a_log'][l], f)[None, :]
        small[l, :, 4:8] = np.asarray(inp['gdn_dt_bias'][l], f)[None, :]
    hcols = np.zeros((128, L * 8), f)
    for l in range(L):
        hcols[:, l * 8 + 0] = inp['gdn_out_norm'][l]
        hcols[:, l * 8 + 1] = inp['diff_out_norm'][l]
        hcols[:, l * 8 + 2] = inp['gla_out_norm'][l]
    lamv = np.zeros((L, 128, 256), f)
    for l in range(L):
        row = np.concatenate([inp['diff_lam_q1'][l], inp['diff_lam_k1'][l], inp['diff_lam_q2'][l], inp['diff_lam_k2'][l]])
        lamv[l] = np.asarray(row, f)[None, :]
    glab = np.zeros((64, L * 4), f)
    for l in range(L):
        glab[:, l * 4:(l + 1) * 4] = np.asarray(inp['gla_gate_b'][l], f).reshape(4, 64).T
    return dict(gains=np.ascontiguousarray(gains), convw=np.ascontiguousarray(convw), small=small, hcols=hcols,
                lamv=lamv, w2=np.ascontiguousarray(np.asarray(inp['gla_gate_w2'], f)[:L]), glab=glab)


_CACHE = {}


def run(inp, T, depth, n_cores, debug=(), stages=None, trace=False):
    import time
    t0 = time.time()
    key = (T, depth, tuple(debug), None if stages is None else tuple(stages))
    if key not in _CACHE:
        _CACHE[key] = Builder(T, depth, debug, stages)
    b = _CACHE[key]
    print("build_s", time.time() - t0, flush=True)
    f = np.float32
    lay = host_layout(inp, depth)
    common = dict(lay)
    for k in ('w_in', 'w_out', 'w_gate', 'w_up', 'w_down'):
        common[k] = np.ascontiguousarray(np.asarray(inp[k], f)[:depth])
    B = inp['x'].shape[0]
    in_maps = []
    for ci in range(n_cores):
        m = dict(common)
        m['x'] = np.ascontiguousarray(np.asarray(inp['x'][ci % B], f))
        in_maps.append(m)
    t0 = time.time()
    res = run_bass_kernel_spmd(b.nc, in_maps, core_ids=list(range(n_cores)), **({'trace': True} if trace else {}))
    print("run_s", time.time() - t0, flush=True)
    return res, b


def kernel(**inputs):
    x = np.asarray(inputs['x'])
    B, T, _ = x.shape
    res, b = run(inputs, T, 4, 8)
    out = np.stack([np.asarray(res.results[i]['out']) for i in range(B)], axis=0)
    return out.astype(np.float32)
```

```python
import math
import numpy as np
import concourse.bass as bass
import concourse.mybir as mybir
from concourse.bass_utils import run_bass_kernel_spmd

F32 = mybir.dt.float32
BF16 = mybir.dt.bfloat16
AF = mybir.ActivationFunctionType
ALU = mybir.AluOpType
AX = mybir.AxisListType

COMPUTE = ('pe', 'act', 'dve', 'pool')
ENGS = ('pe', 'act', 'dve', 'pool', 'sp')
N_DMA_SEMS = 48


class Res:
    __slots__ = ('w', 'r', 'excl')

    def __init__(self):
        self.w = None
        self.r = {}
        self.excl = False


class Op:
    __slots__ = ('eng', 'emit', 'deps', 'dma', 'sem', 'val', 'signal', 'prev_same_sem')

    def __init__(self, eng, emit, dma):
        self.eng = eng
        self.emit = emit
        self.dma = dma
        self.deps = []
        self.sem = None
        self.val = 0
        self.signal = dma
        self.prev_same_sem = None


class Buf:
    __slots__ = ('t', 'r')

    def __init__(self, t):
        self.t = t
        self.r = Res()


class BankAlloc:
    def __init__(self, banks, rot, held):
        self.rot = [banks[i] for i in rot]
        self.held = [banks[i] for i in held]
        self.i = 0
        self.j = 0

    def bank(self):
        b = self.rot[self.i % len(self.rot)]
        self.i += 1
        return b

    def hbank(self):
        b = self.held[self.j % len(self.held)]
        self.j += 1
        return b


class Ring:
    def __init__(self, bufs):
        self.bufs = bufs
        self.i = 0

    def next(self):
        b = self.bufs[self.i % len(self.bufs)]
        self.i += 1
        return b


class Prog:
    def __init__(self, nc):
        self.nc = nc
        self.ops = {e: [] for e in ENGS}
        self.n_dma = 0
        self.dma_last = [None] * N_DMA_SEMS
        self.out_dmas = []
        self.pending_dmas = []

    def add(self, eng, emit, reads=(), writes=(), dma=False, is_out=False):
        op = Op(eng, emit, dma)
        deps = {}
        for r in reads:
            if r.w is not None:
                deps[id(r.w)] = r.w
            if r.excl:
                for k, o in r.r.items():
                    if k != eng:
                        deps[id(o)] = o
        for w in writes:
            if w.w is not None:
                deps[id(w.w)] = w.w
            for o in w.r.values():
                deps[id(o)] = o
        for d in deps.values():
            if eng == 'pe' and d.eng == 'pe' and not d.dma and not dma:
                continue
            d.signal = True
            op.deps.append(d)
        for r in reads:
            if dma:
                r.r[('dma', self.n_dma)] = op
            else:
                r.r[eng] = op
        for w in writes:
            w.w = op
            w.r = {}
        if dma:
            slot = self.n_dma % N_DMA_SEMS
            op.prev_same_sem = self.dma_last[slot]
            self.dma_last[slot] = op
            op.sem = slot
            self.n_dma += 1
            self.pending_dmas.append(op)
            if is_out:
                self.out_dmas.append(op)
        self.ops[eng].append(op)
        return op

    def dma(self, eng, out, in_, reads=(), writes=(), is_out=False, **kw):
        return self.add(eng, lambda e: e.dma_start(out=out, in_=in_, **kw), reads, writes,
                        dma=True, is_out=is_out)

    def barrier(self):
        lasts = []
        for e in COMPUTE:
            for op in reversed(self.ops[e]):
                if op.emit is not None and not op.dma:
                    op.signal = True
                    lasts.append(op)
                    break
        pend = list(self.pending_dmas)
        self.pending_dmas = []
        for e in ENGS:
            b = Op(e, None, False)
            b.deps = [d for d in lasts if d.eng != e] + pend
            self.ops[e].append(b)

    def emit_all(self):
        nc = self.nc
        fin = Op('sp', None, False)
        fin.deps = list(self.out_dmas)
        self.ops['sp'].append(fin)
        esem = {e: nc.alloc_semaphore('es_' + e) for e in COMPUTE}
        dsem = [nc.alloc_semaphore('ds_%d' % i) for i in range(N_DMA_SEMS)]
        for e in ENGS:
            cnt = 0
            for op in self.ops[e]:
                if op.dma or op.emit is None:
                    continue
                if op.signal:
                    cnt += 1
                    op.val = cnt
                    op.sem = esem[e]
        for slot in range(N_DMA_SEMS):
            chain = []
            o = self.dma_last[slot]
            while o is not None:
                chain.append(o)
                o = o.prev_same_sem
            chain.reverse()
            v = 0
            for o in chain:
                v += 16
                o.val = v
                o.sem = dsem[slot]
        stats = {}
        with nc.Block() as block:
            def make(e):
                def body(eng):
                    waited = {}
                    nw = 0
                    for op in self.ops[e]:
                        need = {}
                        deps = op.deps
                        if op.dma and op.prev_same_sem is not None:
                            deps = deps + [op.prev_same_sem]
                        for d in deps:
                            k = id(d.sem)
                            if k not in need or need[k][1] < d.val:
                                need[k] = (d.sem, d.val)
                        for k, (s, v) in need.items():
                            if waited.get(k, 0) >= v:
                                continue
                            eng.wait_ge(s, v)
                            waited[k] = v
                            nw += 1
                        if op.emit is None:
                            continue
                        inst = op.emit(eng)
                        if op.signal:
                            inst.then_inc(op.sem, 16 if op.dma else 1)
                    stats[e] = (len(self.ops[e]), nw)
                return body
            block.tensor(make('pe'))
            block.scalar(make('act'))
            block.vector(make('dve'))
            block.gpsimd(make('pool'))
            block.sync(make('sp'))
        return stats


D = 2048
FF = 5632
INC = 6680
NH = 4
EPS = 1e-6
BIG = 30000.0
OFF = dict(sb_q=0, sb_k=512, sb_v=1024, gd_q=1536, gd_k=2048, gd_v=2560, gd_z=3072, gd_b=3584,
           gd_a=3588, df_q=3592, df_k=4104, df_v=4616, gl_q=5128, gl_k=5384, gl_v=5640,
           gl_g=6152, gl_r=6664)


class Builder:
    def __init__(self, T, depth, debug=(), stages=None):
        self.T = T
        self.depth = depth
        self.debug = set(debug)
        self.stages = stages
        self.uid = 0
        nc = self.nc = bass.Bass("TRN2", target_bir_lowering=False)
        self.P = Prog(nc)
        self.build()

    def sb(self, shape, dt, name='t'):
        self.uid += 1
        return Buf(self.nc.alloc_sbuf_tensor('%s_%d' % (name, self.uid), shape, dt))

    def ring(self, n, shape, dt, name='r'):
        return Ring([self.sb(shape, dt, name) for _ in range(n)])

    def dram(self, name, shape, dt):
        kind = "ExternalOutput" if name in self.debug else "Internal"
        return self.nc.dram_tensor(name, shape, dt, kind=kind).ap()

    def inp(self, name, shape, dt=F32):
        return self.nc.dram_tensor(name, shape, dt, kind="ExternalInput").ap()

    def bank(self):
        return self.cur.bank()

    def hbank(self):
        return self.cur.hbank()

    def interleave(self, gens_allocs):
        live = list(gens_allocs)
        while live:
            nxt = []
            for g, a in live:
                self.cur = a
                try:
                    next(g)
                    nxt.append((g, a))
                except StopIteration:
                    pass
            live = nxt
        self.cur = self.defalloc

    def want(self, s):
        return self.stages is None or s in self.stages

    def build(self):
        nc, P, T, L = self.nc, self.P, self.T, self.depth
        self.x_in = self.inp("x", [T, D])
        self.w_in = self.inp("w_in", [L, D, INC])
        self.w_out = self.inp("w_out", [L, D, D])
        self.w_gate = self.inp("w_gate", [L, D, FF])
        self.w_up = self.inp("w_up", [L, D, FF])
        self.w_down = self.inp("w_down", [L, FF, D])
        self.gains_in = self.inp("gains", [128, (2 * L + 1) * 16])
        self.convw_in = self.inp("convw", [L, 128, 4 * 1536])
        self.small_in = self.inp("small", [L, 128, 16])
        self.hcols_in = self.inp("hcols", [128, L * 8])
        self.lam_in = self.inp("lamv", [L, 128, 256])
        self.w2_in = self.inp("w2", [L, 16, 256])
        self.glab_in = self.inp("glab", [64, L * 4])
        self.out = self.nc.dram_tensor("out", [T, D], F32, kind="ExternalOutput").ap()

        self.xT = self.dram("xT", [D, T], F32)
        self.hT = self.dram("hT", [D, T], BF16)
        self.mixT = self.dram("mixT", [D, T], BF16)
        self.gT = self.dram("gT", [FF, T], BF16)
        self.qsbT = self.dram("qsbT", [512, T], BF16)
        self.ksbT = self.dram("ksbT", [512, T], BF16)
        self.vsb = self.dram("vsb", [T, 512], BF16)
        self.gqkv = self.dram("gqkv", [T, 1536], F32)
        self.gzT = self.dram("gzT", [512, T], F32)
        self.gba = self.dram("gba", [T, 8], F32)
        self.qdfT = self.dram("qdfT", [512, T], BF16)
        self.kdfT = self.dram("kdfT", [512, T], BF16)
        self.vdf = self.dram("vdf", [T, 512], BF16)
        self.qglT = self.dram("qglT", [256, T], F32)
        self.kglT = self.dram("kglT", [256, T], F32)
        self.vgl = self.dram("vgl", [T, 512], F32)
        self.ggT = self.dram("ggT", [512, T], F32)
        self.grT = self.dram("grT", [16, T], F32)

        self.banks = [Buf(nc.alloc_psum_tensor('bank%d' % i, [128, 512], F32)) for i in range(8)]
        self.defalloc = BankAlloc(self.banks, [0, 1, 2, 3], [4, 5, 6, 7])
        self.cur = self.defalloc
        for b_ in self.banks:
            b_.r.excl = True

        self.consts()
        P.barrier()
        if self.want('pre'):
            with nc.reset_on_exit():
                self.stage_transpose_in()
            P.barrier()
        for l in range(L):
            self.layer(l)
        if self.want('post'):
            with nc.reset_on_exit():
                self.stage_final()
        self.stats = P.emit_all()

    def consts(self):
        nc, P, L = self.nc, self.P, self.depth
        c = self.c = {}

        def mk(name, shape, dt):
            c[name] = self.sb(shape, dt, name)
            return c[name]
        ones = mk('ones', [128, 512], F32)
        P.add('pool', lambda e: e.memset(ones.t[:], 1.0), writes=[ones.r])
        onesb = mk('onesb', [128, 128], BF16)
        P.add('pool', lambda e: e.memset(onesb.t[:], 1.0), writes=[onesb.r])
        nonesb = mk('nonesb', [128, 128], BF16)
        P.add('pool', lambda e: e.memset(nonesb.t[:], -1.0), writes=[nonesb.r])
        nones = mk('nones', [128, 128], F32)
        P.add('pool', lambda e: e.memset(nones.t[:], -1.0), writes=[nones.r])
        zer = mk('zer', [128, 512], F32)
        P.add('pool', lambda e: e.memset(zer.t[:], 0.0), writes=[zer.r])
        cst = mk('cst', [128, 4], F32)
        P.add('pool', lambda e: e.memset(cst.t[:, 0:1], 1.0), writes=[cst.r])
        P.add('pool', lambda e: e.memset(cst.t[:, 1:2], EPS), writes=[cst.r])
        P.add('pool', lambda e: e.memset(cst.t[:, 2:3], 0.0), writes=[cst.r])

        def sel(name, src, pattern, op, fill, base, cm, shape=(128, 128), dt=F32):
            b = mk(name, list(shape), dt)
            n = shape[1]
            P.add('pool', lambda e: e.affine_select(out=b.t[:], in_=src.t[:, :n], pattern=pattern,
                                                    compare_op=op, fill=fill, base=base,
                                                    channel_multiplier=cm),
                  reads=[src.r], writes=[b.r])
            return b
        ident = sel('ident', ones, [[-1, 128]], ALU.is_equal, 0.0, 0, 1)
        ustr_f = sel('ustr_f', ones, [[-1, 128]], ALU.is_gt, 0.0, 0, 1)
        nustr = mk('nustr', [128, 128], BF16)
        P.add('dve', lambda e: e.tensor_scalar(out=nustr.t[:], in0=ustr_f.t[:], scalar1=-1.0,
                                               scalar2=None, op0=ALU.mult),
              reads=[ustr_f.r], writes=[nustr.r])
        for j in range(4):
            a = sel('mstr%d' % j, ones, [[1, 512]], ALU.is_gt, 0.0, -128 * j, -1, shape=(128, 512))
            b = mk('mstrb%d' % j, [128, 512], BF16)
            P.add('dve', lambda e, a=a, b=b: e.tensor_copy(out=b.t[:], in_=a.t[:]), reads=[a.r], writes=[b.r])
            a2 = sel('minc%d' % j, ones, [[1, 512]], ALU.is_ge, 0.0, -128 * j, -1, shape=(128, 512))
            b2 = mk('mincb%d' % j, [128, 512], BF16)
            P.add('dve', lambda e, a=a2, b=b2: e.tensor_copy(out=b.t[:], in_=a.t[:]), reads=[a2.r], writes=[b2.r])
        bd = mk('bd', [128, 128], F32)
        P.add('pool', lambda e: e.memset(bd.t[:], 0.0), writes=[bd.r])
        P.add('pool', lambda e: e.memset(bd.t[0:64, 0:64], 1.0), writes=[bd.r])
        P.add('pool', lambda e: e.memset(bd.t[64:128, 64:128], 1.0), writes=[bd.r])
        upi = sel('upi', bd, [[1, 128]], ALU.is_ge, 0.0, 0, -1)
        lows = sel('lows', bd, [[-1, 128]], ALU.is_gt, 0.0, 0, 1)
        c['upi'] = upi
        pms = mk('pms', [128, 128], F32)
        P.add('dve', lambda e: e.tensor_scalar(out=pms.t[:], in0=lows.t[:], scalar1=BIG, scalar2=-BIG,
                                               op0=ALU.mult, op1=ALU.add), reads=[lows.r], writes=[pms.r])
        nmu = mk('nmu', [128, 128], F32)
        P.add('dve', lambda e: e.tensor_scalar(out=nmu.t[:], in0=upi.t[:], scalar1=-BIG, scalar2=BIG,
                                               op0=ALU.mult, op1=ALU.add), reads=[upi.r], writes=[nmu.r])
        cm = mk('cmask', [128, 512], F32)
        P.add('pool', lambda e: e.memset(cm.t[:], 1.0), writes=[cm.r])
        for k in range(8):
            P.add('pool', lambda e, k=k: e.memset(cm.t[:, 64 * k:64 * k + 1], 0.0), writes=[cm.r])
        gains = mk('gains', [128, (2 * L + 1) * 16], F32)
        P.dma('sp', gains.t[:], self.gains_in, writes=[gains.r])
        hcols = mk('hcols', [128, L * 8], F32)
        P.dma('sp', hcols.t[:], self.hcols_in, writes=[hcols.r])
        glab = mk('glab', [64, L * 4], F32)
        P.dma('sp', glab.t[:], self.glab_in, writes=[glab.r])
        nglab = mk('nglab', [64, L * 4], F32)
        P.add('dve', lambda e: e.tensor_scalar(out=nglab.t[:], in0=glab.t[:], scalar1=-1.0, scalar2=None,
                                               op0=ALU.mult), reads=[glab.r], writes=[nglab.r])

    def stage_transpose_in(self):
        nc, P, T, c = self.nc, self.P, self.T, self.c
        xr = self.ring(2, [128, D], F32, 'xin')
        st = self.ring(2, [128, 16, 512], F32, 'xst')
        xTv = self.xT.rearrange("(c p) t -> p c t", p=128)
        for tb in range(T // 512):
            s = st.next()
            for j in range(4):
                xb = xr.next()
                r0 = tb * 512 + j * 128
                P.dma('sp', xb.t[:], self.x_in[r0:r0 + 128, :], writes=[xb.r])
                for cg in range(4):
                    bk = self.bank()
                    for q in range(4):
                        cc = cg * 4 + q
                        P.add('pe', lambda e, bk=bk, q=q, xb=xb, cc=cc: e.transpose(
                            out=bk.t[:, q * 128:(q + 1) * 128], in_=xb.t[:, cc * 128:(cc + 1) * 128],
                            identity=c['ident'].t[:]), reads=[xb.r, c['ident'].r], writes=[bk.r])
                    eng = 'dve' if cg % 2 == 0 else 'act'
                    if eng == 'dve':
                        P.add('dve', lambda e, bk=bk, s=s, cg=cg, j=j: e.tensor_copy(
                            out=s.t[:, cg * 4:cg * 4 + 4, j * 128:(j + 1) * 128],
                            in_=bk.t[:, :].rearrange("p (q f) -> p q f", q=4)), reads=[bk.r], writes=[s.r])
                    else:
                        P.add('act', lambda e, bk=bk, s=s, cg=cg, j=j: e.copy(
                            out=s.t[:, cg * 4:cg * 4 + 4, j * 128:(j + 1) * 128],
                            in_=bk.t[:, :].rearrange("p (q f) -> p q f", q=4)), reads=[bk.r], writes=[s.r])
            P.dma('sp', xTv[:, :, tb * 512:(tb + 1) * 512], s.t[:], reads=[s.r])

    def norm_block(self, xt, sq, h_out_fn, gi, rs):
        P, c = self.P, self.c
        P.add('act', lambda e: e.activation(out=sq.t[:], in_=xt.t[:], func=AF.Square), reads=[xt.r], writes=[sq.r])
        bk = self.bank()
        for cc in range(16):
            P.add('pe', lambda e, cc=cc: e.matmul(bk.t[:], lhsT=c['onesb'].t[:], rhs=sq.t[:, cc, :],
                                                  start=(cc == 0), stop=(cc == 15)),
                  reads=[sq.r, c['onesb'].r], writes=[bk.r])
        self.rstd_from(bk, rs, 1.0 / D)
        for cc in range(16):
            o, orr = h_out_fn(cc)
            P.add('dve', lambda e, cc=cc, o=o: e.scalar_tensor_tensor(
                out=o, in0=xt.t[:, cc, :], scalar=c['gains'].t[:, gi * 16 + cc:gi * 16 + cc + 1], in1=rs.t[:],
                op0=ALU.mult, op1=ALU.mult), reads=[xt.r, rs.r, c['gains'].r], writes=[orr])

    def rstd_from(self, bk, rs, inv_n, n=512):
        P, c = self.P, self.c
        P.add('dve', lambda e: e.tensor_scalar(out=rs.t[:, :n], in0=bk.t[:, :n], scalar1=inv_n, scalar2=EPS,
                                               op0=ALU.mult, op1=ALU.add), reads=[bk.r], writes=[rs.r])
        P.add('act', lambda e: e.activation(out=rs.t[:, :n], in_=rs.t[:, :n], func=AF.Sqrt), reads=[rs.r], writes=[rs.r])
        P.add('dve', lambda e: e.reciprocal(out=rs.t[:, :n], in_=rs.t[:, :n]), reads=[rs.r], writes=[rs.r])

    def stage_norm(self, gi):
        nc, P, T = self.nc, self.P, self.T
        xr = self.ring(2, [128, 16, 512], F32, 'nx')
        sqr = self.ring(1, [128, 16, 512], BF16, 'nsq')
        hr = self.ring(2, [128, 16, 512], BF16, 'nh')
        rsr = self.ring(2, [128, 512], F32, 'nrs')
        xTv = self.xT.rearrange("(c p) t -> p c t", p=128)
        hTv = self.hT.rearrange("(c p) t -> p c t", p=128)
        for tb in range(T // 512):
            xt = xr.next()
            P.dma('sp', xt.t[:], xTv[:, :, tb * 512:(tb + 1) * 512], writes=[xt.r])
            h = hr.next()
            self.norm_block(xt, sqr.next(), lambda cc, h=h: (h.t[:, cc, :], h.r), gi, rsr.next())
            P.dma('sp', hTv[:, :, tb * 512:(tb + 1) * 512], h.t[:], reads=[h.r])

    def stage_final(self):
        nc, P, T, c = self.nc, self.P, self.T, self.c
        gi = 2 * self.depth
        xr = self.ring(2, [128, 16, 512], F32, 'fx')
        sqr = self.ring(1, [128, 16, 512], BF16, 'fsq')
        hr = self.ring(1, [128, 16, 512], F32, 'fh')
        rsr = self.ring(2, [128, 512], F32, 'frs')
        orr = self.ring(2, [128, D], F32, 'fo')
        xTv = self.xT.rearrange("(c p) t -> p c t", p=128)
        for tb in range(T // 512):
            xt = xr.next()
            P.dma('sp', xt.t[:], xTv[:, :, tb * 512:(tb + 1) * 512], writes=[xt.r])
            h = hr.next()
            self.norm_block(xt, sqr.next(), lambda cc, h=h: (h.t[:, cc, :], h.r), gi, rsr.next())
            for j in range(4):
                ob = orr.next()
                for cg in range(4):
                    bk = self.bank()
                    for q in range(4):
                        cc = cg * 4 + q
                        P.add('pe', lambda e, bk=bk, q=q, h=h, cc=cc, j=j: e.transpose(
                            out=bk.t[:, q * 128:(q + 1) * 128], in_=h.t[:, cc, j * 128:(j + 1) * 128],
                            identity=c['ident'].t[:]), reads=[h.r, c['ident'].r], writes=[bk.r])
                    if cg % 2 == 0:
                        P.add('dve', lambda e, bk=bk, ob=ob, cg=cg: e.tensor_copy(
                            out=ob.t[:, cg * 512:(cg + 1) * 512], in_=bk.t[:]), reads=[bk.r], writes=[ob.r])
                    else:
                        P.add('act', lambda e, bk=bk, ob=ob, cg=cg: e.copy(
                            out=ob.t[:, cg * 512:(cg + 1) * 512], in_=bk.t[:]), reads=[bk.r], writes=[ob.r])
                r0 = tb * 512 + j * 128
                P.dma('sp', self.out[r0:r0 + 128, :], ob.t[:], reads=[ob.r], is_out=True)

    def dense(self, inT, KC, groups):
        nc, P, T = self.nc, self.P, self.T
        TB = min(T, 1024)
        pwmax = 512 if KC <= 16 else 256
        nin = 2 if KC <= 16 else 1
        xin = self.ring(nin, [128, KC, TB], BF16, 'din')
        nw = max(len(g['ws']) for g in groups)
        KG = 4 if KC % 4 == 0 else 1
        wr = [Ring([[self.sb([128, KG, pwmax], BF16, 'dw') for _ in range(KC // KG)] for _ in range(2)])
              for _ in range(nw)]
        inTv = inT.rearrange("(c p) t -> p c t", p=128)
        for tb in range(T // TB):
            xb = xin.next()
            for k0 in range(0, KC, 8):
                k1 = min(KC, k0 + 8)
                P.dma('sp', xb.t[:, k0:k1, :], inTv[:, k0:k1, tb * TB:(tb + 1) * TB], writes=[xb.r])
            for g in groups:
                ncols = g['ncols']
                for c0 in range(0, ncols, pwmax):
                    pw = min(pwmax, ncols - c0)
                    wbs = []
                    for wi, (W, col0) in enumerate(g['ws']):
                        wb = wr[wi].next()
                        Wv = W[:, col0 + c0:col0 + c0 + pw].rearrange("(c p) m -> p c m", p=128)
                        for kg in range(KC // KG):
                            P.dma('pool', wb[kg].t[:, :, :pw], Wv[:, kg * KG:(kg + 1) * KG, :], writes=[wb[kg].r])
                        wbs.append(wb)
                    if g['mode'] == 'F':
                        for m0 in range(0, pw, 128):
                            mw = min(128, pw - m0)
                            for n0 in range(0, TB, 512):
                                bks = []
                                for wb in wbs:
                                    bk = self.bank()
                                    for kc in range(KC):
                                        wk = wb[kc // KG]
                                        P.add('pe', lambda e, bk=bk, wk=wk, kc=kc, m0=m0, mw=mw, n0=n0, xb=xb: e.matmul(
                                            bk.t[:mw, :], lhsT=wk.t[:, kc % KG, m0:m0 + mw], rhs=xb.t[:, kc, n0:n0 + 512],
                                            start=(kc == 0), stop=(kc == KC - 1)), reads=[wk.r, xb.r], writes=[bk.r])
                                    bks.append(bk)
                                g['epi'](bks, c0 + m0, mw, tb * TB + n0)
                    else:
                        wb = wbs[0]
                        for t0 in range(0, TB, 128):
                            bk = self.bank()
                            for kc in range(KC):
                                wk = wb[kc // KG]
                                P.add('pe', lambda e, bk=bk, wk=wk, kc=kc, pw=pw, t0=t0, xb=xb: e.matmul(
                                    bk.t[:, :pw], lhsT=xb.t[:, kc, t0:t0 + 128], rhs=wk.t[:, kc % KG, :pw],
                                    start=(kc == 0), stop=(kc == KC - 1)), reads=[wk.r, xb.r], writes=[bk.r])
                            g['epi']([bk], c0, pw, tb * TB + t0)

    def epi_F_store(self, dst, dt, func=AF.Copy, scale=1.0):
        P = self.P
        st = self.strings[dt]

        def epi(bks, cofs, mw, t0):
            s = st.next()
            P.add('act', lambda e: e.activation(out=s.t[:mw, :], in_=bks[0].t[:mw, :], func=func, scale=scale),
                  reads=[bks[0].r], writes=[s.r])
            P.dma('sp', dst[cofs:cofs + mw, t0:t0 + 512], s.t[:mw, :], reads=[s.r])
        return epi

    def epi_T_store(self, dst, dt):
        P = self.P
        st = self.strings[dt]

        def epi(bks, cofs, pw, t0):
            s = st.next()
            P.add('dve', lambda e: e.tensor_copy(out=s.t[:, :pw], in_=bks[0].t[:, :pw]), reads=[bks[0].r], writes=[s.r])
            P.dma('sp', dst[t0:t0 + 128, cofs:cofs + pw], s.t[:, :pw], reads=[s.r])
        return epi

    def epi_resid(self):
        P = self.P
        xr = self.ring(3, [128, 512], F32, 'erx')

        def epi(bks, cofs, mw, t0):
            x = xr.next()
            P.dma('sp', x.t[:], self.xT[cofs:cofs + 128, t0:t0 + 512], writes=[x.r])
            P.add('dve', lambda e: e.tensor_tensor(out=x.t[:], in0=bks[0].t[:], in1=x.t[:], op=ALU.add),
                  reads=[bks[0].r, x.r], writes=[x.r])
            P.dma('sp', self.xT[cofs:cofs + 128, t0:t0 + 512], x.t[:], reads=[x.r])
        return epi

    def epi_swiglu(self):
        P = self.P
        sr = self.ring(2, [128, 512], F32, 'esw')
        gr = self.ring(3, [128, 512], BF16, 'esg')

        def epi(bks, cofs, mw, t0):
            s = sr.next()
            g = gr.next()
            P.add('act', lambda e: e.activation(out=s.t[:], in_=bks[0].t[:], func=AF.Silu), reads=[bks[0].r], writes=[s.r])
            P.add('dve', lambda e: e.tensor_tensor(out=g.t[:], in0=bks[1].t[:], in1=s.t[:], op=ALU.mult),
                  reads=[bks[1].r, s.r], writes=[g.r])
            P.dma('sp', self.gT[cofs:cofs + 128, t0:t0 + 512], g.t[:], reads=[g.r])
        return epi

    def headnorm(self, o, n, gain_col, gate, dst, tmp, greads=()):
        P, c = self.P, self.c
        sq, rs, y, yb = tmp
        P.add('act', lambda e: e.activation(out=sq.t[:, :n], in_=o.t[:, :n], func=AF.Square), reads=[o.r], writes=[sq.r])
        bk = self.bank()
        P.add('pe', lambda e: e.matmul(bk.t[:, :n], lhsT=c['onesb'].t[:], rhs=sq.t[:, :n], start=True, stop=True),
              reads=[sq.r, c['onesb'].r], writes=[bk.r])
        self.rstd_from(bk, rs, 1.0 / 128, n)
        if gate is None:
            P.add('dve', lambda e: e.scalar_tensor_tensor(out=yb.t[:, :n], in0=o.t[:, :n], scalar=gain_col, in1=rs.t[:, :n],
                                                          op0=ALU.mult, op1=ALU.mult), reads=[o.r, rs.r, c['hcols'].r] + list(greads), writes=[yb.r])
        else:
            P.add('dve', lambda e: e.scalar_tensor_tensor(out=y.t[:, :n], in0=o.t[:, :n], scalar=gain_col, in1=rs.t[:, :n],
                                                          op0=ALU.mult, op1=ALU.mult), reads=[o.r, rs.r, c['hcols'].r], writes=[y.r])
            P.add('pool', lambda e: e.tensor_tensor(out=yb.t[:, :n], in0=y.t[:, :n], in1=gate[0], op=ALU.mult),
                  reads=[y.r, gate[1]], writes=[yb.r])
        P.dma('sp', dst, yb.t[:, :n], reads=[yb.r])

    def hn_tmp(self):
        return (self.sb([128, 512], BF16, 'hsq'), self.sb([128, 512], F32, 'hrs'),
                self.sb([128, 512], F32, 'hy'), self.sb([128, 512], BF16, 'hyb'))

    def layer(self, l):
        nc, P, T = self.nc, self.P, self.T
        lam_init = 0.8 - 0.6 * math.exp(-0.3 * l)
        if self.want('norm1'):
            with nc.reset_on_exit():
                self.stage_norm(2 * l)
            P.barrier()
        if self.want('inproj'):
            with nc.reset_on_exit():
                W = self.w_in[l]
                g = []
                self.strings = {F32: self.ring(3, [128, 512], F32, 'stf'), BF16: self.ring(3, [128, 512], BF16, 'stb')}

                def G(mode, name, ncols, epi):
                    g.append(dict(mode=mode, ws=[(W, OFF[name])], ncols=ncols, epi=epi))
                G('F', 'sb_q', 512, self.epi_F_store(self.qsbT, BF16, scale=128 ** -0.5))
                G('F', 'sb_k', 512, self.epi_F_store(self.ksbT, BF16))
                G('T', 'sb_v', 512, self.epi_T_store(self.vsb, BF16))
                G('T', 'gd_q', 1536, self.epi_T_store(self.gqkv, F32))
                G('F', 'gd_z', 512, self.epi_F_store(self.gzT, F32, func=AF.Silu))
                G('T', 'gd_b', 8, self.epi_T_store(self.gba, F32))
                G('F', 'df_q', 512, self.epi_F_store(self.qdfT, BF16, scale=64 ** -0.5))
                G('F', 'df_k', 512, self.epi_F_store(self.kdfT, BF16))
                G('T', 'df_v', 512, self.epi_T_store(self.vdf, BF16))
                G('F', 'gl_q', 256, self.epi_F_store(self.qglT, F32, scale=64 ** -0.5))
                G('F', 'gl_k', 256, self.epi_F_store(self.kglT, F32))
                G('T', 'gl_v', 512, self.epi_T_store(self.vgl, F32))
                G('F', 'gl_g', 512, self.epi_F_store(self.ggT, F32, func=AF.Silu))
                G('F', 'gl_r', 16, self.epi_F_store(self.grT, F32))
                self.dense(self.hT, 16, g)
            P.barrier()
        if False:
            with nc.reset_on_exit():
                a1 = BankAlloc(self.banks, [0, 1], [4])
                a2 = BankAlloc(self.banks, [2, 3], [5, 6])
                self.interleave([(self.stage_sb(), a1), (self.stage_diff(l, lam_init), a2)])
            P.barrier()
        else:
            if self.want('sb'):
                with nc.reset_on_exit():
                    self.interleave([(self.stage_sb(), self.defalloc)])
                P.barrier()
            if self.want('diff') and self.want('gla'):
                with nc.reset_on_exit():
                    a1 = BankAlloc(self.banks, [0, 1], [4, 5])
                    a2 = BankAlloc(self.banks, [2, 3, 6, 7], [])
                    self.interleave([(self.stage_diff(l, lam_init), a1), (self.stage_gla(l), a2)])
                P.barrier()
            else:
                if self.want('diff'):
                    with nc.reset_on_exit():
                        self.interleave([(self.stage_diff(l, lam_init), BankAlloc(self.banks, [0, 1, 2, 3], [4, 5]))])
                    P.barrier()
                if self.want('gla'):
                    with nc.reset_on_exit():
                        self.interleave([(self.stage_gla(l), self.defalloc)])
                    P.barrier()
        if self.want('gdn'):
            with nc.reset_on_exit():
                self.stage_gdn(l)
            P.barrier()
        if self.want('outproj'):
            with nc.reset_on_exit():
                self.dense(self.mixT, 16, [dict(mode='F', ws=[(self.w_out[l], 0)], ncols=D, epi=self.epi_resid())])
            P.barrier()
        if self.want('ffn'):
            with nc.reset_on_exit():
                self.stage_norm(2 * l + 1)
            P.barrier()
            with nc.reset_on_exit():
                self.dense(self.hT, 16, [dict(mode='F', ws=[(self.w_gate[l], 0), (self.w_up[l], 0)], ncols=FF,
                                              epi=self.epi_swiglu())])
            P.barrier()
            with nc.reset_on_exit():
                self.dense(self.gT, 44, [dict(mode='F', ws=[(self.w_down[l], 0)], ncols=D, epi=self.epi_resid())])
            P.barrier()

    def stage_sb(self):
        nc, P, T, c = self.nc, self.P, self.T, self.c
        NT = T // 128
        kT = self.ring(2, [128, T], BF16, 'sbk')
        vt = self.ring(2, [128, NT, 128], BF16, 'sbv')
        qr = self.ring(2, [128, 512], BF16, 'sbq')
        Er = self.ring(3, [128, 512], F32, 'sbE')
        SPr = self.ring(4, [128, 512], BF16, 'sbSP')
        t1r = self.ring(3, [128, 512], F32, 'sbt1')
        t2r = self.ring(3, [128, 512], F32, 'sbt2')
        attr = self.ring(5, [128, 512], BF16, 'sbatt')
        Lf = self.ring(2, [128, 512], F32, 'sbLf')
        Lb = self.ring(3, [128, 512], BF16, 'sbLb')
        zb = Ring([self.banks[0], self.banks[1], self.banks[2]])
        tb = Ring([self.banks[3], self.banks[5]])
        pob = Ring([self.banks[4], self.banks[6]])
        orr = self.ring(2, [128, 512], BF16, 'sbo')
        vv = self.vsb.rearrange("(n p) c -> p n c", p=128)
        for h in range(NH):
            k = kT.next()
            P.dma('sp', k.t[:], self.ksbT[h * 128:(h + 1) * 128, :], writes=[k.r])
            v = vt.next()
            for n0_ in range(0, NT, 8):
                P.dma('sp', v.t[:, n0_:min(NT, n0_ + 8), :], vv[:, n0_:min(NT, n0_ + 8), h * 128:(h + 1) * 128], writes=[v.r])
            for qt in range(T // 512):
                q = qr.next()
                P.dma('sp', q.t[:], self.qsbT[h * 128:(h + 1) * 128, qt * 512:(qt + 1) * 512], writes=[q.r])
                lf = Lf.next()
                lbs = [Lb.next()]
                P.add('pool', lambda e, lf=lf: e.memset(lf.t[:], 0.0), writes=[lf.r])
                P.add('pool', lambda e, lb=lbs[0]: e.memset(lb.t[:], 0.0), writes=[lbs[0].r])
                po = pob.next()
                kmax = 4 * (qt + 1) - 1
                kbs = list(range(kmax, -1, -1))
                nb = len(kbs)

                def phA(kb, k=k, q=q, qt=qt):
                    jd = kb - 4 * qt
                    pz = zb.next()
                    P.add('pe', lambda e, pz=pz, k=k, kb=kb, q=q: e.matmul(
                        pz.t[:], lhsT=k.t[:, kb * 128:(kb + 1) * 128], rhs=q.t[:], start=True, stop=True),
                        reads=[k.r, q.r], writes=[pz.r])
                    E = Er.next()
                    SP = SPr.next()
                    P.add('act', lambda e, E=E, pz=pz: e.activation(out=E.t[:], in_=pz.t[:], func=AF.Exp), reads=[pz.r], writes=[E.r])
                    P.add('act', lambda e, E=E, SP=SP: e.activation(out=SP.t[:], in_=E.t[:], func=AF.Ln, bias=c['cst'].t[:, 0:1]),
                          reads=[E.r, c['cst'].r], writes=[SP.r])
                    if jd >= 0:
                        m = c['mstrb%d' % jd]
                        P.add('dve', lambda e, SP=SP, m=m: e.tensor_tensor(out=SP.t[:], in0=SP.t[:], in1=m.t[:], op=ALU.mult),
                              reads=[SP.r, m.r], writes=[SP.r])
                    return dict(kb=kb, jd=jd, pz=pz, SP=SP)

                def phB(b, lf=lf, lbs=lbs):
                    pz, SP, jd, kb = b['pz'], b['SP'], b['jd'], b['kb']
                    lb = lbs[0]
                    pt = tb.next()
                    P.add('pe', lambda e, pt=pt, SP=SP: e.matmul(pt.t[:], lhsT=c['nustr'].t[:], rhs=SP.t[:], start=True, stop=False),
                          reads=[SP.r, c['nustr'].r], writes=[pt.r])
                    P.add('pe', lambda e, pt=pt, lb=lb: e.matmul(pt.t[:], lhsT=c['nonesb'].t[:], rhs=lb.t[:], start=False, stop=True),
                          reads=[lb.r, c['nonesb'].r], writes=[pt.r])
                    t1 = t1r.next()
                    P.add('dve', lambda e, t1=t1, pz=pz, SP=SP: e.tensor_tensor(out=t1.t[:], in0=pz.t[:], in1=SP.t[:], op=ALU.subtract),
                          reads=[pz.r, SP.r], writes=[t1.r])
                    t2 = t2r.next()
                    P.add('dve', lambda e, t2=t2, pt=pt, t1=t1: e.tensor_tensor(out=t2.t[:], in0=pt.t[:], in1=t1.t[:], op=ALU.add),
                          reads=[pt.r, t1.r], writes=[t2.r])
                    att = attr.next()
                    P.add('act', lambda e, att=att, t2=t2: e.activation(out=att.t[:], in_=t2.t[:], func=AF.Exp), reads=[t2.r], writes=[att.r])
                    if jd >= 0:
                        m = c['mstrb%d' % jd]
                        P.add('pool', lambda e, att=att, m=m: e.tensor_tensor(out=att.t[:], in0=att.t[:], in1=m.t[:], op=ALU.mult),
                              reads=[att.r, m.r], writes=[att.r])
                    if kb > 0:
                        P.add('pool', lambda e, lf=lf, SP=SP: e.tensor_tensor(out=lf.t[:], in0=lf.t[:], in1=SP.t[:], op=ALU.add),
                              reads=[lf.r, SP.r], writes=[lf.r])
                        lb2 = Lb.next()
                        P.add('pool', lambda e, lf=lf, lb2=lb2: e.tensor_copy(out=lb2.t[:], in_=lf.t[:]), reads=[lf.r], writes=[lb2.r])
                        lbs[0] = lb2
                    b['att'] = att

                def phC(b, po=po, v=v, kmax=kmax):
                    kb, att = b['kb'], b['att']
                    P.add('pe', lambda e, po=po, v=v, kb=kb, att=att, kmax=kmax: e.matmul(
                        po.t[:], lhsT=v.t[:, kb, :], rhs=att.t[:], start=(kb == kmax), stop=(kb == 0)),
                        reads=[v.r, att.r], writes=[po.r])

                blk = [None] * nb
                blk[0] = phA(kbs[0])
                for i in range(nb):
                    if i + 1 < nb:
                        blk[i + 1] = phA(kbs[i + 1])
                    phB(blk[i])
                    if i >= 2:
                        phC(blk[i - 2])
                    yield
                for i in range(max(0, nb - 2), nb):
                    phC(blk[i])
                o = orr.next()
                P.add('act', lambda e, o=o, po=po: e.copy(out=o.t[:], in_=po.t[:]), reads=[po.r], writes=[o.r])
                P.dma('sp', self.mixT[h * 128:(h + 1) * 128, qt * 512:(qt + 1) * 512], o.t[:], reads=[o.r])

    def stage_diff(self, l, lam_init):
        nc, P, T, c = self.nc, self.P, self.T, self.c
        NT = T // 128
        lv = self.sb([128, 256], F32, 'lv')
        P.dma('sp', lv.t[:], self.lam_in[l], writes=[lv.r])
        pr = self.sb([128, 128], F32, 'lpr')
        dots = self.sb([128, 2], F32, 'ldots')
        nlam = self.sb([128, 1], F32, 'nlam')
        P.add('dve', lambda e: e.tensor_tensor(out=pr.t[:, 0:64], in0=lv.t[:, 0:64], in1=lv.t[:, 64:128], op=ALU.mult), reads=[lv.r], writes=[pr.r])
        P.add('dve', lambda e: e.tensor_tensor(out=pr.t[:, 64:128], in0=lv.t[:, 128:192], in1=lv.t[:, 192:256], op=ALU.mult), reads=[lv.r, pr.r], writes=[pr.r])
        P.add('dve', lambda e: e.tensor_reduce(out=dots.t[:], in_=pr.t[:].rearrange("p (a b) -> p a b", a=2), axis=AX.X, op=ALU.add),
              reads=[pr.r], writes=[dots.r])
        P.add('act', lambda e: e.activation(out=dots.t[:], in_=dots.t[:], func=AF.Exp), reads=[dots.r], writes=[dots.r])
        P.add('dve', lambda e: e.tensor_tensor(out=nlam.t[:], in0=dots.t[:, 1:2], in1=dots.t[:, 0:1], op=ALU.subtract), reads=[dots.r], writes=[nlam.r])
        P.add('dve', lambda e: e.tensor_scalar(out=nlam.t[:], in0=nlam.t[:], scalar1=-lam_init, scalar2=None, op0=ALU.add), reads=[nlam.r], writes=[nlam.r])
        gcol = self.sb([128, 1], F32, 'dgc')
        P.add('dve', lambda e: e.tensor_scalar(out=gcol.t[:], in0=c['hcols'].t[:, l * 8 + 1:l * 8 + 2], scalar1=1.0 - lam_init, scalar2=None,
                                               op0=ALU.mult), reads=[c['hcols'].r], writes=[gcol.r])
        kT = self.ring(2, [128, T], BF16, 'dfk')
        vt = self.ring(2, [128, NT, 128], BF16, 'dfv')
        qr = self.ring(2, [128, 512], BF16, 'dfq')
        attr = self.ring(4, [128, 512], BF16, 'dfatt')
        rr = self.ring(2, [128, 512], F32, 'dfr')
        o0r = self.ring(2, [128, 512], F32, 'dfo0')
        o1r = self.ring(2, [128, 512], F32, 'dfo1')
        tmp = self.hn_tmp()
        vv = self.vdf.rearrange("(n p) c -> p n c", p=128)
        for h in range(NH):
            k = kT.next()
            P.dma('sp', k.t[:], self.kdfT[h * 128:(h + 1) * 128, :], writes=[k.r])
            v = vt.next()
            for n0_ in range(0, NT, 8):
                P.dma('sp', v.t[:, n0_:min(NT, n0_ + 8), :], vv[:, n0_:min(NT, n0_ + 8), h * 128:(h + 1) * 128], writes=[v.r])
            for qt in range(T // 512):
                q = qr.next()
                P.dma('sp', q.t[:], self.qdfT[h * 128:(h + 1) * 128, qt * 512:(qt + 1) * 512], writes=[q.r])
                kmax = 4 * (qt + 1) - 1
                om = [o0r.next(), o1r.next()]
                for m in range(2):
                    pom = self.hbank()
                    psm = self.hbank()
                    def zmm(kb, m=m, k=k, q=q):
                        pz = self.bank()
                        P.add('pe', lambda e, pz=pz, k=k, kb=kb, q=q, m=m: e.matmul(
                            pz.t[:], lhsT=k.t[64 * m:64 * m + 64, kb * 128:(kb + 1) * 128], rhs=q.t[64 * m:64 * m + 64, :],
                            start=True, stop=True), reads=[k.r, q.r], writes=[pz.r])
                        return pz
                    pzn = zmm(0)
                    for kb in range(kmax + 1):
                        jd = kb - 4 * qt
                        pz = pzn
                        att = attr.next()
                        P.add('act', lambda e, att=att, pz=pz: e.activation(out=att.t[:], in_=pz.t[:], func=AF.Exp), reads=[pz.r], writes=[att.r])
                        if kb < kmax:
                            pzn = zmm(kb + 1)
                        if jd >= 0:
                            mk = c['mincb%d' % jd]
                            P.add('dve', lambda e, att=att, mk=mk: e.tensor_tensor(out=att.t[:], in0=att.t[:], in1=mk.t[:], op=ALU.mult),
                                  reads=[att.r, mk.r], writes=[att.r])
                        P.add('pe', lambda e, pb=pom, v=v, kb=kb, att=att, kmax=kmax: e.matmul(
                            pb.t[:], lhsT=v.t[:, kb, :], rhs=att.t[:], start=(kb == 0), stop=(kb == kmax)),
                            reads=[v.r, att.r], writes=[pom.r])
                        P.add('pe', lambda e, pb=psm, att=att, kb=kb, kmax=kmax: e.matmul(
                            pb.t[:], lhsT=c['onesb'].t[:], rhs=att.t[:], start=(kb == 0), stop=(kb == kmax)),
                            reads=[att.r, c['onesb'].r], writes=[psm.r])
                        yield
                    r0 = rr.next()
                    P.add('dve', lambda e, r0=r0, pb=psm: e.reciprocal(out=r0.t[:], in_=pb.t[:]), reads=[psm.r], writes=[r0.r])
                    P.add('dve', lambda e, o=om[m], pb=pom, r0=r0: e.tensor_tensor(out=o.t[:], in0=pb.t[:], in1=r0.t[:], op=ALU.mult),
                          reads=[pom.r, r0.r], writes=[om[m].r])
                o0, o1 = om
                P.add('dve', lambda e, o0=o0, o1=o1: e.scalar_tensor_tensor(out=o0.t[:], in0=o1.t[:], scalar=nlam.t[:, 0:1], in1=o0.t[:],
                                                                             op0=ALU.mult, op1=ALU.add), reads=[o0.r, o1.r, nlam.r], writes=[o0.r])
                self.headnorm(o0, 512, gcol.t[:, 0:1], None,
                              self.mixT[(8 + h) * 128:(9 + h) * 128, qt * 512:(qt + 1) * 512], tmp, greads=[gcol.r])
                yield

    def stage_gla(self, l):
        nc, P, T, c = self.nc, self.P, self.T, self.c
        NT = T // 128
        NC = T // 64
        w2 = self.sb([16, 256], F32, 'w2')
        P.dma('sp', w2.t[:], self.w2_in[l], writes=[w2.r])
        grt = self.sb([16, T], F32, 'grt')
        P.dma('sp', grt.t[:], self.grT, writes=[grt.r])
        qtl = self.ring(1, [64, T], F32, 'glq')
        ktl = self.ring(1, [64, T], F32, 'glk')
        ktok = self.ring(1, [128, NT, 64], F32, 'glkt')
        vtok = self.ring(1, [128, NT, 128], F32, 'glv')
        elr = self.ring(2, [64, NC], F32, 'glel')
        Er = self.ring(2, [64, 512], F32, 'glE')
        cumr = self.ring(2, [64, 512], F32, 'glcum')
        ebr = self.ring(2, [64, 512], F32, 'gleb')
        attr = self.ring(2, [128, 128], F32, 'glatt')
        Sr = self.ring(3, [64, 128], F32, 'glS')
        Uer = self.ring(3, [64, 128], F32, 'glUe')
        orr = self.ring(2, [128, 512], F32, 'glo')
        gater = self.ring(2, [128, 512], F32, 'glg')
        tmp = self.hn_tmp()
        vv = self.vgl.rearrange("(n p) c -> p n c", p=128)
        for h in range(NH):
            qt_ = qtl.next()
            kt_ = ktl.next()
            P.dma('sp', qt_.t[:], self.qglT[h * 64:(h + 1) * 64, :], writes=[qt_.r])
            P.dma('sp', kt_.t[:], self.kglT[h * 64:(h + 1) * 64, :], writes=[kt_.r])
            v = vtok.next()
            for n0_ in range(0, NT, 8):
                P.dma('sp', v.t[:, n0_:min(NT, n0_ + 8), :], vv[:, n0_:min(NT, n0_ + 8), h * 128:(h + 1) * 128], writes=[v.r])
            el = elr.next()
            ktk = ktok.next()
            for tb in range(T // 512):
                sl = slice(tb * 512, (tb + 1) * 512)
                pu = self.bank()
                P.add('pe', lambda e, pu=pu, sl=sl, h=h: e.matmul(pu.t[:64, :], lhsT=w2.t[:, h * 64:(h + 1) * 64], rhs=grt.t[:, sl],
                                                                  start=True, stop=True), reads=[w2.r, grt.r], writes=[pu.r])
                E = Er.next()
                P.add('act', lambda e, E=E, pu=pu, h=h: e.activation(out=E.t[:], in_=pu.t[:64, :], func=AF.Exp, scale=-1.0,
                                                                     bias=c['nglab'].t[:, l * 4 + h:l * 4 + h + 1]),
                      reads=[pu.r, c['nglab'].r], writes=[E.r])
                P.add('act', lambda e, E=E: e.activation(out=E.t[:], in_=E.t[:], func=AF.Ln, bias=c['cst'].t[0:64, 0:1]),
                      reads=[E.r, c['cst'].r], writes=[E.r])
                cum = cumr.next()
                P.add('dve', lambda e, cum=cum, E=E: e.tensor_tensor_scan(out=cum.t[:], data0=c['cmask'].t[0:64, :], data1=E.t[:],
                                                                          initial=0.0, op0=ALU.mult, op1=ALU.add),
                      reads=[E.r, c['cmask'].r], writes=[cum.r])
                eb = ebr.next()
                P.add('act', lambda e, eb=eb, cum=cum: e.activation(out=eb.t[:], in_=cum.t[:], func=AF.Exp, scale=-1.0 / 16), reads=[cum.r], writes=[eb.r])
                P.add('dve', lambda e, eb=eb, el=el, tb=tb: e.tensor_copy(
                    out=el.t[:, tb * 8:(tb + 1) * 8], in_=eb.t[:, :].rearrange("p (n f) -> p n f", f=64)[:, :, 63]),
                    reads=[eb.r], writes=[el.r])
                P.add('dve', lambda e, eb=eb, qt_=qt_, sl=sl: e.tensor_tensor(out=qt_.t[:, sl], in0=qt_.t[:, sl], in1=eb.t[:], op=ALU.mult),
                      reads=[eb.r, qt_.r], writes=[qt_.r])
                enb = ebr.next()
                P.add('act', lambda e, enb=enb, cum=cum: e.activation(out=enb.t[:], in_=cum.t[:], func=AF.Exp, scale=1.0 / 16), reads=[cum.r], writes=[enb.r])
                P.add('dve', lambda e, enb=enb, kt_=kt_, sl=sl: e.tensor_tensor(out=kt_.t[:, sl], in0=kt_.t[:, sl], in1=enb.t[:], op=ALU.mult),
                      reads=[enb.r, kt_.r], writes=[kt_.r])
                pk = self.bank()
                for j in range(4):
                    P.add('pe', lambda e, pk=pk, j=j, kt_=kt_, tb=tb: e.transpose(
                        out=pk.t[:, j * 64:(j + 1) * 64], in_=kt_.t[:, tb * 512 + j * 128: tb * 512 + (j + 1) * 128],
                        identity=c['ident'].t[0:64, 0:64]), reads=[kt_.r, c['ident'].r], writes=[pk.r])
                P.add('act', lambda e, pk=pk, ktk=ktk, tb=tb: e.copy(
                    out=ktk.t[:, tb * 4:(tb + 1) * 4, :], in_=pk.t[:, 0:256].rearrange("p (j d) -> p j d", j=4)),
                    reads=[pk.r], writes=[ktk.r])
                yield
            S = Sr.next()
            P.add('pool', lambda e, S=S: e.memset(S.t[:], 0.0), writes=[S.r])
            o = None
            for tt in range(NT):
                if tt % 4 == 0:
                    o = orr.next()
                tsl = slice(tt * 128, (tt + 1) * 128)
                pa = self.bank()
                P.add('pe', lambda e, pa=pa, kt_=kt_, qt_=qt_, tsl=tsl: e.matmul(pa.t[:, :128], lhsT=kt_.t[:, tsl], rhs=qt_.t[:, tsl],
                                                                                 start=True, stop=True), reads=[kt_.r, qt_.r], writes=[pa.r])
                att = attr.next()
                P.add('dve', lambda e, att=att, pa=pa: e.tensor_tensor(out=att.t[:], in0=pa.t[:, :128], in1=c['upi'].t[:], op=ALU.mult),
                      reads=[pa.r, c['upi'].r], writes=[att.r])
                po = self.bank()
                P.add('pe', lambda e, po=po, v=v, tt=tt, att=att: e.matmul(po.t[:, :128], lhsT=v.t[:, tt, :], rhs=att.t[:], start=True, stop=False),
                      reads=[v.r, att.r], writes=[po.r])
                for half in range(2):
                    n = 2 * tt + half
                    rows = slice(64 * half, 64 * half + 64)
                    cols = slice(64 * half, 64 * half + 64)
                    P.add('pe', lambda e, po=po, S=S, qt_=qt_, n=n, cols=cols, half=half: e.matmul(
                        po.t[:, cols], lhsT=S.t[:], rhs=qt_.t[:, n * 64:(n + 1) * 64], start=False, stop=(half == 1)),
                        reads=[S.r, qt_.r], writes=[po.r])
                    pU = self.bank()
                    P.add('pe', lambda e, pU=pU, ktk=ktk, v=v, tt=tt, rows=rows: e.matmul(
                        pU.t[:64, :128], lhsT=ktk.t[rows, tt, :], rhs=v.t[rows, tt, :], start=True, stop=True),
                        reads=[ktk.r, v.r], writes=[pU.r])
                    Ue = Uer.next()
                    P.add('act', lambda e, Ue=Ue, pU=pU, el=el, n=n: e.activation(out=Ue.t[:], in_=pU.t[:64, :128], func=AF.Copy,
                                                                                  scale=el.t[:, n:n + 1]), reads=[pU.r, el.r], writes=[Ue.r])
                    S2 = Sr.next()
                    P.add('dve', lambda e, S2=S2, S=S, el=el, n=n, Ue=Ue: e.scalar_tensor_tensor(
                        out=S2.t[:], in0=S.t[:], scalar=el.t[:, n:n + 1], in1=Ue.t[:], op0=ALU.mult, op1=ALU.add),
                        reads=[S.r, el.r, Ue.r], writes=[S2.r])
                    S = S2
                P.add('act', lambda e, o=o, po=po, tt=tt: e.copy(out=o.t[:, (tt % 4) * 128:(tt % 4 + 1) * 128], in_=po.t[:, :128]),
                      reads=[po.r], writes=[o.r])
                if tt % 4 == 3:
                    tb = tt // 4
                    gt = gater.next()
                    P.dma('sp', gt.t[:], self.ggT[h * 128:(h + 1) * 128, tb * 512:(tb + 1) * 512], writes=[gt.r])
                    self.headnorm(o, 512, c['hcols'].t[:, l * 8 + 2:l * 8 + 3], (gt.t[:], gt.r),
                                  self.mixT[(12 + h) * 128:(13 + h) * 128, tb * 512:(tb + 1) * 512], tmp)
                yield

    def stage_gdn(self, l):
        nc, P, T, c = self.nc, self.P, self.T, self.c
        NT = T // 128
        cw = self.sb([128, 4 * 1536], F32, 'cw')
        P.dma('sp', cw.t[:], self.convw_in[l], writes=[cw.r])
        sm = self.sb([128, 16], F32, 'sm')
        P.dma('sp', sm.t[:], self.small_in[l], writes=[sm.r])
        negA = self.sb([128, 4], F32, 'negA')
        P.add('act', lambda e: e.activation(out=negA.t[:], in_=sm.t[:, 0:4], func=AF.Exp), reads=[sm.r], writes=[negA.r])
        P.add('dve', lambda e: e.tensor_scalar(out=negA.t[:], in0=negA.t[:], scalar1=-1.0, scalar2=None, op0=ALU.mult), reads=[negA.r], writes=[negA.r])
        Xr = [self.ring(1, [128, 1536], F32, 'gx%d' % j) for j in range(4)]
        cvr = self.ring(2, [128, 1536], F32, 'gcv')
        tpr = self.ring(1, [128, 1536], F32, 'gtp')
        sqr = self.ring(1, [128, 1024], F32, 'gsq')
        bar = self.ring(2, [128, 8], F32, 'gba')
        smr = self.ring(2, [128, 32], F32, 'gsm')
        S = [self.sb([128, 128], F32, 'gS%d' % h) for h in range(NH)]
        oacc = [self.ring(1, [128, 512], F32, 'go%d' % h) for h in range(NH)]
        gater = self.ring(2, [128, 512], F32, 'ggate')
        tmp = self.hn_tmp()

        def rg(n, shape=(128, 128), nm='g', dt=F32):
            return [self.ring(n, list(shape), dt, nm + str(h)) for h in range(NH)]
        kTbr = rg(1, (128, 256), 'gkTb', F32)
        PAr = rg(1, nm='gPA')
        vnbr = rg(1, nm='gvnb', dt=F32)
        Sb = [self.sb([128, 128], F32, 'gSb%d' % h) for h in range(NH)]
        kTr = rg(2, (128, 256), 'gkT')
        dgr = rg(1, nm='gdg')
        eGr = rg(2, nm='geG')
        dSr = rg(1, nm='gdS')
        dTr = rg(1, nm='gdT')
        Pr = rg(3, (128, 256), 'gP', F32)
        Tr = rg(3, (128, 256), 'gT', F32)
        QKr = rg(2, nm='gQK', dt=F32)
        qtr = rg(2, nm='gqt', dt=F32)
        vbr = rg(1, nm='gvb', dt=F32)
        kbgr = rg(1, nm='gkbg', dt=F32)
        khr = rg(2, nm='gkh', dt=F32)
        ur = rg(2, nm='gu')
        wTr = rg(2, nm='gwT', dt=F32)
        vnr = rg(1, nm='gvn')
        p2sr = rg(1, nm='gp2s')
        for h in range(NH):
            P.add('pool', lambda e, h=h: e.memset(S[h].t[:], 0.0), writes=[S[h].r])
            P.add('pool', lambda e, h=h: e.memset(Sb[h].t[:], 0.0), writes=[Sb[h].r])
        oh = [None] * NH
        tiles = {}

        def front(tt):
            if True:
                r0 = tt * 128
                X = [Xr[j].next() for j in range(4)]
                for j in range(4):
                    sh = 3 - j
                    if r0 - sh < 0:
                        P.add('pool', lambda e, xb=X[j]: e.memset(xb.t[0:32, :], 0.0), writes=[X[j].r])
                        if sh > 0:
                            P.dma('sp', X[j].t[sh:128, :], self.gqkv[0:128 - sh, :], writes=[X[j].r])
                        else:
                            P.dma('sp', X[j].t[:], self.gqkv[0:128, :], writes=[X[j].r])
                    else:
                        P.dma('sp', X[j].t[:], self.gqkv[r0 - sh:r0 - sh + 128, :], writes=[X[j].r])
                cv = cvr.next()
                tp = tpr.next()
                P.add('dve', lambda e, cv=cv, X=X: e.tensor_tensor(out=cv.t[:], in0=X[3].t[:], in1=cw.t[:, 3 * 1536:4 * 1536], op=ALU.mult),
                      reads=[X[3].r, cw.r], writes=[cv.r])
                for j in range(3):
                    P.add('pool', lambda e, tp=tp, X=X, j=j: e.tensor_tensor(out=tp.t[:], in0=X[j].t[:], in1=cw.t[:, j * 1536:(j + 1) * 1536], op=ALU.mult),
                          reads=[X[j].r, cw.r], writes=[tp.r])
                    P.add('dve', lambda e, cv=cv, tp=tp: e.tensor_tensor(out=cv.t[:], in0=cv.t[:], in1=tp.t[:], op=ALU.add),
                          reads=[cv.r, tp.r], writes=[cv.r])
                P.add('act', lambda e, cv=cv: e.activation(out=cv.t[:], in_=cv.t[:], func=AF.Silu), reads=[cv.r], writes=[cv.r])
                sq = sqr.next()
                s_ = smr.next()
                P.add('pool', lambda e, sq=sq, cv=cv: e.tensor_tensor(out=sq.t[:], in0=cv.t[:, 0:1024], in1=cv.t[:, 0:1024], op=ALU.mult),
                      reads=[cv.r], writes=[sq.r])
                P.add('dve', lambda e, s_=s_, sq=sq: e.tensor_reduce(out=s_.t[:, 0:8], in_=sq.t[:].rearrange("p (a b) -> p a b", a=8), axis=AX.X, op=ALU.add),
                      reads=[sq.r], writes=[s_.r])
                P.add('dve', lambda e, s_=s_: e.tensor_scalar(out=s_.t[:, 0:8], in0=s_.t[:, 0:8], scalar1=EPS, scalar2=None, op0=ALU.add), reads=[s_.r], writes=[s_.r])
                P.add('act', lambda e, s_=s_: e.activation(out=s_.t[:, 0:8], in_=s_.t[:, 0:8], func=AF.Sqrt), reads=[s_.r], writes=[s_.r])
                P.add('dve', lambda e, s_=s_: e.reciprocal(out=s_.t[:, 0:8], in_=s_.t[:, 0:8]), reads=[s_.r], writes=[s_.r])
                P.add('dve', lambda e, s_=s_: e.tensor_scalar(out=s_.t[:, 0:4], in0=s_.t[:, 0:4], scalar1=128 ** -0.5, scalar2=None, op0=ALU.mult), reads=[s_.r], writes=[s_.r])
                for i in range(8):
                    eng = 'dve' if i % 2 == 0 else 'pool'
                    P.add(eng, lambda e, cv=cv, s_=s_, i=i: e.tensor_scalar(out=cv.t[:, i * 128:(i + 1) * 128], in0=cv.t[:, i * 128:(i + 1) * 128],
                                                                            scalar1=s_.t[:, i:i + 1], scalar2=None, op0=ALU.mult),
                          reads=[cv.r, s_.r], writes=[cv.r])
                ba = bar.next()
                P.dma('sp', ba.t[:], self.gba[r0:r0 + 128, :], writes=[ba.r])
                P.add('act', lambda e, s_=s_, ba=ba: e.activation(out=s_.t[:, 8:12], in_=ba.t[:, 0:4], func=AF.Sigmoid), reads=[ba.r], writes=[s_.r])
                P.add('dve', lambda e, ba=ba: e.tensor_tensor(out=ba.t[:, 4:8], in0=ba.t[:, 4:8], in1=sm.t[:, 4:8], op=ALU.add), reads=[ba.r, sm.r], writes=[ba.r])
                P.add('act', lambda e, ba=ba: e.activation(out=ba.t[:, 4:8], in_=ba.t[:, 4:8], func=AF.Exp), reads=[ba.r], writes=[ba.r])
                P.add('act', lambda e, ba=ba: e.activation(out=ba.t[:, 4:8], in_=ba.t[:, 4:8], func=AF.Ln, bias=c['cst'].t[:, 0:1]), reads=[ba.r, c['cst'].r], writes=[ba.r])
                P.add('dve', lambda e, s_=s_, ba=ba: e.tensor_tensor(out=s_.t[:, 12:16], in0=ba.t[:, 4:8], in1=negA.t[:], op=ALU.mult), reads=[ba.r, negA.r], writes=[s_.r])
                pg = self.bank()
                P.add('pe', lambda e, pg=pg, s_=s_: e.matmul(pg.t[:, 0:4], lhsT=c['upi'].t[:], rhs=s_.t[:, 12:16], start=True, stop=True),
                      reads=[s_.r, c['upi'].r], writes=[pg.r])
                P.add('dve', lambda e, pg=pg, s_=s_: e.tensor_copy(out=s_.t[:, 16:20], in_=pg.t[:, 0:4]), reads=[pg.r], writes=[s_.r])
                P.add('act', lambda e, s_=s_: e.activation(out=s_.t[:, 20:24], in_=s_.t[:, 16:20], func=AF.Exp), reads=[s_.r], writes=[s_.r])
                P.add('dve', lambda e, s_=s_: e.tensor_tensor(out=s_.t[:, 24:28], in0=s_.t[:, 20:24], in1=s_.t[:, 8:12], op=ALU.mult), reads=[s_.r], writes=[s_.r])
                P.add('dve', lambda e, s_=s_: e.tensor_scalar(out=s_.t[:, 28:32], in0=s_.t[:, 8:12], scalar1=-1.0, scalar2=None, op0=ALU.mult), reads=[s_.r], writes=[s_.r])
                yield
                st = [dict() for _ in range(NH)]
                for h in range(NH):
                    d = st[h]
                    qh = cv.t[:, h * 128:(h + 1) * 128]
                    kh = cv.t[:, 512 + h * 128:512 + (h + 1) * 128]
                    vh = cv.t[:, 1024 + h * 128:1024 + (h + 1) * 128]
                    gccol = s_.t[:, 16 + h:17 + h]
                    pT = self.bank()
                    P.add('pe', lambda e, pT=pT, qh=qh: e.transpose(out=pT.t[:, 0:128], in_=qh, identity=c['ident'].t[:]), reads=[cv.r, c['ident'].r], writes=[pT.r])
                    P.add('pe', lambda e, pT=pT, kh=kh: e.transpose(out=pT.t[:, 128:256], in_=kh, identity=c['ident'].t[:]), reads=[cv.r, c['ident'].r], writes=[pT.r])
                    kT = kTr[h].next()
                    P.add('act', lambda e, kT=kT, pT=pT: e.copy(out=kT.t[:], in_=pT.t[:, 0:256]), reads=[pT.r], writes=[kT.r])
                    kTb = kTbr[h].next()
                    P.add('act', lambda e, kTb=kTb, pT=pT: e.copy(out=kTb.t[:], in_=pT.t[:, 0:256]), reads=[pT.r], writes=[kTb.r])
                    dg = dgr[h].next()
                    P.add('act', lambda e, dg=dg, gccol=gccol: e.activation(out=dg.t[:], in_=c['ident'].t[:], func=AF.Copy, scale=gccol),
                          reads=[s_.r, c['ident'].r], writes=[dg.r])
                    pG = self.bank()
                    P.add('pe', lambda e, pG=pG, dg=dg: e.matmul(pG.t[:, 0:128], lhsT=c['ones'].t[:, 0:128], rhs=dg.t[:], start=True, stop=True),
                          reads=[dg.r, c['ones'].r], writes=[pG.r])
                    eG = eGr[h].next()
                    P.add('act', lambda e, eG=eG, pG=pG: e.activation(out=eG.t[:], in_=pG.t[:, 0:128], func=AF.Exp), reads=[pG.r], writes=[eG.r])
                    pD = self.bank()
                    P.add('pe', lambda e, pD=pD, dg=dg: e.matmul(pD.t[:, 0:128], lhsT=dg.t[:], rhs=c['ones'].t[:, 0:128], start=True, stop=False),
                          reads=[dg.r, c['ones'].r], writes=[pD.r])
                    P.add('pe', lambda e, pD=pD, dg=dg: e.matmul(pD.t[:, 0:128], lhsT=c['nones'].t[:], rhs=dg.t[:], start=False, stop=True),
                          reads=[dg.r, c['nones'].r], writes=[pD.r])
                    dS = dSr[h].next()
                    P.add('dve', lambda e, dS=dS, pD=pD: e.tensor_tensor(out=dS.t[:], in0=pD.t[:, 0:128], in1=c['pms'].t[:], op=ALU.add),
                          reads=[pD.r, c['pms'].r], writes=[dS.r])
                    P.add('act', lambda e, dS=dS: e.activation(out=dS.t[:], in_=dS.t[:], func=AF.Exp), reads=[dS.r], writes=[dS.r])
                    dT = dTr[h].next()
                    P.add('dve', lambda e, dT=dT, pD=pD: e.tensor_tensor(out=dT.t[:], in0=pD.t[:, 0:128], in1=c['nmu'].t[:], op=ALU.add),
                          reads=[pD.r, c['nmu'].r], writes=[dT.r])
                    P.add('act', lambda e, dT=dT: e.activation(out=dT.t[:], in_=dT.t[:], func=AF.Exp, scale=-1.0), reads=[dT.r], writes=[dT.r])
                    pK = self.bank()
                    P.add('pe', lambda e, pK=pK, kT=kTb: e.matmul(pK.t[:, 0:128], lhsT=kT.t[:, 128:256], rhs=kT.t[:, 128:256], start=True, stop=True),
                          reads=[kTb.r], writes=[pK.r])
                    P.add('pe', lambda e, pK=pK, kT=kTb: e.matmul(pK.t[:, 128:256], lhsT=kT.t[:, 128:256], rhs=kT.t[:, 0:128], start=True, stop=True),
                          reads=[kTb.r], writes=[pK.r])
                    Pm = Pr[h].next()
                    PA = PAr[h].next()
                    P.add('act', lambda e, PA=PA, pK=pK, s_=s_, h=h: e.activation(out=PA.t[:], in_=pK.t[:, 0:128], func=AF.Copy,
                                                                                 scale=s_.t[:, 28 + h:29 + h]), reads=[pK.r, s_.r], writes=[PA.r])
                    P.add('dve', lambda e, PA=PA, dS=dS: e.tensor_tensor(out=PA.t[:], in0=PA.t[:], in1=dS.t[:], op=ALU.mult),
                          reads=[PA.r, dS.r], writes=[PA.r])
                    P.add('act', lambda e, Pm=Pm, PA=PA: e.copy(out=Pm.t[:, 0:128], in_=PA.t[:]), reads=[PA.r], writes=[Pm.r])
                    QK = QKr[h].next()
                    P.add('dve', lambda e, QK=QK, pK=pK, dT=dT: e.tensor_tensor(out=QK.t[:], in0=pK.t[:, 128:256], in1=dT.t[:], op=ALU.mult),
                          reads=[pK.r, dT.r], writes=[QK.r])
                    pB = self.bank()
                    P.add('pe', lambda e, pB=pB, PA=PA: e.transpose(out=pB.t[:, 0:128], in_=PA.t[:], identity=c['ident'].t[:]),
                          reads=[PA.r, c['ident'].r], writes=[pB.r])
                    P.add('act', lambda e, Pm=Pm, pB=pB: e.copy(out=Pm.t[:, 128:256], in_=pB.t[:, 0:128]), reads=[pB.r, Pm.r], writes=[Pm.r])
                    Tm = Tr[h].next()
                    P.add('pool', lambda e, Tm=Tm, Pm=Pm: e.tensor_tensor(out=Tm.t[:, 0:128], in0=Pm.t[:, 0:128], in1=c['ident'].t[:], op=ALU.add),
                          reads=[Pm.r, c['ident'].r], writes=[Tm.r])
                    P.add('pool', lambda e, Tm=Tm, Pm=Pm: e.tensor_tensor(out=Tm.t[:, 128:256], in0=Pm.t[:, 128:256], in1=c['ident'].t[:], op=ALU.add),
                          reads=[Pm.r, c['ident'].r, Tm.r], writes=[Tm.r])
                    qt_ = qtr[h].next()
                    P.add('pool', lambda e, qt_=qt_, kT=kT, eG=eG: e.tensor_tensor(out=qt_.t[:], in0=kT.t[:, 0:128], in1=eG.t[:], op=ALU.mult),
                          reads=[kT.r, eG.r], writes=[qt_.r])
                    vb = vbr[h].next()
                    P.add('act', lambda e, vb=vb, vh=vh, s_=s_, h=h: e.activation(out=vb.t[:], in_=vh, func=AF.Copy, scale=s_.t[:, 8 + h:9 + h]),
                          reads=[cv.r, s_.r], writes=[vb.r])
                    kbg = kbgr[h].next()
                    P.add('act', lambda e, kbg=kbg, kh=kh, s_=s_, h=h: e.activation(out=kbg.t[:], in_=kh, func=AF.Copy, scale=s_.t[:, 24 + h:25 + h]),
                          reads=[cv.r, s_.r], writes=[kbg.r])
                    khat = khr[h].next()
                    for half in range(2):
                        rows = slice(64 * half, 64 * half + 64)
                        P.add('dve', lambda e, khat=khat, kh=kh, dT=dT, rows=rows, half=half: e.tensor_scalar(
                            out=khat.t[rows, :], in0=kh[rows, :], scalar1=dT.t[rows, 64 * half + 63:64 * half + 64], scalar2=None, op0=ALU.mult),
                            reads=[cv.r, dT.r], writes=[khat.r])
                    d.update(Pm=Pm, Tm=Tm, QK=QK, qt=qt_, vb=vb, kbg=kbg, khat=khat, eG=eG)
                    yield
                for it in range(5):
                    for h in range(NH):
                        d = st[h]
                        Pm, Tm = d['Pm'], d['Tm']
                        pp = self.bank()
                        P.add('pe', lambda e, pp=pp, Pm=Pm: e.matmul(pp.t[:, 0:128], lhsT=Pm.t[:, 128:256], rhs=Pm.t[:, 0:128], start=True, stop=True),
                              reads=[Pm.r], writes=[pp.r])
                        P.add('pe', lambda e, pp=pp, Pm=Pm: e.matmul(pp.t[:, 128:256], lhsT=Pm.t[:, 0:128], rhs=Pm.t[:, 128:256], start=True, stop=True),
                              reads=[Pm.r], writes=[pp.r])
                        P2 = Pr[h].next()
                        P.add('act', lambda e, P2=P2, pp=pp: e.copy(out=P2.t[:], in_=pp.t[:, 0:256]), reads=[pp.r], writes=[P2.r])
                        pq = self.bank()
                        P.add('pe', lambda e, pq=pq, Tm=Tm, P2=P2: e.matmul(pq.t[:, 0:128], lhsT=Tm.t[:, 128:256], rhs=P2.t[:, 0:128], start=True, stop=True),
                              reads=[Tm.r, P2.r], writes=[pq.r])
                        P.add('pe', lambda e, pq=pq, Tm=Tm, P2=P2: e.matmul(pq.t[:, 128:256], lhsT=Tm.t[:, 0:128], rhs=P2.t[:, 128:256], start=True, stop=True),
                              reads=[Tm.r, P2.r], writes=[pq.r])
                        T2 = Tr[h].next()
                        P.add('dve', lambda e, T2=T2, pq=pq, Tm=Tm: e.tensor_tensor(out=T2.t[:], in0=pq.t[:, 0:256], in1=Tm.t[:], op=ALU.add),
                              reads=[pq.r, Tm.r], writes=[T2.r])
                        d['Pm'], d['Tm'] = P2, T2
                    yield
                for h in range(NH):
                    d = st[h]
                    Tm = d['Tm']
                    pu = self.bank()
                    P.add('pe', lambda e, pu=pu, Tm=Tm, vb=d['vb']: e.matmul(pu.t[:, 0:128], lhsT=Tm.t[:, 128:256], rhs=vb.t[:], start=True, stop=True),
                          reads=[Tm.r, d['vb'].r], writes=[pu.r])
                    P.add('pe', lambda e, pu=pu, Tm=Tm, kbg=d['kbg']: e.matmul(pu.t[:, 128:256], lhsT=kbg.t[:], rhs=Tm.t[:, 128:256], start=True, stop=True),
                          reads=[Tm.r, d['kbg'].r], writes=[pu.r])
                    u = ur[h].next()
                    wT = wTr[h].next()
                    P.add('act', lambda e, u=u, pu=pu: e.copy(out=u.t[:], in_=pu.t[:, 0:128]), reads=[pu.r], writes=[u.r])
                    P.add('dve', lambda e, wT=wT, pu=pu: e.tensor_copy(out=wT.t[:], in_=pu.t[:, 128:256]), reads=[pu.r], writes=[wT.r])
                    d.update(u=u, wT=wT)
                tiles[tt] = st
                yield

        def scan(tt):
            if True:
                st = tiles.pop(tt)
                if tt % 4 == 0:
                    for h in range(NH):
                        oh[h] = oacc[h].next()
                posb = self.hbank()
                for half in range(2):
                    rows = slice(64 * half, 64 * half + 64)
                    cols = rows
                    for h in range(NH):
                        d = st[h]
                        p1 = self.bank()
                        P.add('pe', lambda e, p1=p1, wT=d['wT'], h=h: e.matmul(p1.t[:, 0:128], lhsT=wT.t[:], rhs=Sb[h].t[:], start=True, stop=True),
                              reads=[d['wT'].r, Sb[h].r], writes=[p1.r])
                        if half == 0:
                            vn = vnr[h].next()
                            d['vn'] = vn
                            d['vnb'] = vnbr[h].next()
                        vn = d['vn']
                        vnb = d['vnb']
                        P.add('dve', lambda e, vn=vn, u=d['u'], p1=p1, rows=rows: e.tensor_tensor(out=vn.t[rows, :], in0=p1.t[rows, 0:128], in1=u.t[rows, :], op=ALU.subtract),
                              reads=[d['u'].r, p1.r], writes=[vn.r])
                        P.add('dve', lambda e, vn=vn, vnb=vnb, rows=rows: e.tensor_scalar(out=vnb.t[rows, :], in0=vn.t[rows, :], scalar1=-1.0, scalar2=None, op0=ALU.mult),
                              reads=[vn.r], writes=[vnb.r])
                        P.add('pe', lambda e, po=posb, h=h, qt_=d['qt'], cols=cols: e.matmul(po.t[:, h * 128 + cols.start:h * 128 + cols.stop], lhsT=Sb[h].t[:], rhs=qt_.t[:, cols], start=True, stop=False),
                              reads=[Sb[h].r, d['qt'].r], writes=[posb.r])
                        P.add('pe', lambda e, po=posb, vn=vnb, QK=d['QK'], rows=rows, cols=cols, h=h: e.matmul(po.t[:, h * 128 + cols.start:h * 128 + cols.stop], lhsT=vn.t[rows, :], rhs=QK.t[rows, cols], start=False, stop=True),
                              reads=[vnb.r, d['QK'].r], writes=[posb.r])
                        p2 = self.bank()
                        P.add('pe', lambda e, p2=p2, khat=d['khat'], vn=vnb, rows=rows: e.matmul(p2.t[:, 0:128], lhsT=khat.t[rows, :], rhs=vn.t[rows, :], start=True, stop=True),
                              reads=[d['khat'].r, vnb.r], writes=[p2.r])
                        p2s = p2sr[h].next()
                        P.add('act', lambda e, p2s=p2s, p2=p2: e.copy(out=p2s.t[:], in_=p2.t[:, 0:128]), reads=[p2.r], writes=[p2s.r])
                        P.add('dve', lambda e, h=h, eG=d['eG'], p2s=p2s, half=half: e.scalar_tensor_tensor(
                            out=S[h].t[:], in0=S[h].t[:], scalar=eG.t[:, 64 * half + 63:64 * half + 64], in1=p2s.t[:], op0=ALU.mult, op1=ALU.add),
                            reads=[S[h].r, d['eG'].r, p2s.r], writes=[S[h].r])
                        P.add('act', lambda e, h=h: e.copy(out=Sb[h].t[:], in_=S[h].t[:]), reads=[S[h].r], writes=[Sb[h].r])
                        yield
                for h in range(NH):
                    o = oh[h]
                    P.add('act', lambda e, o=o, po=posb, tt=tt, h=h: e.copy(out=o.t[:, (tt % 4) * 128:(tt % 4 + 1) * 128], in_=po.t[:, h * 128:(h + 1) * 128]),
                          reads=[posb.r], writes=[o.r])
                    if tt % 4 == 3:
                        tb = tt // 4
                        gt = gater.next()
                        P.dma('sp', gt.t[:], self.gzT[h * 128:(h + 1) * 128, tb * 512:(tb + 1) * 512], writes=[gt.r])
                        self.headnorm(o, 512, c['hcols'].t[:, l * 8:l * 8 + 1], (gt.t[:], gt.r),
                                      self.mixT[(4 + h) * 128:(5 + h) * 128, tb * 512:(tb + 1) * 512], tmp)


                yield

        fa = BankAlloc(self.banks, [0, 1, 2, 3], [])
        sa = BankAlloc(self.banks, [6, 7], [4, 5])
        self.interleave([(front(0), fa)])
        for tt in range(NT):
            gens = [(scan(tt), sa)]
            if tt + 1 < NT:
                gens.insert(0, (front(tt + 1), fa))
            self.interleave(gens)

def host_layout(inp, depth):
    f = np.float32
    L = depth
    g = [None] * (2 * L + 1)
    for l in range(L):
        g[2 * l] = inp['attn_norm'][l]
        g[2 * l + 1] = inp['ffn_norm'][l]
    g[2 * L] = inp['final_norm']
    gains = np.concatenate([np.asarray(v, f).reshape(16, 128).T for v in g], axis=1)
    convw = np.stack([np.broadcast_to(np.asarray(inp['gdn_conv_w'][l], f).reshape(1, 4 * 1536), (128, 4 * 1536)) for l in range(L)])
    small = np.zeros((L, 128, 16), f)
    for l in range(L):
        small[l, :, 0:4] = np.asarray(inp['gdn_a_log'][l], f)[None, :]
        small[l, :, 4:8] = np.asarray(inp['gdn_dt_bias'][l], f)[None, :]
    hcols = np.zeros((128, L * 8), f)
    for l in range(L):
        hcols[:, l * 8 + 0] = inp['gdn_out_norm'][l]
        hcols[:, l * 8 + 1] = inp['diff_out_norm'][l]
        hcols[:, l * 8 + 2] = inp['gla_out_norm'][l]
    lamv = np.zeros((L, 128, 256), f)
    for l in range(L):
        row = np.concatenate([inp['diff_lam_q1'][l], inp['diff_lam_k1'][l], inp['diff_lam_q2'][l], inp['diff_lam_k2'][l]])
        lamv[l] = np.asarray(row, f)[None, :]
    glab = np.zeros((64, L * 4), f)
    for l in range(L):
        glab[:, l * 4:(l + 1) * 4] = np.asarray(inp['gla_gate_b'][l], f).reshape(4, 64).T
    return dict(gains=np.ascontiguousarray(gains), convw=np.ascontiguousarray(convw), small=small, hcols=hcols,
                lamv=lamv, w2=np.ascontiguousarray(np.asarray(inp['gla_gate_w2'], f)[:L]), glab=glab)


_CACHE = {}


def run(inp, T, depth, n_cores, debug=(), stages=None, trace=False):
    import time
    t0 = time.time()
    key = (T, depth, tuple(debug), None if stages is None else tuple(stages))
    if key not in _CACHE:
        _CACHE[key] = Builder(T, depth, debug, stages)
    b = _CACHE[key]
    print("build_s", time.time() - t0, flush=True)
    f = np.float32
    lay = host_layout(inp, depth)
    common = dict(lay)
    for k in ('w_in', 'w_out', 'w_gate', 'w_up', 'w_down'):
        common[k] = np.ascontiguousarray(np.asarray(inp[k], f)[:depth])
    B = inp['x'].shape[0]
    in_maps = []
    for ci in range(n_cores):
        m = dict(common)
        m['x'] = np.ascontiguousarray(np.asarray(inp['x'][ci % B], f))
        in_maps.append(m)
    t0 = time.time()
    res = run_bass_kernel_spmd(b.nc, in_maps, core_ids=list(range(n_cores)), **({'trace': True} if trace else {}))
    print("run_s", time.time() - t0, flush=True)
    return res, b


def kernel(**inputs):
    x = np.asarray(inputs['x'])
    B, T, _ = x.shape
    res, b = run(inputs, T, 4, 8)
    out = np.stack([np.asarray(res.results[i]['out']) for i in range(B)], axis=0)
    return out.astype(np.float32)
```

```python
import math
import numpy as np
import concourse.bass as bass
import concourse.mybir as mybir
from concourse.bass_utils import run_bass_kernel_spmd

F32 = mybir.dt.float32
BF16 = mybir.dt.bfloat16
AF = mybir.ActivationFunctionType
ALU = mybir.AluOpType
AX = mybir.AxisListType

COMPUTE = ('pe', 'act', 'dve', 'pool')
ENGS = ('pe', 'act', 'dve', 'pool', 'sp')
N_DMA_SEMS = 48


class Res:
    __slots__ = ('w', 'r', 'excl')

    def __init__(self):
        self.w = None
        self.r = {}
        self.excl = False


class Op:
    __slots__ = ('eng', 'emit', 'deps', 'dma', 'sem', 'val', 'signal', 'prev_same_sem')

    def __init__(self, eng, emit, dma):
        self.eng = eng
        self.emit = emit
        self.dma = dma
        self.deps = []
        self.sem = None
        self.val = 0
        self.signal = dma
        self.prev_same_sem = None


class Buf:
    __slots__ = ('t', 'r')

    def __init__(self, t):
        self.t = t
        self.r = Res()


class BankAlloc:
    def __init__(self, banks, rot, held):
        self.rot = [banks[i] for i in rot]
        self.held = [banks[i] for i in held]
        self.i = 0
        self.j = 0

    def bank(self):
        b = self.rot[self.i % len(self.rot)]
        self.i += 1
        return b

    def hbank(self):
        b = self.held[self.j % len(self.held)]
        self.j += 1
        return b


class Ring:
    def __init__(self, bufs):
        self.bufs = bufs
        self.i = 0

    def next(self):
        b = self.bufs[self.i % len(self.bufs)]
        self.i += 1
        return b


class Prog:
    def __init__(self, nc):
        self.nc = nc
        self.ops = {e: [] for e in ENGS}
        self.n_dma = 0
        self.dma_last = [None] * N_DMA_SEMS
        self.out_dmas = []
        self.pending_dmas = []

    def add(self, eng, emit, reads=(), writes=(), dma=False, is_out=False):
        op = Op(eng, emit, dma)
        deps = {}
        for r in reads:
            if r.w is not None:
                deps[id(r.w)] = r.w
            if r.excl:
                for k, o in r.r.items():
                    if k != eng:
                        deps[id(o)] = o
        for w in writes:
            if w.w is not None:
                deps[id(w.w)] = w.w
            for o in w.r.values():
                deps[id(o)] = o
        for d in deps.values():
            if eng == 'pe' and d.eng == 'pe' and not d.dma and not dma:
                continue
            d.signal = True
            op.deps.append(d)
        for r in reads:
            if dma:
                r.r[('dma', self.n_dma)] = op
            else:
                r.r[eng] = op
        for w in writes:
            w.w = op
            w.r = {}
        if dma:
            slot = self.n_dma % N_DMA_SEMS
            op.prev_same_sem = self.dma_last[slot]
            self.dma_last[slot] = op
            op.sem = slot
            self.n_dma += 1
            self.pending_dmas.append(op)
            if is_out:
                self.out_dmas.append(op)
        self.ops[eng].append(op)
        return op

    def dma(self, eng, out, in_, reads=(), writes=(), is_out=False, **kw):
        return self.add(eng, lambda e: e.dma_start(out=out, in_=in_, **kw), reads, writes,
                        dma=True, is_out=is_out)

    def barrier(self):
        lasts = []
        for e in COMPUTE:
            for op in reversed(self.ops[e]):
                if op.emit is not None and not op.dma:
                    op.signal = True
                    lasts.append(op)
                    break
        pend = list(self.pending_dmas)
        self.pending_dmas = []
        for e in ENGS:
            b = Op(e, None, False)
            b.deps = [d for d in lasts if d.eng != e] + pend
            self.ops[e].append(b)

    def emit_all(self):
        nc = self.nc
        fin = Op('sp', None, False)
        fin.deps = list(self.out_dmas)
        self.ops['sp'].append(fin)
        esem = {e: nc.alloc_semaphore('es_' + e) for e in COMPUTE}
        dsem = [nc.alloc_semaphore('ds_%d' % i) for i in range(N_DMA_SEMS)]
        for e in ENGS:
            cnt = 0
            for op in self.ops[e]:
                if op.dma or op.emit is None:
                    continue
                if op.signal:
                    cnt += 1
                    op.val = cnt
                    op.sem = esem[e]
        for slot in range(N_DMA_SEMS):
            chain = []
            o = self.dma_last[slot]
            while o is not None:
                chain.append(o)
                o = o.prev_same_sem
            chain.reverse()
            v = 0
            for o in chain:
                v += 16
                o.val = v
                o.sem = dsem[slot]
        stats = {}
        with nc.Block() as block:
            def make(e):
                def body(eng):
                    waited = {}
                    nw = 0
                    for op in self.ops[e]:
                        need = {}
                        deps = op.deps
                        if op.dma and op.prev_same_sem is not None:
                            deps = deps + [op.prev_same_sem]
                        for d in deps:
                            k = id(d.sem)
                            if k not in need or need[k][1] < d.val:
                                need[k] = (d.sem, d.val)
                        for k, (s, v) in need.items():
                            if waited.get(k, 0) >= v:
                                continue
                            eng.wait_ge(s, v)
                            waited[k] = v
                            nw += 1
                        if op.emit is None:
                            continue
                        inst = op.emit(eng)
                        if op.signal:
                            inst.then_inc(op.sem, 16 if op.dma else 1)
                    stats[e] = (len(self.ops[e]), nw)
                return body
            block.tensor(make('pe'))
            block.scalar(make('act'))
            block.vector(make('dve'))
            block.gpsimd(make('pool'))
            block.sync(make('sp'))
        return stats


D = 2048
FF = 5632
INC = 6680
NH = 4
EPS = 1e-6
BIG = 30000.0
OFF = dict(sb_q=0, sb_k=512, sb_v=1024, gd_q=1536, gd_k=2048, gd_v=2560, gd_z=3072, gd_b=3584,
           gd_a=3588, df_q=3592, df_k=4104, df_v=4616, gl_q=5128, gl_k=5384, gl_v=5640,
           gl_g=6152, gl_r=6664)


class Builder:
    def __init__(self, T, depth, debug=(), stages=None):
        self.T = T
        self.depth = depth
        self.debug = set(debug)
        self.stages = stages
        self.uid = 0
        nc = self.nc = bass.Bass("TRN2", target_bir_lowering=False)
        self.P = Prog(nc)
        self.build()

    def sb(self, shape, dt, name='t'):
        self.uid += 1
        return Buf(self.nc.alloc_sbuf_tensor('%s_%d' % (name, self.uid), shape, dt))

    def ring(self, n, shape, dt, name='r'):
        return Ring([self.sb(shape, dt, name) for _ in range(n)])

    def dram(self, name, shape, dt):
        kind = "ExternalOutput" if name in self.debug else "Internal"
        return self.nc.dram_tensor(name, shape, dt, kind=kind).ap()

    def inp(self, name, shape, dt=F32):
        return self.nc.dram_tensor(name, shape, dt, kind="ExternalInput").ap()

    def bank(self):
        return self.cur.bank()

    def hbank(self):
        return self.cur.hbank()

    def interleave(self, gens_allocs):
        live = list(gens_allocs)
        while live:
            nxt = []
            for g, a in live:
                self.cur = a
                try:
                    next(g)
                    nxt.append((g, a))
                except StopIteration:
                    pass
            live = nxt
        self.cur = self.defalloc

    def want(self, s):
        return self.stages is None or s in self.stages

    def build(self):
        nc, P, T, L = self.nc, self.P, self.T, self.depth
        self.x_in = self.inp("x", [T, D])
        self.w_in = self.inp("w_in", [L, D, INC])
        self.w_out = self.inp("w_out", [L, D, D])
        self.w_gate = self.inp("w_gate", [L, D, FF])
        self.w_up = self.inp("w_up", [L, D, FF])
        self.w_down = self.inp("w_down", [L, FF, D])
        self.gains_in = self.inp("gains", [128, (2 * L + 1) * 16])
        self.convw_in = self.inp("convw", [L, 128, 4 * 1536])
        self.small_in = self.inp("small", [L, 128, 16])
        self.hcols_in = self.inp("hcols", [128, L * 8])
        self.lam_in = self.inp("lamv", [L, 128, 256])
        self.w2_in = self.inp("w2", [L, 16, 256])
        self.glab_in = self.inp("glab", [64, L * 4])
        self.out = self.nc.dram_tensor("out", [T, D], F32, kind="ExternalOutput").ap()

        self.xT = self.dram("xT", [D, T], F32)
        self.hT = self.dram("hT", [D, T], BF16)
        self.mixT = self.dram("mixT", [D, T], BF16)
        self.gT = self.dram("gT", [FF, T], BF16)
        self.qsbT = self.dram("qsbT", [512, T], BF16)
        self.ksbT = self.dram("ksbT", [512, T], BF16)
        self.vsb = self.dram("vsb", [T, 512], BF16)
        self.gqkv = self.dram("gqkv", [T, 1536], F32)
        self.gzT = self.dram("gzT", [512, T], F32)
        self.gba = self.dram("gba", [T, 8], F32)
        self.qdfT = self.dram("qdfT", [512, T], BF16)
        self.kdfT = self.dram("kdfT", [512, T], BF16)
        self.vdf = self.dram("vdf", [T, 512], BF16)
        self.qglT = self.dram("qglT", [256, T], F32)
        self.kglT = self.dram("kglT", [256, T], F32)
        self.vgl = self.dram("vgl", [T, 512], F32)
        self.ggT = self.dram("ggT", [512, T], F32)
        self.grT = self.dram("grT", [16, T], F32)

        self.banks = [Buf(nc.alloc_psum_tensor('bank%d' % i, [128, 512], F32)) for i in range(8)]
        self.defalloc = BankAlloc(self.banks, [0, 1, 2, 3], [4, 5, 6, 7])
        self.cur = self.defalloc
        for b_ in self.banks:
            b_.r.excl = True

        self.consts()
        P.barrier()
        if self.want('pre'):
            with nc.reset_on_exit():
                self.stage_transpose_in()
            P.barrier()
        for l in range(L):
            self.layer(l)
        if self.want('post'):
            with nc.reset_on_exit():
                self.stage_final()
        self.stats = P.emit_all()

    def consts(self):
        nc, P, L = self.nc, self.P, self.depth
        c = self.c = {}

        def mk(name, shape, dt):
            c[name] = self.sb(shape, dt, name)
            return c[name]
        ones = mk('ones', [128, 512], F32)
        P.add('pool', lambda e: e.memset(ones.t[:], 1.0), writes=[ones.r])
        onesb = mk('onesb', [128, 128], BF16)
        P.add('pool', lambda e: e.memset(onesb.t[:], 1.0), writes=[onesb.r])
        nonesb = mk('nonesb', [128, 128], BF16)
        P.add('pool', lambda e: e.memset(nonesb.t[:], -1.0), writes=[nonesb.r])
        nones = mk('nones', [128, 128], F32)
        P.add('pool', lambda e: e.memset(nones.t[:], -1.0), writes=[nones.r])
        zer = mk('zer', [128, 512], F32)
        P.add('pool', lambda e: e.memset(zer.t[:], 0.0), writes=[zer.r])
        cst = mk('cst', [128, 4], F32)
        P.add('pool', lambda e: e.memset(cst.t[:, 0:1], 1.0), writes=[cst.r])
        P.add('pool', lambda e: e.memset(cst.t[:, 1:2], EPS), writes=[cst.r])
        P.add('pool', lambda e: e.memset(cst.t[:, 2:3], 0.0), writes=[cst.r])

        def sel(name, src, pattern, op, fill, base, cm, shape=(128, 128), dt=F32):
            b = mk(name, list(shape), dt)
            n = shape[1]
            P.add('pool', lambda e: e.affine_select(out=b.t[:], in_=src.t[:, :n], pattern=pattern,
                                                    compare_op=op, fill=fill, base=base,
                                                    channel_multiplier=cm),
                  reads=[src.r], writes=[b.r])
            return b
        ident = sel('ident', ones, [[-1, 128]], ALU.is_equal, 0.0, 0, 1)
        ustr_f = sel('ustr_f', ones, [[-1, 128]], ALU.is_gt, 0.0, 0, 1)
        nustr = mk('nustr', [128, 128], BF16)
        P.add('dve', lambda e: e.tensor_scalar(out=nustr.t[:], in0=ustr_f.t[:], scalar1=-1.0,
                                               scalar2=None, op0=ALU.mult),
              reads=[ustr_f.r], writes=[nustr.r])
        for j in range(4):
            a = sel('mstr%d' % j, ones, [[1, 512]], ALU.is_gt, 0.0, -128 * j, -1, shape=(128, 512))
            b = mk('mstrb%d' % j, [128, 512], BF16)
            P.add('dve', lambda e, a=a, b=b: e.tensor_copy(out=b.t[:], in_=a.t[:]), reads=[a.r], writes=[b.r])
            a2 = sel('minc%d' % j, ones, [[1, 512]], ALU.is_ge, 0.0, -128 * j, -1, shape=(128, 512))
            b2 = mk('mincb%d' % j, [128, 512], BF16)
            P.add('dve', lambda e, a=a2, b=b2: e.tensor_copy(out=b.t[:], in_=a.t[:]), reads=[a2.r], writes=[b2.r])
        bd = mk('bd', [128, 128], F32)
        P.add('pool', lambda e: e.memset(bd.t[:], 0.0), writes=[bd.r])
        P.add('pool', lambda e: e.memset(bd.t[0:64, 0:64], 1.0), writes=[bd.r])
        P.add('pool', lambda e: e.memset(bd.t[64:128, 64:128], 1.0), writes=[bd.r])
        upi = sel('upi', bd, [[1, 128]], ALU.is_ge, 0.0, 0, -1)
        lows = sel('lows', bd, [[-1, 128]], ALU.is_gt, 0.0, 0, 1)
        c['upi'] = upi
        pms = mk('pms', [128, 128], F32)
        P.add('dve', lambda e: e.tensor_scalar(out=pms.t[:], in0=lows.t[:], scalar1=BIG, scalar2=-BIG,
                                               op0=ALU.mult, op1=ALU.add), reads=[lows.r], writes=[pms.r])
        nmu = mk('nmu', [128, 128], F32)
        P.add('dve', lambda e: e.tensor_scalar(out=nmu.t[:], in0=upi.t[:], scalar1=-BIG, scalar2=BIG,
                                               op0=ALU.mult, op1=ALU.add), reads=[upi.r], writes=[nmu.r])
        cm = mk('cmask', [128, 512], F32)
        P.add('pool', lambda e: e.memset(cm.t[:], 1.0), writes=[cm.r])
        for k in range(8):
            P.add('pool', lambda e, k=k: e.memset(cm.t[:, 64 * k:64 * k + 1], 0.0), writes=[cm.r])
        gains = mk('gains', [128, (2 * L + 1) * 16], F32)
        P.dma('sp', gains.t[:], self.gains_in, writes=[gains.r])
        hcols = mk('hcols', [128, L * 8], F32)
        P.dma('sp', hcols.t[:], self.hcols_in, writes=[hcols.r])
        glab = mk('glab', [64, L * 4], F32)
        P.dma('sp', glab.t[:], self.glab_in, writes=[glab.r])
        nglab = mk('nglab', [64, L * 4], F32)
        P.add('dve', lambda e: e.tensor_scalar(out=nglab.t[:], in0=glab.t[:], scalar1=-1.0, scalar2=None,
                                               op0=ALU.mult), reads=[glab.r], writes=[nglab.r])

    def stage_transpose_in(self):
        nc, P, T, c = self.nc, self.P, self.T, self.c
        xr = self.ring(2, [128, D], F32, 'xin')
        st = self.ring(2, [128, 16, 512], F32, 'xst')
        xTv = self.xT.rearrange("(c p) t -> p c t", p=128)
        for tb in range(T // 512):
            s = st.next()
            for j in range(4):
                xb = xr.next()
                r0 = tb * 512 + j * 128
                P.dma('sp', xb.t[:], self.x_in[r0:r0 + 128, :], writes=[xb.r])
                for cg in range(4):
                    bk = self.bank()
                    for q in range(4):
                        cc = cg * 4 + q
                        P.add('pe', lambda e, bk=bk, q=q, xb=xb, cc=cc: e.transpose(
                            out=bk.t[:, q * 128:(q + 1) * 128], in_=xb.t[:, cc * 128:(cc + 1) * 128],
                            identity=c['ident'].t[:]), reads=[xb.r, c['ident'].r], writes=[bk.r])
                    eng = 'dve' if cg % 2 == 0 else 'act'
                    if eng == 'dve':
                        P.add('dve', lambda e, bk=bk, s=s, cg=cg, j=j: e.tensor_copy(
                            out=s.t[:, cg * 4:cg * 4 + 4, j * 128:(j + 1) * 128],
                            in_=bk.t[:, :].rearrange("p (q f) -> p q f", q=4)), reads=[bk.r], writes=[s.r])
                    else:
                        P.add('act', lambda e, bk=bk, s=s, cg=cg, j=j: e.copy(
                            out=s.t[:, cg * 4:cg * 4 + 4, j * 128:(j + 1) * 128],
                            in_=bk.t[:, :].rearrange("p (q f) -> p q f", q=4)), reads=[bk.r], writes=[s.r])
            P.dma('sp', xTv[:, :, tb * 512:(tb + 1) * 512], s.t[:], reads=[s.r])

    def norm_block(self, xt, sq, h_out_fn, gi, rs):
        P, c = self.P, self.c
        P.add('act', lambda e: e.activation(out=sq.t[:], in_=xt.t[:], func=AF.Square), reads=[xt.r], writes=[sq.r])
        bk = self.bank()
        for cc in range(16):
            P.add('pe', lambda e, cc=cc: e.matmul(bk.t[:], lhsT=c['onesb'].t[:], rhs=sq.t[:, cc, :],
                                                  start=(cc == 0), stop=(cc == 15)),
                  reads=[sq.r, c['onesb'].r], writes=[bk.r])
        self.rstd_from(bk, rs, 1.0 / D)
        for cc in range(16):
            o, orr = h_out_fn(cc)
            P.add('dve', lambda e, cc=cc, o=o: e.scalar_tensor_tensor(
                out=o, in0=xt.t[:, cc, :], scalar=c['gains'].t[:, gi * 16 + cc:gi * 16 + cc + 1], in1=rs.t[:],
                op0=ALU.mult, op1=ALU.mult), reads=[xt.r, rs.r, c['gains'].r], writes=[orr])

    def rstd_from(self, bk, rs, inv_n, n=512):
        P, c = self.P, self.c
        P.add('dve', lambda e: e.tensor_scalar(out=rs.t[:, :n], in0=bk.t[:, :n], scalar1=inv_n, scalar2=EPS,
                                               op0=ALU.mult, op1=ALU.add), reads=[bk.r], writes=[rs.r])
        P.add('act', lambda e: e.activation(out=rs.t[:, :n], in_=rs.t[:, :n], func=AF.Sqrt), reads=[rs.r], writes=[rs.r])
        P.add('dve', lambda e: e.reciprocal(out=rs.t[:, :n], in_=rs.t[:, :n]), reads=[rs.r], writes=[rs.r])

    def stage_norm(self, gi):
        nc, P, T = self.nc, self.P, self.T
        xr = self.ring(2, [128, 16, 512], F32, 'nx')
        sqr = self.ring(2, [128, 16, 512], BF16, 'nsq')
        hr = self.ring(2, [128, 16, 512], BF16, 'nh')
        rsr = self.ring(2, [128, 512], F32, 'nrs')
        xTv = self.xT.rearrange("(c p) t -> p c t", p=128)
        hTv = self.hT.rearrange("(c p) t -> p c t", p=128)
        for tb in range(T // 512):
            xt = xr.next()
            P.dma('sp', xt.t[:], xTv[:, :, tb * 512:(tb + 1) * 512], writes=[xt.r])
            h = hr.next()
            self.norm_block(xt, sqr.next(), lambda cc, h=h: (h.t[:, cc, :], h.r), gi, rsr.next())
            P.dma('sp', hTv[:, :, tb * 512:(tb + 1) * 512], h.t[:], reads=[h.r])

    def stage_final(self):
        nc, P, T, c = self.nc, self.P, self.T, self.c
        gi = 2 * self.depth
        xr = self.ring(2, [128, 16, 512], F32, 'fx')
        sqr = self.ring(1, [128, 16, 512], BF16, 'fsq')
        hr = self.ring(1, [128, 16, 512], F32, 'fh')
        rsr = self.ring(2, [128, 512], F32, 'frs')
        orr = self.ring(2, [128, D], F32, 'fo')
        xTv = self.xT.rearrange("(c p) t -> p c t", p=128)
        for tb in range(T // 512):
            xt = xr.next()
            P.dma('sp', xt.t[:], xTv[:, :, tb * 512:(tb + 1) * 512], writes=[xt.r])
            h = hr.next()
            self.norm_block(xt, sqr.next(), lambda cc, h=h: (h.t[:, cc, :], h.r), gi, rsr.next())
            for j in range(4):
                ob = orr.next()
                for cg in range(4):
                    bk = self.bank()
                    for q in range(4):
                        cc = cg * 4 + q
                        P.add('pe', lambda e, bk=bk, q=q, h=h, cc=cc, j=j: e.transpose(
                            out=bk.t[:, q * 128:(q + 1) * 128], in_=h.t[:, cc, j * 128:(j + 1) * 128],
                            identity=c['ident'].t[:]), reads=[h.r, c['ident'].r], writes=[bk.r])
                    if cg % 2 == 0:
                        P.add('dve', lambda e, bk=bk, ob=ob, cg=cg: e.tensor_copy(
                            out=ob.t[:, cg * 512:(cg + 1) * 512], in_=bk.t[:]), reads=[bk.r], writes=[ob.r])
                    else:
                        P.add('act', lambda e, bk=bk, ob=ob, cg=cg: e.copy(
                            out=ob.t[:, cg * 512:(cg + 1) * 512], in_=bk.t[:]), reads=[bk.r], writes=[ob.r])
                r0 = tb * 512 + j * 128
                P.dma('sp', self.out[r0:r0 + 128, :], ob.t[:], reads=[ob.r], is_out=True)

    def dense(self, inT, KC, groups):
        nc, P, T = self.nc, self.P, self.T
        TB = min(T, 1024)
        pwmax = 512 if KC <= 16 else 256
        nin = 2 if KC <= 16 else 1
        xin = self.ring(nin, [128, KC, TB], BF16, 'din')
        nw = max(len(g['ws']) for g in groups)
        KG = 4 if KC % 4 == 0 else 1
        wr = [Ring([[self.sb([128, KG, pwmax], BF16, 'dw') for _ in range(KC // KG)] for _ in range(2)])
              for _ in range(nw)]
        inTv = inT.rearrange("(c p) t -> p c t", p=128)
        for tb in range(T // TB):
            xb = xin.next()
            for k0 in range(0, KC, 8):
                k1 = min(KC, k0 + 8)
                P.dma('sp', xb.t[:, k0:k1, :], inTv[:, k0:k1, tb * TB:(tb + 1) * TB], writes=[xb.r])
            for g in groups:
                ncols = g['ncols']
                for c0 in range(0, ncols, pwmax):
                    pw = min(pwmax, ncols - c0)
                    wbs = []
                    for wi, (W, col0) in enumerate(g['ws']):
                        wb = wr[wi].next()
                        Wv = W[:, col0 + c0:col0 + c0 + pw].rearrange("(c p) m -> p c m", p=128)
                        for kg in range(KC // KG):
                            P.dma('pool', wb[kg].t[:, :, :pw], Wv[:, kg * KG:(kg + 1) * KG, :], writes=[wb[kg].r])
                        wbs.append(wb)
                    if g['mode'] == 'F':
                        for m0 in range(0, pw, 128):
                            mw = min(128, pw - m0)
                            for n0 in range(0, TB, 512):
                                bks = []
                                for wb in wbs:
                                    bk = self.bank()
                                    for kc in range(KC):
                                        wk = wb[kc // KG]
                                        P.add('pe', lambda e, bk=bk, wk=wk, kc=kc, m0=m0, mw=mw, n0=n0, xb=xb: e.matmul(
                                            bk.t[:mw, :], lhsT=wk.t[:, kc % KG, m0:m0 + mw], rhs=xb.t[:, kc, n0:n0 + 512],
                                            start=(kc == 0), stop=(kc == KC - 1)), reads=[wk.r, xb.r], writes=[bk.r])
                                    bks.append(bk)
                                g['epi'](bks, c0 + m0, mw, tb * TB + n0)
                    else:
                        wb = wbs[0]
                        for t0 in range(0, TB, 128):
                            bk = self.bank()
                            for kc in range(KC):
                                wk = wb[kc // KG]
                                P.add('pe', lambda e, bk=bk, wk=wk, kc=kc, pw=pw, t0=t0, xb=xb: e.matmul(
                                    bk.t[:, :pw], lhsT=xb.t[:, kc, t0:t0 + 128], rhs=wk.t[:, kc % KG, :pw],
                                    start=(kc == 0), stop=(kc == KC - 1)), reads=[wk.r, xb.r], writes=[bk.r])
                            g['epi']([bk], c0, pw, tb * TB + t0)

    def epi_F_store(self, dst, dt, func=AF.Copy, scale=1.0):
        P = self.P
        st = self.strings[dt]

        def epi(bks, cofs, mw, t0):
            s = st.next()
            P.add('act', lambda e: e.activation(out=s.t[:mw, :], in_=bks[0].t[:mw, :], func=func, scale=scale),
                  reads=[bks[0].r], writes=[s.r])
            P.dma('sp', dst[cofs:cofs + mw, t0:t0 + 512], s.t[:mw, :], reads=[s.r])
        return epi

    def epi_T_store(self, dst, dt):
        P = self.P
        st = self.strings[dt]

        def epi(bks, cofs, pw, t0):
            s = st.next()
            P.add('dve', lambda e: e.tensor_copy(out=s.t[:, :pw], in_=bks[0].t[:, :pw]), reads=[bks[0].r], writes=[s.r])
            P.dma('sp', dst[t0:t0 + 128, cofs:cofs + pw], s.t[:, :pw], reads=[s.r])
        return epi

    def epi_resid(self):
        P = self.P
        xr = self.ring(3, [128, 512], F32, 'erx')

        def epi(bks, cofs, mw, t0):
            x = xr.next()
            P.dma('sp', x.t[:], self.xT[cofs:cofs + 128, t0:t0 + 512], writes=[x.r])
            P.add('dve', lambda e: e.tensor_tensor(out=x.t[:], in0=bks[0].t[:], in1=x.t[:], op=ALU.add),
                  reads=[bks[0].r, x.r], writes=[x.r])
            P.dma('sp', self.xT[cofs:cofs + 128, t0:t0 + 512], x.t[:], reads=[x.r])
        return epi

    def epi_swiglu(self):
        P = self.P
        sr = self.ring(2, [128, 512], F32, 'esw')
        gr = self.ring(3, [128, 512], BF16, 'esg')

        def epi(bks, cofs, mw, t0):
            s = sr.next()
            g = gr.next()
            P.add('act', lambda e: e.activation(out=s.t[:], in_=bks[0].t[:], func=AF.Silu), reads=[bks[0].r], writes=[s.r])
            P.add('dve', lambda e: e.tensor_tensor(out=g.t[:], in0=bks[1].t[:], in1=s.t[:], op=ALU.mult),
                  reads=[bks[1].r, s.r], writes=[g.r])
            P.dma('sp', self.gT[cofs:cofs + 128, t0:t0 + 512], g.t[:], reads=[g.r])
        return epi

    def headnorm(self, o, n, gain_col, gate, dst, tmp, greads=()):
        P, c = self.P, self.c
        sq, rs, y, yb = tmp
        P.add('act', lambda e: e.activation(out=sq.t[:, :n], in_=o.t[:, :n], func=AF.Square), reads=[o.r], writes=[sq.r])
        bk = self.bank()
        P.add('pe', lambda e: e.matmul(bk.t[:, :n], lhsT=c['onesb'].t[:], rhs=sq.t[:, :n], start=True, stop=True),
              reads=[sq.r, c['onesb'].r], writes=[bk.r])
        self.rstd_from(bk, rs, 1.0 / 128, n)
        if gate is None:
            P.add('dve', lambda e: e.scalar_tensor_tensor(out=yb.t[:, :n], in0=o.t[:, :n], scalar=gain_col, in1=rs.t[:, :n],
                                                          op0=ALU.mult, op1=ALU.mult), reads=[o.r, rs.r, c['hcols'].r] + list(greads), writes=[yb.r])
        else:
            P.add('dve', lambda e: e.scalar_tensor_tensor(out=y.t[:, :n], in0=o.t[:, :n], scalar=gain_col, in1=rs.t[:, :n],
                                                          op0=ALU.mult, op1=ALU.mult), reads=[o.r, rs.r, c['hcols'].r], writes=[y.r])
            P.add('pool', lambda e: e.tensor_tensor(out=yb.t[:, :n], in0=y.t[:, :n], in1=gate[0], op=ALU.mult),
                  reads=[y.r, gate[1]], writes=[yb.r])
        P.dma('sp', dst, yb.t[:, :n], reads=[yb.r])

    def hn_tmp(self):
        return (self.sb([128, 512], BF16, 'hsq'), self.sb([128, 512], F32, 'hrs'),
                self.sb([128, 512], F32, 'hy'), self.sb([128, 512], BF16, 'hyb'))

    def layer(self, l):
        nc, P, T = self.nc, self.P, self.T
        lam_init = 0.8 - 0.6 * math.exp(-0.3 * l)
        if self.want('norm1'):
            with nc.reset_on_exit():
                self.stage_norm(2 * l)
            P.barrier()
        if self.want('inproj'):
            with nc.reset_on_exit():
                W = self.w_in[l]
                g = []
                self.strings = {F32: self.ring(3, [128, 512], F32, 'stf'), BF16: self.ring(3, [128, 512], BF16, 'stb')}

                def G(mode, name, ncols, epi):
                    g.append(dict(mode=mode, ws=[(W, OFF[name])], ncols=ncols, epi=epi))
                G('F', 'sb_q', 512, self.epi_F_store(self.qsbT, BF16, scale=128 ** -0.5))
                G('F', 'sb_k', 512, self.epi_F_store(self.ksbT, BF16))
                G('T', 'sb_v', 512, self.epi_T_store(self.vsb, BF16))
                G('T', 'gd_q', 1536, self.epi_T_store(self.gqkv, F32))
                G('F', 'gd_z', 512, self.epi_F_store(self.gzT, F32, func=AF.Silu))
                G('T', 'gd_b', 8, self.epi_T_store(self.gba, F32))
                G('F', 'df_q', 512, self.epi_F_store(self.qdfT, BF16, scale=64 ** -0.5))
                G('F', 'df_k', 512, self.epi_F_store(self.kdfT, BF16))
                G('T', 'df_v', 512, self.epi_T_store(self.vdf, BF16))
                G('F', 'gl_q', 256, self.epi_F_store(self.qglT, F32, scale=64 ** -0.5))
                G('F', 'gl_k', 256, self.epi_F_store(self.kglT, F32))
                G('T', 'gl_v', 512, self.epi_T_store(self.vgl, F32))
                G('F', 'gl_g', 512, self.epi_F_store(self.ggT, F32, func=AF.Silu))
                G('F', 'gl_r', 16, self.epi_F_store(self.grT, F32))
                self.dense(self.hT, 16, g)
            P.barrier()
        if False:
            with nc.reset_on_exit():
                a1 = BankAlloc(self.banks, [0, 1], [4])
                a2 = BankAlloc(self.banks, [2, 3], [5, 6])
                self.interleave([(self.stage_sb(), a1), (self.stage_diff(l, lam_init), a2)])
            P.barrier()
        else:
            if self.want('sb'):
                with nc.reset_on_exit():
                    self.interleave([(self.stage_sb(), self.defalloc)])
                P.barrier()
            if self.want('diff') and self.want('gla'):
                with nc.reset_on_exit():
                    a1 = BankAlloc(self.banks, [0, 1], [4, 5])
                    a2 = BankAlloc(self.banks, [2, 3, 6, 7], [])
                    self.interleave([(self.stage_diff(l, lam_init), a1), (self.stage_gla(l), a2)])
                P.barrier()
            else:
                if self.want('diff'):
                    with nc.reset_on_exit():
                        self.interleave([(self.stage_diff(l, lam_init), BankAlloc(self.banks, [0, 1, 2, 3], [4, 5]))])
                    P.barrier()
                if self.want('gla'):
                    with nc.reset_on_exit():
                        self.interleave([(self.stage_gla(l), self.defalloc)])
                    P.barrier()
        if self.want('gdn'):
            with nc.reset_on_exit():
                self.stage_gdn(l)
            P.barrier()
        if self.want('outproj'):
            with nc.reset_on_exit():
                self.dense(self.mixT, 16, [dict(mode='F', ws=[(self.w_out[l], 0)], ncols=D, epi=self.epi_resid())])
            P.barrier()
        if self.want('ffn'):
            with nc.reset_on_exit():
                self.stage_norm(2 * l + 1)
            P.barrier()
            with nc.reset_on_exit():
                self.dense(self.hT, 16, [dict(mode='F', ws=[(self.w_gate[l], 0), (self.w_up[l], 0)], ncols=FF,
                                              epi=self.epi_swiglu())])
            P.barrier()
            with nc.reset_on_exit():
                self.dense(self.gT, 44, [dict(mode='F', ws=[(self.w_down[l], 0)], ncols=D, epi=self.epi_resid())])
            P.barrier()

    def stage_sb(self):
        nc, P, T, c = self.nc, self.P, self.T, self.c
        NT = T // 128
        kT = self.ring(2, [128, T], BF16, 'sbk')
        vt = self.ring(2, [128, NT, 128], BF16, 'sbv')
        qr = self.ring(2, [128, 512], BF16, 'sbq')
        Er = self.ring(3, [128, 512], F32, 'sbE')
        SPr = self.ring(4, [128, 512], BF16, 'sbSP')
        t1r = self.ring(3, [128, 512], F32, 'sbt1')
        t2r = self.ring(3, [128, 512], F32, 'sbt2')
        attr = self.ring(5, [128, 512], BF16, 'sbatt')
        Lf = self.ring(2, [128, 512], F32, 'sbLf')
        Lb = self.ring(3, [128, 512], BF16, 'sbLb')
        zb = Ring([self.banks[0], self.banks[1], self.banks[2]])
        tb = Ring([self.banks[3], self.banks[5]])
        pob = Ring([self.banks[4], self.banks[6]])
        orr = self.ring(2, [128, 512], BF16, 'sbo')
        vv = self.vsb.rearrange("(n p) c -> p n c", p=128)
        for h in range(NH):
            k = kT.next()
            P.dma('sp', k.t[:], self.ksbT[h * 128:(h + 1) * 128, :], writes=[k.r])
            v = vt.next()
            for n0_ in range(0, NT, 8):
                P.dma('sp', v.t[:, n0_:min(NT, n0_ + 8), :], vv[:, n0_:min(NT, n0_ + 8), h * 128:(h + 1) * 128], writes=[v.r])
            for qt in range(T // 512):
                q = qr.next()
                P.dma('sp', q.t[:], self.qsbT[h * 128:(h + 1) * 128, qt * 512:(qt + 1) * 512], writes=[q.r])
                lf = Lf.next()
                lbs = [Lb.next()]
                P.add('pool', lambda e, lf=lf: e.memset(lf.t[:], 0.0), writes=[lf.r])
                P.add('pool', lambda e, lb=lbs[0]: e.memset(lb.t[:], 0.0), writes=[lbs[0].r])
                po = pob.next()
                kmax = 4 * (qt + 1) - 1
                kbs = list(range(kmax, -1, -1))
                nb = len(kbs)

                def phA(kb, k=k, q=q, qt=qt):
                    jd = kb - 4 * qt
                    pz = zb.next()
                    P.add('pe', lambda e, pz=pz, k=k, kb=kb, q=q: e.matmul(
                        pz.t[:], lhsT=k.t[:, kb * 128:(kb + 1) * 128], rhs=q.t[:], start=True, stop=True),
                        reads=[k.r, q.r], writes=[pz.r])
                    E = Er.next()
                    SP = SPr.next()
                    P.add('act', lambda e, E=E, pz=pz: e.activation(out=E.t[:], in_=pz.t[:], func=AF.Exp), reads=[pz.r], writes=[E.r])
                    P.add('act', lambda e, E=E, SP=SP: e.activation(out=SP.t[:], in_=E.t[:], func=AF.Ln, bias=c['cst'].t[:, 0:1]),
                          reads=[E.r, c['cst'].r], writes=[SP.r])
                    if jd >= 0:
                        m = c['mstrb%d' % jd]
                        P.add('dve', lambda e, SP=SP, m=m: e.tensor_tensor(out=SP.t[:], in0=SP.t[:], in1=m.t[:], op=ALU.mult),
                              reads=[SP.r, m.r], writes=[SP.r])
                    return dict(kb=kb, jd=jd, pz=pz, SP=SP)

                def phB(b, lf=lf, lbs=lbs):
                    pz, SP, jd, kb = b['pz'], b['SP'], b['jd'], b['kb']
                    lb = lbs[0]
                    pt = tb.next()
                    P.add('pe', lambda e, pt=pt, SP=SP: e.matmul(pt.t[:], lhsT=c['nustr'].t[:], rhs=SP.t[:], start=True, stop=False),
                          reads=[SP.r, c['nustr'].r], writes=[pt.r])
                    P.add('pe', lambda e, pt=pt, lb=lb: e.matmul(pt.t[:], lhsT=c['nonesb'].t[:], rhs=lb.t[:], start=False, stop=True),
                          reads=[lb.r, c['nonesb'].r], writes=[pt.r])
                    t1 = t1r.next()
                    P.add('dve', lambda e, t1=t1, pz=pz, SP=SP: e.tensor_tensor(out=t1.t[:], in0=pz.t[:], in1=SP.t[:], op=ALU.subtract),
                          reads=[pz.r, SP.r], writes=[t1.r])
                    t2 = t2r.next()
                    P.add('dve', lambda e, t2=t2, pt=pt, t1=t1: e.tensor_tensor(out=t2.t[:], in0=pt.t[:], in1=t1.t[:], op=ALU.add),
                          reads=[pt.r, t1.r], writes=[t2.r])
                    att = attr.next()
                    P.add('act', lambda e, att=att, t2=t2: e.activation(out=att.t[:], in_=t2.t[:], func=AF.Exp), reads=[t2.r], writes=[att.r])
                    if jd >= 0:
                        m = c['mstrb%d' % jd]
                        P.add('pool', lambda e, att=att, m=m: e.tensor_tensor(out=att.t[:], in0=att.t[:], in1=m.t[:], op=ALU.mult),
                              reads=[att.r, m.r], writes=[att.r])
                    if kb > 0:
                        P.add('pool', lambda e, lf=lf, SP=SP: e.tensor_tensor(out=lf.t[:], in0=lf.t[:], in1=SP.t[:], op=ALU.add),
                              reads=[lf.r, SP.r], writes=[lf.r])
                        lb2 = Lb.next()
                        P.add('act', lambda e, lf=lf, lb2=lb2: e.copy(out=lb2.t[:], in_=lf.t[:]), reads=[lf.r], writes=[lb2.r])
                        lbs[0] = lb2
                    b['att'] = att

                def phC(b, po=po, v=v, kmax=kmax):
                    kb, att = b['kb'], b['att']
                    P.add('pe', lambda e, po=po, v=v, kb=kb, att=att, kmax=kmax: e.matmul(
                        po.t[:], lhsT=v.t[:, kb, :], rhs=att.t[:], start=(kb == kmax), stop=(kb == 0)),
                        reads=[v.r, att.r], writes=[po.r])

                blk = [None] * nb
                blk[0] = phA(kbs[0])
                for i in range(nb):
                    if i + 1 < nb:
                        blk[i + 1] = phA(kbs[i + 1])
                    phB(blk[i])
                    if i >= 2:
                        phC(blk[i - 2])
                    yield
                for i in range(max(0, nb - 2), nb):
                    phC(blk[i])
                o = orr.next()
                P.add('act', lambda e, o=o, po=po: e.copy(out=o.t[:], in_=po.t[:]), reads=[po.r], writes=[o.r])
                P.dma('sp', self.mixT[h * 128:(h + 1) * 128, qt * 512:(qt + 1) * 512], o.t[:], reads=[o.r])

    def stage_diff(self, l, lam_init):
        nc, P, T, c = self.nc, self.P, self.T, self.c
        NT = T // 128
        lv = self.sb([128, 256], F32, 'lv')
        P.dma('sp', lv.t[:], self.lam_in[l], writes=[lv.r])
        pr = self.sb([128, 128], F32, 'lpr')
        dots = self.sb([128, 2], F32, 'ldots')
        nlam = self.sb([128, 1], F32, 'nlam')
        P.add('dve', lambda e: e.tensor_tensor(out=pr.t[:, 0:64], in0=lv.t[:, 0:64], in1=lv.t[:, 64:128], op=ALU.mult), reads=[lv.r], writes=[pr.r])
        P.add('dve', lambda e: e.tensor_tensor(out=pr.t[:, 64:128], in0=lv.t[:, 128:192], in1=lv.t[:, 192:256], op=ALU.mult), reads=[lv.r, pr.r], writes=[pr.r])
        P.add('dve', lambda e: e.tensor_reduce(out=dots.t[:], in_=pr.t[:].rearrange("p (a b) -> p a b", a=2), axis=AX.X, op=ALU.add),
              reads=[pr.r], writes=[dots.r])
        P.add('act', lambda e: e.activation(out=dots.t[:], in_=dots.t[:], func=AF.Exp), reads=[dots.r], writes=[dots.r])
        P.add('dve', lambda e: e.tensor_tensor(out=nlam.t[:], in0=dots.t[:, 1:2], in1=dots.t[:, 0:1], op=ALU.subtract), reads=[dots.r], writes=[nlam.r])
        P.add('dve', lambda e: e.tensor_scalar(out=nlam.t[:], in0=nlam.t[:], scalar1=-lam_init, scalar2=None, op0=ALU.add), reads=[nlam.r], writes=[nlam.r])
        gcol = self.sb([128, 1], F32, 'dgc')
        P.add('dve', lambda e: e.tensor_scalar(out=gcol.t[:], in0=c['hcols'].t[:, l * 8 + 1:l * 8 + 2], scalar1=1.0 - lam_init, scalar2=None,
                                               op0=ALU.mult), reads=[c['hcols'].r], writes=[gcol.r])
        kT = self.ring(2, [128, T], BF16, 'dfk')
        vt = self.ring(2, [128, NT, 128], BF16, 'dfv')
        qr = self.ring(2, [128, 512], BF16, 'dfq')
        attr = self.ring(4, [128, 512], BF16, 'dfatt')
        rr = self.ring(2, [128, 512], F32, 'dfr')
        o0r = self.ring(2, [128, 512], F32, 'dfo0')
        o1r = self.ring(2, [128, 512], F32, 'dfo1')
        tmp = self.hn_tmp()
        vv = self.vdf.rearrange("(n p) c -> p n c", p=128)
        for h in range(NH):
            k = kT.next()
            P.dma('sp', k.t[:], self.kdfT[h * 128:(h + 1) * 128, :], writes=[k.r])
            v = vt.next()
            for n0_ in range(0, NT, 8):
                P.dma('sp', v.t[:, n0_:min(NT, n0_ + 8), :], vv[:, n0_:min(NT, n0_ + 8), h * 128:(h + 1) * 128], writes=[v.r])
            for qt in range(T // 512):
                q = qr.next()
                P.dma('sp', q.t[:], self.qdfT[h * 128:(h + 1) * 128, qt * 512:(qt + 1) * 512], writes=[q.r])
                kmax = 4 * (qt + 1) - 1
                om = [o0r.next(), o1r.next()]
                for m in range(2):
                    pom = self.hbank()
                    psm = self.hbank()
                    def zmm(kb, m=m, k=k, q=q):
                        pz = self.bank()
                        P.add('pe', lambda e, pz=pz, k=k, kb=kb, q=q, m=m: e.matmul(
                            pz.t[:], lhsT=k.t[64 * m:64 * m + 64, kb * 128:(kb + 1) * 128], rhs=q.t[64 * m:64 * m + 64, :],
                            start=True, stop=True), reads=[k.r, q.r], writes=[pz.r])
                        return pz
                    pzn = zmm(0)
                    for kb in range(kmax + 1):
                        jd = kb - 4 * qt
                        pz = pzn
                        att = attr.next()
                        P.add('act', lambda e, att=att, pz=pz: e.activation(out=att.t[:], in_=pz.t[:], func=AF.Exp), reads=[pz.r], writes=[att.r])
                        if kb < kmax:
                            pzn = zmm(kb + 1)
                        if jd >= 0:
                            mk = c['mincb%d' % jd]
                            P.add('dve', lambda e, att=att, mk=mk: e.tensor_tensor(out=att.t[:], in0=att.t[:], in1=mk.t[:], op=ALU.mult),
                                  reads=[att.r, mk.r], writes=[att.r])
                        P.add('pe', lambda e, pb=pom, v=v, kb=kb, att=att, kmax=kmax: e.matmul(
                            pb.t[:], lhsT=v.t[:, kb, :], rhs=att.t[:], start=(kb == 0), stop=(kb == kmax)),
                            reads=[v.r, att.r], writes=[pom.r])
                        P.add('pe', lambda e, pb=psm, att=att, kb=kb, kmax=kmax: e.matmul(
                            pb.t[:], lhsT=c['onesb'].t[:], rhs=att.t[:], start=(kb == 0), stop=(kb == kmax)),
                            reads=[att.r, c['onesb'].r], writes=[psm.r])
                        yield
                    r0 = rr.next()
                    P.add('dve', lambda e, r0=r0, pb=psm: e.reciprocal(out=r0.t[:], in_=pb.t[:]), reads=[psm.r], writes=[r0.r])
                    P.add('dve', lambda e, o=om[m], pb=pom, r0=r0: e.tensor_tensor(out=o.t[:], in0=pb.t[:], in1=r0.t[:], op=ALU.mult),
                          reads=[pom.r, r0.r], writes=[om[m].r])
                o0, o1 = om
                P.add('dve', lambda e, o0=o0, o1=o1: e.scalar_tensor_tensor(out=o0.t[:], in0=o1.t[:], scalar=nlam.t[:, 0:1], in1=o0.t[:],
                                                                             op0=ALU.mult, op1=ALU.add), reads=[o0.r, o1.r, nlam.r], writes=[o0.r])
                self.headnorm(o0, 512, gcol.t[:, 0:1], None,
                              self.mixT[(8 + h) * 128:(9 + h) * 128, qt * 512:(qt + 1) * 512], tmp, greads=[gcol.r])
                yield

    def stage_gla(self, l):
        nc, P, T, c = self.nc, self.P, self.T, self.c
        NT = T // 128
        NC = T // 64
        w2 = self.sb([16, 256], F32, 'w2')
        P.dma('sp', w2.t[:], self.w2_in[l], writes=[w2.r])
        grt = self.sb([16, T], F32, 'grt')
        P.dma('sp', grt.t[:], self.grT, writes=[grt.r])
        qtl = self.ring(1, [64, T], F32, 'glq')
        ktl = self.ring(1, [64, T], F32, 'glk')
        ktok = self.ring(1, [128, NT, 64], F32, 'glkt')
        vtok = self.ring(1, [128, NT, 128], F32, 'glv')
        elr = self.ring(2, [64, NC], F32, 'glel')
        Er = self.ring(2, [64, 512], F32, 'glE')
        cumr = self.ring(2, [64, 512], F32, 'glcum')
        ebr = self.ring(2, [64, 512], F32, 'gleb')
        attr = self.ring(2, [128, 128], F32, 'glatt')
        Sr = self.ring(3, [64, 128], F32, 'glS')
        Uer = self.ring(3, [64, 128], F32, 'glUe')
        orr = self.ring(2, [128, 512], F32, 'glo')
        gater = self.ring(2, [128, 512], F32, 'glg')
        tmp = self.hn_tmp()
        vv = self.vgl.rearrange("(n p) c -> p n c", p=128)
        for h in range(NH):
            qt_ = qtl.next()
            kt_ = ktl.next()
            P.dma('sp', qt_.t[:], self.qglT[h * 64:(h + 1) * 64, :], writes=[qt_.r])
            P.dma('sp', kt_.t[:], self.kglT[h * 64:(h + 1) * 64, :], writes=[kt_.r])
            v = vtok.next()
            for n0_ in range(0, NT, 8):
                P.dma('sp', v.t[:, n0_:min(NT, n0_ + 8), :], vv[:, n0_:min(NT, n0_ + 8), h * 128:(h + 1) * 128], writes=[v.r])
            el = elr.next()
            ktk = ktok.next()
            for tb in range(T // 512):
                sl = slice(tb * 512, (tb + 1) * 512)
                pu = self.bank()
                P.add('pe', lambda e, pu=pu, sl=sl, h=h: e.matmul(pu.t[:64, :], lhsT=w2.t[:, h * 64:(h + 1) * 64], rhs=grt.t[:, sl],
                                                                  start=True, stop=True), reads=[w2.r, grt.r], writes=[pu.r])
                E = Er.next()
                P.add('act', lambda e, E=E, pu=pu, h=h: e.activation(out=E.t[:], in_=pu.t[:64, :], func=AF.Exp, scale=-1.0,
                                                                     bias=c['nglab'].t[:, l * 4 + h:l * 4 + h + 1]),
                      reads=[pu.r, c['nglab'].r], writes=[E.r])
                P.add('act', lambda e, E=E: e.activation(out=E.t[:], in_=E.t[:], func=AF.Ln, bias=c['cst'].t[0:64, 0:1]),
                      reads=[E.r, c['cst'].r], writes=[E.r])
                cum = cumr.next()
                P.add('dve', lambda e, cum=cum, E=E: e.tensor_tensor_scan(out=cum.t[:], data0=c['cmask'].t[0:64, :], data1=E.t[:],
                                                                          initial=0.0, op0=ALU.mult, op1=ALU.add),
                      reads=[E.r, c['cmask'].r], writes=[cum.r])
                eb = ebr.next()
                P.add('act', lambda e, eb=eb, cum=cum: e.activation(out=eb.t[:], in_=cum.t[:], func=AF.Exp, scale=-1.0 / 16), reads=[cum.r], writes=[eb.r])
                P.add('dve', lambda e, eb=eb, el=el, tb=tb: e.tensor_copy(
                    out=el.t[:, tb * 8:(tb + 1) * 8], in_=eb.t[:, :].rearrange("p (n f) -> p n f", f=64)[:, :, 63]),
                    reads=[eb.r], writes=[el.r])
                P.add('dve', lambda e, eb=eb, qt_=qt_, sl=sl: e.tensor_tensor(out=qt_.t[:, sl], in0=qt_.t[:, sl], in1=eb.t[:], op=ALU.mult),
                      reads=[eb.r, qt_.r], writes=[qt_.r])
                enb = ebr.next()
                P.add('act', lambda e, enb=enb, cum=cum: e.activation(out=enb.t[:], in_=cum.t[:], func=AF.Exp, scale=1.0 / 16), reads=[cum.r], writes=[enb.r])
                P.add('dve', lambda e, enb=enb, kt_=kt_, sl=sl: e.tensor_tensor(out=kt_.t[:, sl], in0=kt_.t[:, sl], in1=enb.t[:], op=ALU.mult),
                      reads=[enb.r, kt_.r], writes=[kt_.r])
                pk = self.bank()
                for j in range(4):
                    P.add('pe', lambda e, pk=pk, j=j, kt_=kt_, tb=tb: e.transpose(
                        out=pk.t[:, j * 64:(j + 1) * 64], in_=kt_.t[:, tb * 512 + j * 128: tb * 512 + (j + 1) * 128],
                        identity=c['ident'].t[0:64, 0:64]), reads=[kt_.r, c['ident'].r], writes=[pk.r])
                P.add('act', lambda e, pk=pk, ktk=ktk, tb=tb: e.copy(
                    out=ktk.t[:, tb * 4:(tb + 1) * 4, :], in_=pk.t[:, 0:256].rearrange("p (j d) -> p j d", j=4)),
                    reads=[pk.r], writes=[ktk.r])
                yield
            S = Sr.next()
            P.add('pool', lambda e, S=S: e.memset(S.t[:], 0.0), writes=[S.r])
            o = None
            for tt in range(NT):
                if tt % 4 == 0:
                    o = orr.next()
                tsl = slice(tt * 128, (tt + 1) * 128)
                pa = self.bank()
                P.add('pe', lambda e, pa=pa, kt_=kt_, qt_=qt_, tsl=tsl: e.matmul(pa.t[:, :128], lhsT=kt_.t[:, tsl], rhs=qt_.t[:, tsl],
                                                                                 start=True, stop=True), reads=[kt_.r, qt_.r], writes=[pa.r])
                att = attr.next()
                P.add('dve', lambda e, att=att, pa=pa: e.tensor_tensor(out=att.t[:], in0=pa.t[:, :128], in1=c['upi'].t[:], op=ALU.mult),
                      reads=[pa.r, c['upi'].r], writes=[att.r])
                po = self.bank()
                P.add('pe', lambda e, po=po, v=v, tt=tt, att=att: e.matmul(po.t[:, :128], lhsT=v.t[:, tt, :], rhs=att.t[:], start=True, stop=False),
                      reads=[v.r, att.r], writes=[po.r])
                for half in range(2):
                    n = 2 * tt + half
                    rows = slice(64 * half, 64 * half + 64)
                    cols = slice(64 * half, 64 * half + 64)
                    P.add('pe', lambda e, po=po, S=S, qt_=qt_, n=n, cols=cols, half=half: e.matmul(
                        po.t[:, cols], lhsT=S.t[:], rhs=qt_.t[:, n * 64:(n + 1) * 64], start=False, stop=(half == 1)),
                        reads=[S.r, qt_.r], writes=[po.r])
                    pU = self.bank()
                    P.add('pe', lambda e, pU=pU, ktk=ktk, v=v, tt=tt, rows=rows: e.matmul(
                        pU.t[:64, :128], lhsT=ktk.t[rows, tt, :], rhs=v.t[rows, tt, :], start=True, stop=True),
                        reads=[ktk.r, v.r], writes=[pU.r])
                    Ue = Uer.next()
                    P.add('act', lambda e, Ue=Ue, pU=pU, el=el, n=n: e.activation(out=Ue.t[:], in_=pU.t[:64, :128], func=AF.Copy,
                                                                                  scale=el.t[:, n:n + 1]), reads=[pU.r, el.r], writes=[Ue.r])
                    S2 = Sr.next()
                    P.add('dve', lambda e, S2=S2, S=S, el=el, n=n, Ue=Ue: e.scalar_tensor_tensor(
                        out=S2.t[:], in0=S.t[:], scalar=el.t[:, n:n + 1], in1=Ue.t[:], op0=ALU.mult, op1=ALU.add),
                        reads=[S.r, el.r, Ue.r], writes=[S2.r])
                    S = S2
                P.add('act', lambda e, o=o, po=po, tt=tt: e.copy(out=o.t[:, (tt % 4) * 128:(tt % 4 + 1) * 128], in_=po.t[:, :128]),
                      reads=[po.r], writes=[o.r])
                if tt % 4 == 3:
                    tb = tt // 4
                    gt = gater.next()
                    P.dma('sp', gt.t[:], self.ggT[h * 128:(h + 1) * 128, tb * 512:(tb + 1) * 512], writes=[gt.r])
                    self.headnorm(o, 512, c['hcols'].t[:, l * 8 + 2:l * 8 + 3], (gt.t[:], gt.r),
                                  self.mixT[(12 + h) * 128:(13 + h) * 128, tb * 512:(tb + 1) * 512], tmp)
                yield

    def stage_gdn(self, l):
        nc, P, T, c = self.nc, self.P, self.T, self.c
        NT = T // 128
        cw = self.sb([128, 4 * 1536], F32, 'cw')
        P.dma('sp', cw.t[:], self.convw_in[l], writes=[cw.r])
        sm = self.sb([128, 16], F32, 'sm')
        P.dma('sp', sm.t[:], self.small_in[l], writes=[sm.r])
        negA = self.sb([128, 4], F32, 'negA')
        P.add('act', lambda e: e.activation(out=negA.t[:], in_=sm.t[:, 0:4], func=AF.Exp), reads=[sm.r], writes=[negA.r])
        P.add('dve', lambda e: e.tensor_scalar(out=negA.t[:], in0=negA.t[:], scalar1=-1.0, scalar2=None, op0=ALU.mult), reads=[negA.r], writes=[negA.r])
        Xr = [self.ring(1, [128, 1536], F32, 'gx%d' % j) for j in range(4)]
        cvr = self.ring(2, [128, 1536], F32, 'gcv')
        tpr = self.ring(1, [128, 1536], F32, 'gtp')
        sqr = self.ring(1, [128, 1024], F32, 'gsq')
        bar = self.ring(2, [128, 8], F32, 'gba')
        smr = self.ring(2, [128, 32], F32, 'gsm')
        S = [self.sb([128, 128], F32, 'gS%d' % h) for h in range(NH)]
        oacc = [self.ring(1, [128, 512], F32, 'go%d' % h) for h in range(NH)]
        gater = self.ring(2, [128, 512], F32, 'ggate')
        tmp = self.hn_tmp()

        def rg(n, shape=(128, 128), nm='g', dt=F32):
            return [self.ring(n, list(shape), dt, nm + str(h)) for h in range(NH)]
        kTbr = rg(1, (128, 256), 'gkTb', F32)
        PAr = rg(1, nm='gPA')
        vnbr = rg(1, nm='gvnb', dt=F32)
        Sb = [self.sb([128, 128], F32, 'gSb%d' % h) for h in range(NH)]
        kTr = rg(2, (128, 256), 'gkT')
        dgr = rg(1, nm='gdg')
        eGr = rg(2, nm='geG')
        dSr = rg(1, nm='gdS')
        dTr = rg(1, nm='gdT')
        Pr = rg(3, (128, 256), 'gP', F32)
        Tr = rg(3, (128, 256), 'gT', F32)
        QKr = rg(2, nm='gQK', dt=F32)
        qtr = rg(2, nm='gqt', dt=F32)
        vbr = rg(1, nm='gvb', dt=F32)
        kbgr = rg(1, nm='gkbg', dt=F32)
        khr = rg(2, nm='gkh', dt=F32)
        ur = rg(2, nm='gu')
        wTr = rg(2, nm='gwT', dt=F32)
        vnr = rg(1, nm='gvn')
        p2sr = rg(1, nm='gp2s')
        for h in range(NH):
            P.add('pool', lambda e, h=h: e.memset(S[h].t[:], 0.0), writes=[S[h].r])
            P.add('pool', lambda e, h=h: e.memset(Sb[h].t[:], 0.0), writes=[Sb[h].r])
        oh = [None] * NH
        tiles = {}

        def front(tt):
            if True:
                r0 = tt * 128
                X = [Xr[j].next() for j in range(4)]
                for j in range(4):
                    sh = 3 - j
                    if r0 - sh < 0:
                        P.add('pool', lambda e, xb=X[j]: e.memset(xb.t[0:32, :], 0.0), writes=[X[j].r])
                        if sh > 0:
                            P.dma('sp', X[j].t[sh:128, :], self.gqkv[0:128 - sh, :], writes=[X[j].r])
                        else:
                            P.dma('sp', X[j].t[:], self.gqkv[0:128, :], writes=[X[j].r])
                    else:
                        P.dma('sp', X[j].t[:], self.gqkv[r0 - sh:r0 - sh + 128, :], writes=[X[j].r])
                cv = cvr.next()
                tp = tpr.next()
                P.add('dve', lambda e, cv=cv, X=X: e.tensor_tensor(out=cv.t[:], in0=X[3].t[:], in1=cw.t[:, 3 * 1536:4 * 1536], op=ALU.mult),
                      reads=[X[3].r, cw.r], writes=[cv.r])
                for j in range(3):
                    P.add('pool', lambda e, tp=tp, X=X, j=j: e.tensor_tensor(out=tp.t[:], in0=X[j].t[:], in1=cw.t[:, j * 1536:(j + 1) * 1536], op=ALU.mult),
                          reads=[X[j].r, cw.r], writes=[tp.r])
                    P.add('dve', lambda e, cv=cv, tp=tp: e.tensor_tensor(out=cv.t[:], in0=cv.t[:], in1=tp.t[:], op=ALU.add),
                          reads=[cv.r, tp.r], writes=[cv.r])
                P.add('act', lambda e, cv=cv: e.activation(out=cv.t[:], in_=cv.t[:], func=AF.Silu), reads=[cv.r], writes=[cv.r])
                sq = sqr.next()
                s_ = smr.next()
                P.add('pool', lambda e, sq=sq, cv=cv: e.tensor_tensor(out=sq.t[:], in0=cv.t[:, 0:1024], in1=cv.t[:, 0:1024], op=ALU.mult),
                      reads=[cv.r], writes=[sq.r])
                P.add('dve', lambda e, s_=s_, sq=sq: e.tensor_reduce(out=s_.t[:, 0:8], in_=sq.t[:].rearrange("p (a b) -> p a b", a=8), axis=AX.X, op=ALU.add),
                      reads=[sq.r], writes=[s_.r])
                P.add('dve', lambda e, s_=s_: e.tensor_scalar(out=s_.t[:, 0:8], in0=s_.t[:, 0:8], scalar1=EPS, scalar2=None, op0=ALU.add), reads=[s_.r], writes=[s_.r])
                P.add('act', lambda e, s_=s_: e.activation(out=s_.t[:, 0:8], in_=s_.t[:, 0:8], func=AF.Sqrt), reads=[s_.r], writes=[s_.r])
                P.add('dve', lambda e, s_=s_: e.reciprocal(out=s_.t[:, 0:8], in_=s_.t[:, 0:8]), reads=[s_.r], writes=[s_.r])
                P.add('dve', lambda e, s_=s_: e.tensor_scalar(out=s_.t[:, 0:4], in0=s_.t[:, 0:4], scalar1=128 ** -0.5, scalar2=None, op0=ALU.mult), reads=[s_.r], writes=[s_.r])
                for i in range(8):
                    eng = 'dve' if i % 2 == 0 else 'pool'
                    P.add(eng, lambda e, cv=cv, s_=s_, i=i: e.tensor_scalar(out=cv.t[:, i * 128:(i + 1) * 128], in0=cv.t[:, i * 128:(i + 1) * 128],
                                                                            scalar1=s_.t[:, i:i + 1], scalar2=None, op0=ALU.mult),
                          reads=[cv.r, s_.r], writes=[cv.r])
                ba = bar.next()
                P.dma('sp', ba.t[:], self.gba[r0:r0 + 128, :], writes=[ba.r])
                P.add('act', lambda e, s_=s_, ba=ba: e.activation(out=s_.t[:, 8:12], in_=ba.t[:, 0:4], func=AF.Sigmoid), reads=[ba.r], writes=[s_.r])
                P.add('dve', lambda e, ba=ba: e.tensor_tensor(out=ba.t[:, 4:8], in0=ba.t[:, 4:8], in1=sm.t[:, 4:8], op=ALU.add), reads=[ba.r, sm.r], writes=[ba.r])
                P.add('act', lambda e, ba=ba: e.activation(out=ba.t[:, 4:8], in_=ba.t[:, 4:8], func=AF.Exp), reads=[ba.r], writes=[ba.r])
                P.add('act', lambda e, ba=ba: e.activation(out=ba.t[:, 4:8], in_=ba.t[:, 4:8], func=AF.Ln, bias=c['cst'].t[:, 0:1]), reads=[ba.r, c['cst'].r], writes=[ba.r])
                P.add('dve', lambda e, s_=s_, ba=ba: e.tensor_tensor(out=s_.t[:, 12:16], in0=ba.t[:, 4:8], in1=negA.t[:], op=ALU.mult), reads=[ba.r, negA.r], writes=[s_.r])
                pg = self.bank()
                P.add('pe', lambda e, pg=pg, s_=s_: e.matmul(pg.t[:, 0:4], lhsT=c['upi'].t[:], rhs=s_.t[:, 12:16], start=True, stop=True),
                      reads=[s_.r, c['upi'].r], writes=[pg.r])
                P.add('dve', lambda e, pg=pg, s_=s_: e.tensor_copy(out=s_.t[:, 16:20], in_=pg.t[:, 0:4]), reads=[pg.r], writes=[s_.r])
                P.add('act', lambda e, s_=s_: e.activation(out=s_.t[:, 20:24], in_=s_.t[:, 16:20], func=AF.Exp), reads=[s_.r], writes=[s_.r])
                P.add('dve', lambda e, s_=s_: e.tensor_tensor(out=s_.t[:, 24:28], in0=s_.t[:, 20:24], in1=s_.t[:, 8:12], op=ALU.mult), reads=[s_.r], writes=[s_.r])
                P.add('dve', lambda e, s_=s_: e.tensor_scalar(out=s_.t[:, 28:32], in0=s_.t[:, 8:12], scalar1=-1.0, scalar2=None, op0=ALU.mult), reads=[s_.r], writes=[s_.r])
                yield
                st = [dict() for _ in range(NH)]
                for h in range(NH):
                    d = st[h]
                    qh = cv.t[:, h * 128:(h + 1) * 128]
                    kh = cv.t[:, 512 + h * 128:512 + (h + 1) * 128]
                    vh = cv.t[:, 1024 + h * 128:1024 + (h + 1) * 128]
                    gccol = s_.t[:, 16 + h:17 + h]
                    pT = self.bank()
                    P.add('pe', lambda e, pT=pT, qh=qh: e.transpose(out=pT.t[:, 0:128], in_=qh, identity=c['ident'].t[:]), reads=[cv.r, c['ident'].r], writes=[pT.r])
                    P.add('pe', lambda e, pT=pT, kh=kh: e.transpose(out=pT.t[:, 128:256], in_=kh, identity=c['ident'].t[:]), reads=[cv.r, c['ident'].r], writes=[pT.r])
                    kT = kTr[h].next()
                    P.add('act', lambda e, kT=kT, pT=pT: e.copy(out=kT.t[:], in_=pT.t[:, 0:256]), reads=[pT.r], writes=[kT.r])
                    kTb = kTbr[h].next()
                    P.add('act', lambda e, kTb=kTb, pT=pT: e.copy(out=kTb.t[:], in_=pT.t[:, 0:256]), reads=[pT.r], writes=[kTb.r])
                    dg = dgr[h].next()
                    P.add('act', lambda e, dg=dg, gccol=gccol: e.activation(out=dg.t[:], in_=c['ident'].t[:], func=AF.Copy, scale=gccol),
                          reads=[s_.r, c['ident'].r], writes=[dg.r])
                    pG = self.bank()
                    P.add('pe', lambda e, pG=pG, dg=dg: e.matmul(pG.t[:, 0:128], lhsT=c['ones'].t[:, 0:128], rhs=dg.t[:], start=True, stop=True),
                          reads=[dg.r, c['ones'].r], writes=[pG.r])
                    eG = eGr[h].next()
                    P.add('act', lambda e, eG=eG, pG=pG: e.activation(out=eG.t[:], in_=pG.t[:, 0:128], func=AF.Exp), reads=[pG.r], writes=[eG.r])
                    pD = self.bank()
                    P.add('pe', lambda e, pD=pD, dg=dg: e.matmul(pD.t[:, 0:128], lhsT=dg.t[:], rhs=c['ones'].t[:, 0:128], start=True, stop=False),
                          reads=[dg.r, c['ones'].r], writes=[pD.r])
                    P.add('pe', lambda e, pD=pD, dg=dg: e.matmul(pD.t[:, 0:128], lhsT=c['nones'].t[:], rhs=dg.t[:], start=False, stop=True),
                          reads=[dg.r, c['nones'].r], writes=[pD.r])
                    dS = dSr[h].next()
                    P.add('dve', lambda e, dS=dS, pD=pD: e.tensor_tensor(out=dS.t[:], in0=pD.t[:, 0:128], in1=c['pms'].t[:], op=ALU.add),
                          reads=[pD.r, c['pms'].r], writes=[dS.r])
                    P.add('act', lambda e, dS=dS: e.activation(out=dS.t[:], in_=dS.t[:], func=AF.Exp), reads=[dS.r], writes=[dS.r])
                    dT = dTr[h].next()
                    P.add('dve', lambda e, dT=dT, pD=pD: e.tensor_tensor(out=dT.t[:], in0=pD.t[:, 0:128], in1=c['nmu'].t[:], op=ALU.add),
                          reads=[pD.r, c['nmu'].r], writes=[dT.r])
                    P.add('act', lambda e, dT=dT: e.activation(out=dT.t[:], in_=dT.t[:], func=AF.Exp, scale=-1.0), reads=[dT.r], writes=[dT.r])
                    pK = self.bank()
                    P.add('pe', lambda e, pK=pK, kT=kTb: e.matmul(pK.t[:, 0:128], lhsT=kT.t[:, 128:256], rhs=kT.t[:, 128:256], start=True, stop=True),
                          reads=[kTb.r], writes=[pK.r])
                    P.add('pe', lambda e, pK=pK, kT=kTb: e.matmul(pK.t[:, 128:256], lhsT=kT.t[:, 128:256], rhs=kT.t[:, 0:128], start=True, stop=True),
                          reads=[kTb.r], writes=[pK.r])
                    Pm = Pr[h].next()
                    PA = PAr[h].next()
                    P.add('act', lambda e, PA=PA, pK=pK, s_=s_, h=h: e.activation(out=PA.t[:], in_=pK.t[:, 0:128], func=AF.Copy,
                                                                                 scale=s_.t[:, 28 + h:29 + h]), reads=[pK.r, s_.r], writes=[PA.r])
                    P.add('dve', lambda e, PA=PA, dS=dS: e.tensor_tensor(out=PA.t[:], in0=PA.t[:], in1=dS.t[:], op=ALU.mult),
                          reads=[PA.r, dS.r], writes=[PA.r])
                    P.add('act', lambda e, Pm=Pm, PA=PA: e.copy(out=Pm.t[:, 0:128], in_=PA.t[:]), reads=[PA.r], writes=[Pm.r])
                    QK = QKr[h].next()
                    P.add('dve', lambda e, QK=QK, pK=pK, dT=dT: e.tensor_tensor(out=QK.t[:], in0=pK.t[:, 128:256], in1=dT.t[:], op=ALU.mult),
                          reads=[pK.r, dT.r], writes=[QK.r])
                    pB = self.bank()
                    P.add('pe', lambda e, pB=pB, PA=PA: e.transpose(out=pB.t[:, 0:128], in_=PA.t[:], identity=c['ident'].t[:]),
                          reads=[PA.r, c['ident'].r], writes=[pB.r])
                    P.add('act', lambda e, Pm=Pm, pB=pB: e.copy(out=Pm.t[:, 128:256], in_=pB.t[:, 0:128]), reads=[pB.r, Pm.r], writes=[Pm.r])
                    Tm = Tr[h].next()
                    P.add('pool', lambda e, Tm=Tm, Pm=Pm: e.tensor_tensor(out=Tm.t[:, 0:128], in0=Pm.t[:, 0:128], in1=c['ident'].t[:], op=ALU.add),
                          reads=[Pm.r, c['ident'].r], writes=[Tm.r])
                    P.add('pool', lambda e, Tm=Tm, Pm=Pm: e.tensor_tensor(out=Tm.t[:, 128:256], in0=Pm.t[:, 128:256], in1=c['ident'].t[:], op=ALU.add),
                          reads=[Pm.r, c['ident'].r, Tm.r], writes=[Tm.r])
                    qt_ = qtr[h].next()
                    P.add('pool', lambda e, qt_=qt_, kT=kT, eG=eG: e.tensor_tensor(out=qt_.t[:], in0=kT.t[:, 0:128], in1=eG.t[:], op=ALU.mult),
                          reads=[kT.r, eG.r], writes=[qt_.r])
                    vb = vbr[h].next()
                    P.add('act', lambda e, vb=vb, vh=vh, s_=s_, h=h: e.activation(out=vb.t[:], in_=vh, func=AF.Copy, scale=s_.t[:, 8 + h:9 + h]),
                          reads=[cv.r, s_.r], writes=[vb.r])
                    kbg = kbgr[h].next()
                    P.add('act', lambda e, kbg=kbg, kh=kh, s_=s_, h=h: e.activation(out=kbg.t[:], in_=kh, func=AF.Copy, scale=s_.t[:, 24 + h:25 + h]),
                          reads=[cv.r, s_.r], writes=[kbg.r])
                    khat = khr[h].next()
                    for half in range(2):
                        rows = slice(64 * half, 64 * half + 64)
                        P.add('dve', lambda e, khat=khat, kh=kh, dT=dT, rows=rows, half=half: e.tensor_scalar(
                            out=khat.t[rows, :], in0=kh[rows, :], scalar1=dT.t[rows, 64 * half + 63:64 * half + 64], scalar2=None, op0=ALU.mult),
                            reads=[cv.r, dT.r], writes=[khat.r])
                    d.update(Pm=Pm, Tm=Tm, QK=QK, qt=qt_, vb=vb, kbg=kbg, khat=khat, eG=eG)
                    yield
                for it in range(5):
                    for h in range(NH):
                        d = st[h]
                        Pm, Tm = d['Pm'], d['Tm']
                        pp = self.bank()
                        P.add('pe', lambda e, pp=pp, Pm=Pm: e.matmul(pp.t[:, 0:128], lhsT=Pm.t[:, 128:256], rhs=Pm.t[:, 0:128], start=True, stop=True),
                              reads=[Pm.r], writes=[pp.r])
                        P.add('pe', lambda e, pp=pp, Pm=Pm: e.matmul(pp.t[:, 128:256], lhsT=Pm.t[:, 0:128], rhs=Pm.t[:, 128:256], start=True, stop=True),
                              reads=[Pm.r], writes=[pp.r])
                        P2 = Pr[h].next()
                        P.add('act', lambda e, P2=P2, pp=pp: e.copy(out=P2.t[:], in_=pp.t[:, 0:256]), reads=[pp.r], writes=[P2.r])
                        pq = self.bank()
                        P.add('pe', lambda e, pq=pq, Tm=Tm, P2=P2: e.matmul(pq.t[:, 0:128], lhsT=Tm.t[:, 128:256], rhs=P2.t[:, 0:128], start=True, stop=True),
                              reads=[Tm.r, P2.r], writes=[pq.r])
                        P.add('pe', lambda e, pq=pq, Tm=Tm, P2=P2: e.matmul(pq.t[:, 128:256], lhsT=Tm.t[:, 0:128], rhs=P2.t[:, 128:256], start=True, stop=True),
                              reads=[Tm.r, P2.r], writes=[pq.r])
                        T2 = Tr[h].next()
                        P.add('dve', lambda e, T2=T2, pq=pq, Tm=Tm: e.tensor_tensor(out=T2.t[:], in0=pq.t[:, 0:256], in1=Tm.t[:], op=ALU.add),
                              reads=[pq.r, Tm.r], writes=[T2.r])
                        d['Pm'], d['Tm'] = P2, T2
                    yield
                for h in range(NH):
                    d = st[h]
                    Tm = d['Tm']
                    pu = self.bank()
                    P.add('pe', lambda e, pu=pu, Tm=Tm, vb=d['vb']: e.matmul(pu.t[:, 0:128], lhsT=Tm.t[:, 128:256], rhs=vb.t[:], start=True, stop=True),
                          reads=[Tm.r, d['vb'].r], writes=[pu.r])
                    P.add('pe', lambda e, pu=pu, Tm=Tm, kbg=d['kbg']: e.matmul(pu.t[:, 128:256], lhsT=kbg.t[:], rhs=Tm.t[:, 128:256], start=True, stop=True),
                          reads=[Tm.r, d['kbg'].r], writes=[pu.r])
                    u = ur[h].next()
                    wT = wTr[h].next()
                    P.add('act', lambda e, u=u, pu=pu: e.copy(out=u.t[:], in_=pu.t[:, 0:128]), reads=[pu.r], writes=[u.r])
                    P.add('dve', lambda e, wT=wT, pu=pu: e.tensor_copy(out=wT.t[:], in_=pu.t[:, 128:256]), reads=[pu.r], writes=[wT.r])
                    d.update(u=u, wT=wT)
                tiles[tt] = st
                yield

        def scan(tt):
            if True:
                st = tiles.pop(tt)
                if tt % 4 == 0:
                    for h in range(NH):
                        oh[h] = oacc[h].next()
                posb = self.hbank()
                for half in range(2):
                    rows = slice(64 * half, 64 * half + 64)
                    cols = rows
                    for h in range(NH):
                        d = st[h]
                        p1 = self.bank()
                        P.add('pe', lambda e, p1=p1, wT=d['wT'], h=h: e.matmul(p1.t[:, 0:128], lhsT=wT.t[:], rhs=Sb[h].t[:], start=True, stop=True),
                              reads=[d['wT'].r, Sb[h].r], writes=[p1.r])
                        if half == 0:
                            vn = vnr[h].next()
                            d['vn'] = vn
                            d['vnb'] = vnbr[h].next()
                        vn = d['vn']
                        vnb = d['vnb']
                        P.add('dve', lambda e, vn=vn, u=d['u'], p1=p1, rows=rows: e.tensor_tensor(out=vn.t[rows, :], in0=p1.t[rows, 0:128], in1=u.t[rows, :], op=ALU.subtract),
                              reads=[d['u'].r, p1.r], writes=[vn.r])
                        P.add('dve', lambda e, vn=vn, vnb=vnb, rows=rows: e.tensor_scalar(out=vnb.t[rows, :], in0=vn.t[rows, :], scalar1=-1.0, scalar2=None, op0=ALU.mult),
                              reads=[vn.r], writes=[vnb.r])
                        P.add('pe', lambda e, po=posb, h=h, qt_=d['qt'], cols=cols: e.matmul(po.t[:, h * 128 + cols.start:h * 128 + cols.stop], lhsT=Sb[h].t[:], rhs=qt_.t[:, cols], start=True, stop=False),
                              reads=[Sb[h].r, d['qt'].r], writes=[posb.r])
                        P.add('pe', lambda e, po=posb, vn=vnb, QK=d['QK'], rows=rows, cols=cols, h=h: e.matmul(po.t[:, h * 128 + cols.start:h * 128 + cols.stop], lhsT=vn.t[rows, :], rhs=QK.t[rows, cols], start=False, stop=True),
                              reads=[vnb.r, d['QK'].r], writes=[posb.r])
                        p2 = self.bank()
                        P.add('pe', lambda e, p2=p2, khat=d['khat'], vn=vnb, rows=rows: e.matmul(p2.t[:, 0:128], lhsT=khat.t[rows, :], rhs=vn.t[rows, :], start=True, stop=True),
                              reads=[d['khat'].r, vnb.r], writes=[p2.r])
                        p2s = p2sr[h].next()
                        P.add('act', lambda e, p2s=p2s, p2=p2: e.copy(out=p2s.t[:], in_=p2.t[:, 0:128]), reads=[p2.r], writes=[p2s.r])
                        P.add('dve', lambda e, h=h, eG=d['eG'], p2s=p2s, half=half: e.scalar_tensor_tensor(
                            out=S[h].t[:], in0=S[h].t[:], scalar=eG.t[:, 64 * half + 63:64 * half + 64], in1=p2s.t[:], op0=ALU.mult, op1=ALU.add),
                            reads=[S[h].r, d['eG'].r, p2s.r], writes=[S[h].r])
                        P.add('act', lambda e, h=h: e.copy(out=Sb[h].t[:], in_=S[h].t[:]), reads=[S[h].r], writes=[Sb[h].r])
                        yield
                for h in range(NH):
                    o = oh[h]
                    P.add('act', lambda e, o=o, po=posb, tt=tt, h=h: e.copy(out=o.t[:, (tt % 4) * 128:(tt % 4 + 1) * 128], in_=po.t[:, h * 128:(h + 1) * 128]),
                          reads=[posb.r], writes=[o.r])
                    if tt % 4 == 3:
                        tb = tt // 4
                        gt = gater.next()
                        P.dma('sp', gt.t[:], self.gzT[h * 128:(h + 1) * 128, tb * 512:(tb + 1) * 512], writes=[gt.r])
                        self.headnorm(o, 512, c['hcols'].t[:, l * 8:l * 8 + 1], (gt.t[:], gt.r),
                                      self.mixT[(4 + h) * 128:(5 + h) * 128, tb * 512:(tb + 1) * 512], tmp)


                yield

        fa = BankAlloc(self.banks, [0, 1, 2, 3], [])
        sa = BankAlloc(self.banks, [6, 7], [4, 5])
        self.interleave([(front(0), fa)])
        for tt in range(NT):
            gens = [(scan(tt), sa)]
            if tt + 1 < NT:
                gens.insert(0, (front(tt + 1), fa))
            self.interleave(gens)

def host_layout(inp, depth):
    f = np.float32
    L = depth
    g = [None] * (2 * L + 1)
    for l in range(L):
        g[2 * l] = inp['attn_norm'][l]
        g[2 * l + 1] = inp['ffn_norm'][l]
    g[2 * L] = inp['final_norm']
    gains = np.concatenate([np.asarray(v, f).reshape(16, 128).T for v in g], axis=1)
    convw = np.stack([np.broadcast_to(np.asarray(inp['gdn_conv_w'][l], f).reshape(1, 4 * 1536), (128, 4 * 1536)) for l in range(L)])
    small = np.zeros((L, 128, 16), f)
    for l in range(L):
        small[l, :, 0:4] = np.asarray(inp['gdn_a_log'][l], f)[None, :]
        small[l, :, 4:8] = np.asarray(inp['gdn_dt_bias'][l], f)[None, :]
    hcols = np.zeros((128, L * 8), f)
    for l in range(L):
        hcols[:, l * 8 + 0] = inp['gdn_out_norm'][l]
        hcols[:, l * 8 + 1] = inp['diff_out_norm'][l]
        hcols[:, l * 8 + 2] = inp['gla_out_norm'][l]
    lamv = np.zeros((L, 128, 256), f)
    for l in range(L):
        row = np.concatenate([inp['diff_lam_q1'][l], inp['diff_lam_k1'][l], inp['diff_lam_q2'][l], inp['diff_lam_k2'][l]])
        lamv[l] = np.asarray(row, f)[None, :]
    glab = np.zeros((64, L * 4), f)
    for l in range(L):
        glab[:, l * 4:(l + 1) * 4] = np.asarray(inp['gla_gate_b'][l], f).reshape(4, 64).T
    return dict(gains=np.ascontiguousarray(gains), convw=np.ascontiguousarray(convw), small=small, hcols=hcols,
                lamv=lamv, w2=np.ascontiguousarray(np.asarray(inp['gla_gate_w2'], f)[:L]), glab=glab)


_CACHE = {}


def run(inp, T, depth, n_cores, debug=(), stages=None, trace=False):
    import time
    t0 = time.time()
    key = (T, depth, tuple(debug), None if stages is None else tuple(stages))
    if key not in _CACHE:
        _CACHE[key] = Builder(T, depth, debug, stages)
    b = _CACHE[key]
    print("build_s", time.time() - t0, flush=True)
    f = np.float32
    lay = host_layout(inp, depth)
    common = dict(lay)
    for k in ('w_in', 'w_out', 'w_gate', 'w_up', 'w_down'):
        common[k] = np.ascontiguousarray(np.asarray(inp[k], f)[:depth])
    B = inp['x'].shape[0]
    in_maps = []
    for ci in range(n_cores):
        m = dict(common)
        m['x'] = np.ascontiguousarray(np.asarray(inp['x'][ci % B], f))
        in_maps.append(m)
    t0 = time.time()
    res = run_bass_kernel_spmd(b.nc, in_maps, core_ids=list(range(n_cores)), **({'trace': True} if trace else {}))
    print("run_s", time.time() - t0, flush=True)
    return res, b


def kernel(**inputs):
    x = np.asarray(inputs['x'])
    B, T, _ = x.shape
    res, b = run(inputs, T, 4, 8)
    out = np.stack([np.asarray(res.results[i]['out']) for i in range(B)], axis=0)
    return out.astype(np.float32)
```
